# Optimizing a Trainium2 kernel written in Bass

```python
import math
import jax, jax.numpy as jnp
from jax import lax
import numpy as np


D_MODEL = 1024
BATCH = 2
SEQ = 8192
DEPTH = 4

GRID_W = 64
CTX_LEN = 256
EPS = 1e-6

DN_H = 4
DN_DK = 128
DN_DV = 128
SHORT_CONV = 3
GLA_H = 4
GLA_DK = 64
GLA_DV = 128
GLA_RANK = 16
GLA_TAU = 16.0
CHUNK = 64
REC_SIZES = (DN_H * DN_DK, DN_H * DN_DK, DN_H * DN_DV, DN_H * DN_DV, 2 * DN_H, 2 * DN_H,
             GLA_H * GLA_DK, GLA_H * GLA_DK, GLA_H * GLA_DV, GLA_H * GLA_DV, 2 * GLA_RANK)
REC_IN = sum(REC_SIZES)
REC_MIX = DN_H * DN_DV + GLA_H * GLA_DV

ATT_H = 8
ATT_KVH = 2
ATT_G = ATT_H // ATT_KVH
ATT_HD = 128
ATT_QKV = (ATT_H + 2 * ATT_KVH) * ATT_HD
Q_BLOCK = 128
ROPE_THETA = 10000.0
ROPE_PAIRS = ATT_HD // 4

D_FF = 2816
FFN_CONV = 3

kernel_name = 'hybrid_deltanet_gla_gqa_dit'


def _rms(x):
    xf = x.astype(jnp.float32)
    return (xf * lax.rsqrt(jnp.mean(xf * xf, axis=-1, keepdims=True) + EPS)).astype(x.dtype)


def _l2n(x):
    return x * lax.rsqrt(jnp.sum(x * x, axis=-1, keepdims=True) + EPS)


def _modulate(h, shift, scale):
    return h * (1.0 + scale) + shift


def _ada(cond, w, b):
    m = (jax.nn.silu(cond) @ w + b)[:, None, :]
    return jnp.split(m, 6, axis=-1)


def _dwconv(x, w):
    K = w.shape[0]
    p = K // 2
    T = x.shape[1]
    xp = jnp.pad(x, ((0, 0), (p, p), (0, 0)))
    out = xp[:, 0:T] * w[0]
    for i in range(1, K):
        out = out + xp[:, i:i + T] * w[i]
    return out


def _heads(a, n_heads):
    B, T, _ = a.shape
    return jnp.swapaxes(a.reshape(B, T, n_heads, -1), 1, 2)


def _tokens(a):
    B, H, T, hd = a.shape
    return jnp.swapaxes(a, 1, 2).reshape(B, T, H * hd)


def _identity(a):
    return a


def _reverse(a):
    return jnp.flip(a, axis=2)


def _delta_chunked(q, k, v, g, beta, s0):
    B, H, T, _ = q.shape
    dv = v.shape[-1]
    n = T // CHUNK
    q, k, v = (a.reshape(B, H, n, CHUNK, a.shape[-1]) for a in (q, k, v))
    g, beta = (a.reshape(B, H, n, CHUNK) for a in (g, beta))
    G = jnp.cumsum(g, axis=-1)
    incl = jnp.tril(jnp.ones((CHUNK, CHUNK), bool))
    decay = jnp.exp(jnp.where(incl, G[..., :, None] - G[..., None, :], -jnp.inf))
    kb = k * beta[..., None]
    A = jnp.tril(jnp.einsum('bhnid,bhnjd->bhnij', kb, k) * decay, -1)
    eye = jnp.eye(CHUNK, dtype=q.dtype)
    tinv = lax.linalg.triangular_solve(eye + A, jnp.broadcast_to(eye, A.shape),
                                       left_side=True, lower=True, unit_diagonal=True)
    u = tinv @ (v * beta[..., None])
    w = tinv @ (kb * jnp.exp(G)[..., None])
    intra = jnp.einsum('bhnid,bhnjd->bhnij', q, k) * decay
    q_dec = q * jnp.exp(G)[..., None]
    k_dec = k * jnp.exp(G[..., -1:] - G)[..., None]
    g_last = jnp.exp(G[..., -1])

    def step(S, xs):
        qd, kd, w_c, u_c, a_c, gl = xs
        v_new = u_c - w_c @ S
        o = qd @ S + a_c @ v_new
        return S * gl[..., None, None] + jnp.swapaxes(kd, -1, -2) @ v_new, o

    xs = tuple(jnp.moveaxis(a, 2, 0) for a in (q_dec, k_dec, w, u, intra, g_last))
    S, o = lax.scan(step, s0, xs)
    return jnp.moveaxis(o, 0, 2).reshape(B, H, T, dv), S


def _gla_chunked(q, k, v, log_a, s0):
    B, H, T, _ = q.shape
    dv = v.shape[-1]
    n = T // CHUNK
    q, k, v, log_a = (a.reshape(B, H, n, CHUNK, a.shape[-1]) for a in (q, k, v, log_a))
    b = jnp.cumsum(log_a, axis=3)
    b_mid = b[:, :, :, CHUNK // 2:CHUNK // 2 + 1]
    att = jnp.tril(jnp.einsum('bhnid,bhnjd->bhnij', q * jnp.exp(b - b_mid), k * jnp.exp(b_mid - b)))
    o_intra = att @ v
    q_inter = q * jnp.exp(b)
    k_state = k * jnp.exp(b[..., -1:, :] - b)
    a_last = jnp.exp(b[..., -1, :])

    def step(S, xs):
        qd, kd, vc, al = xs
        o = qd @ S
        return S * al[..., :, None] + jnp.swapaxes(kd, -1, -2) @ vc, o

    xs = tuple(jnp.moveaxis(a, 2, 0) for a in (q_inter, k_state, v, a_last))
    S, o_inter = lax.scan(step, s0, xs)
    o = o_intra + jnp.moveaxis(o_inter, 0, 2)
    return o.reshape(B, H, T, dv), S


def _bidir_scan(scan_fn, shared_c, dir_c, shared_l, dir_l, state_shape):
    out_c = 0.0
    out_l = 0.0
    for d in range(2):
        f = _identity if d == 0 else _reverse
        args_c = [f(a) for a in shared_c] + [f(a[d]) for a in dir_c]
        args_l = [f(a) for a in shared_l] + [f(a[d]) for a in dir_l]
        o_c, s_c = scan_fn(*args_c, jnp.zeros(state_shape, jnp.float32))
        o_l, _ = scan_fn(*args_l, s_c)
        out_c = out_c + f(o_c)
        out_l = out_l + f(o_l)
    return out_c, out_l


def _recurrent_mixer(h_c, h_l, w_in, conv_w, a_log, dt_bias, dn_norm, gla_w2, gla_b2, gla_norm, w_out):
    offs = np.cumsum(REC_SIZES)[:-1].tolist()
    f32 = jnp.float32

    def prep(h):
        B, T, _ = h.shape
        dq, dk, dv, dz, da, db, gq, gk, gv, gr, gg = jnp.split(h @ w_in, offs, axis=-1)
        qkv = jax.nn.silu(_dwconv(jnp.concatenate([dq, dk, dv], axis=-1), conv_w)).astype(f32)
        dq, dk, dv = jnp.split(qkv, [DN_H * DN_DK, 2 * DN_H * DN_DK], axis=-1)
        dq = _l2n(_heads(dq, DN_H)) * DN_DK ** -0.5
        dk = _l2n(_heads(dk, DN_H))
        dv = _heads(dv, DN_H)
        g = -jnp.exp(a_log) * jax.nn.softplus(da.astype(f32).reshape(B, T, 2, DN_H) + dt_bias)
        beta = jax.nn.sigmoid(db.astype(f32).reshape(B, T, 2, DN_H))
        g = jnp.transpose(g, (2, 0, 3, 1))
        beta = jnp.transpose(beta, (2, 0, 3, 1))
        gq = _heads(gq.astype(f32), GLA_H) * GLA_DK ** -0.5
        gk = _heads(gk.astype(f32), GLA_H)
        gv = _heads(gv.astype(f32), GLA_H)
        la = jnp.einsum('btdr,drk->dbtk', gg.astype(f32).reshape(B, T, 2, GLA_RANK), gla_w2) + gla_b2[:, None, None, :]
        la = jax.nn.log_sigmoid(la) / GLA_TAU
        la = jnp.transpose(la.reshape(2, B, T, GLA_H, GLA_DK), (0, 1, 3, 2, 4))
        return (dq, dk, dv), (g, beta), (gq, gk, gv), (la,), _heads(dz, DN_H), _heads(gr, GLA_H)

    dn_sc, dn_dc, gla_sc, gla_dc, z_c, r_c = prep(h_c)
    dn_sl, dn_dl, gla_sl, gla_dl, z_l, r_l = prep(h_l)
    B = h_l.shape[0]
    dn_c, dn_l = _bidir_scan(_delta_chunked, dn_sc, dn_dc, dn_sl, dn_dl, (B, DN_H, DN_DK, DN_DV))
    gla_c, gla_l = _bidir_scan(_gla_chunked, gla_sc, gla_dc, gla_sl, gla_dl, (B, GLA_H, GLA_DK, GLA_DV))

    def merge(dn, gla, z, r, dtype):
        dn = _rms(dn) * dn_norm * jax.nn.silu(z)
        gla = _rms(gla) * gla_norm * jax.nn.silu(r)
        return jnp.concatenate([_tokens(dn), _tokens(gla)], axis=-1).astype(dtype) @ w_out

    return merge(dn_c, gla_c, z_c, r_c, h_c.dtype), merge(dn_l, gla_l, z_l, r_l, h_l.dtype)


def _rope_half(x, ang):
    x1, x2 = jnp.split(x.astype(jnp.float32), 2, axis=-1)
    cos = jnp.cos(ang)[:, None, :]
    sin = jnp.sin(ang)[:, None, :]
    return jnp.concatenate([x1 * cos - x2 * sin, x1 * sin + x2 * cos], axis=-1)


def _rope2d(x, ang_row, ang_col):
    xr, xc = jnp.split(x, 2, axis=-1)
    return jnp.concatenate([_rope_half(xr, ang_row), _rope_half(xc, ang_col)], axis=-1).astype(x.dtype)


def _attend(q, k, v):
    s = jnp.einsum('bqkgd,bskd->bkgqs', q.astype(jnp.float32), k.astype(jnp.float32)) * ATT_HD ** -0.5
    p = jax.nn.softmax(s, axis=-1)
    return jnp.einsum('bkgqs,bskd->bqkgd', p.astype(v.dtype), v)


def _attention_mixer(h_c, h_l, w_qkv, q_norm, k_norm, w_out, ang_row, ang_col, need_ctx):
    def prep(h, rotary):
        B, T, _ = h.shape
        q, k, v = jnp.split(h @ w_qkv, [ATT_H * ATT_HD, (ATT_H + ATT_KVH) * ATT_HD], axis=-1)
        q = _rms(q.reshape(B, T, ATT_H, ATT_HD)) * q_norm
        k = _rms(k.reshape(B, T, ATT_KVH, ATT_HD)) * k_norm
        if rotary:
            q = _rope2d(q, ang_row, ang_col)
            k = _rope2d(k, ang_row, ang_col)
        return q.reshape(B, T, ATT_KVH, ATT_G, ATT_HD), k, v.reshape(B, T, ATT_KVH, ATT_HD)

    qc, kc, vc = prep(h_c, False)
    ql, kl, vl = prep(h_l, True)
    keys = jnp.concatenate([kc, kl], axis=1)
    vals = jnp.concatenate([vc, vl], axis=1)
    B, T, _ = h_l.shape
    nb = T // Q_BLOCK
    qb = jnp.moveaxis(ql.reshape(B, nb, Q_BLOCK, ATT_KVH, ATT_G, ATT_HD), 1, 0)
    ol = lax.map(lambda qblk: _attend(qblk, keys, vals), qb)
    ol = jnp.moveaxis(ol, 0, 1).reshape(B, T, ATT_H * ATT_HD) @ w_out
    if need_ctx:
        oc = _attend(qc, kc, vc).reshape(B, h_c.shape[1], ATT_H * ATT_HD) @ w_out
    else:
        oc = None
    return oc, ol


def _conv_ffn(h, w_up, conv_w, w_down):
    gate, val = jnp.split(h @ w_up, 2, axis=-1)
    gate = _dwconv(gate, conv_w)
    return (jax.nn.silu(gate) * val) @ w_down


def setup_inputs(seed: int = 0) -> dict:
    key = jax.random.key(seed)
    ks = iter(jax.random.split(key, 32))
    f32 = jnp.float32

    def nrm(shape, scale):
        return jax.random.normal(next(ks), shape, f32) * scale

    def gain(shape):
        return 1.0 + nrm(shape, 0.02)

    NE = (DEPTH + 1) // 2
    NO = DEPTH // 2
    dt = jnp.exp(jax.random.uniform(next(ks), (NE, 2, DN_H), f32, math.log(1e-3), math.log(1e-1)))
    return {
        'x': nrm((BATCH, SEQ, D_MODEL), 1.0),
        'c': nrm((BATCH, D_MODEL), 1.0),
        'ctx': nrm((BATCH, CTX_LEN, D_MODEL), 1.0),
        'c_ctx': nrm((D_MODEL,), 1.0),
        'mod_w': nrm((DEPTH, D_MODEL, 6 * D_MODEL), 0.5 * D_MODEL ** -0.5),
        'mod_b': nrm((DEPTH, 6 * D_MODEL), 0.02),
        'rec_w_in': nrm((NE, D_MODEL, REC_IN), D_MODEL ** -0.5),
        'rec_conv': nrm((NE, SHORT_CONV, 2 * DN_H * DN_DK + DN_H * DN_DV), 0.5),
        'dn_a_log': jnp.log(jax.random.uniform(next(ks), (NE, 2, DN_H), f32, 1.0, 16.0)),
        'dn_dt_bias': dt + jnp.log(-jnp.expm1(-dt)),
        'dn_norm': gain((NE, DN_DV)),
        'gla_w2': nrm((NE, 2, GLA_RANK, GLA_H * GLA_DK), GLA_RANK ** -0.5),
        'gla_b2': nrm((NE, 2, GLA_H * GLA_DK), 0.1),
        'gla_norm': gain((NE, GLA_DV)),
        'rec_w_out': nrm((NE, REC_MIX, D_MODEL), REC_MIX ** -0.5),
        'att_w_qkv': nrm((NO, D_MODEL, ATT_QKV), D_MODEL ** -0.5),
        'att_q_norm': gain((NO, ATT_HD)),
        'att_k_norm': gain((NO, ATT_HD)),
        'att_w_out': nrm((NO, ATT_H * ATT_HD, D_MODEL), (ATT_H * ATT_HD) ** -0.5),
        'ffn_w_up': nrm((DEPTH, D_MODEL, 2 * D_FF), D_MODEL ** -0.5),
        'ffn_conv': nrm((DEPTH, FFN_CONV, D_FF), 0.5),
        'ffn_w_down': nrm((DEPTH, D_FF, D_MODEL), D_FF ** -0.5),
        'final_norm': gain((D_MODEL,)),
    }


def reference(x, c, ctx, c_ctx, mod_w, mod_b, rec_w_in, rec_conv, dn_a_log, dn_dt_bias, dn_norm,
              gla_w2, gla_b2, gla_norm, rec_w_out, att_w_qkv, att_q_norm, att_k_norm, att_w_out,
              ffn_w_up, ffn_conv, ffn_w_down, final_norm):
    f32 = jnp.float32
    n_lat = x.shape[1]
    ROWS = n_lat // GRID_W
    row = jnp.repeat(jnp.arange(ROWS, dtype=f32), GRID_W)
    col = jnp.tile(jnp.arange(GRID_W, dtype=f32), ROWS)
    inv_freq = ROPE_THETA ** (-jnp.arange(ROPE_PAIRS, dtype=f32) / ROPE_PAIRS)
    ang_row = row[:, None] * inv_freq
    ang_col = col[:, None] * inv_freq

    for i in range(DEPTH):
        last = i == DEPTH - 1
        ml = _ada(c, mod_w[i], mod_b[i])
        mc = _ada(c_ctx[None], mod_w[i], mod_b[i])
        h_l = _modulate(_rms(x), ml[0], ml[1])
        h_c = _modulate(_rms(ctx), mc[0], mc[1])
        if i % 2 == 0:
            e = i // 2
            mix_c, mix_l = _recurrent_mixer(h_c, h_l, rec_w_in[e], rec_conv[e], dn_a_log[e], dn_dt_bias[e],
                                            dn_norm[e], gla_w2[e], gla_b2[e], gla_norm[e], rec_w_out[e])
        else:
            o = i // 2
            mix_c, mix_l = _attention_mixer(h_c, h_l, att_w_qkv[o], att_q_norm[o], att_k_norm[o], att_w_out[o],
                                            ang_row, ang_col, not last)
        x = x + ml[2] * mix_l
        x = x + ml[5] * _conv_ffn(_modulate(_rms(x), ml[3], ml[4]), ffn_w_up[i], ffn_conv[i], ffn_w_down[i])
        if not last:
            ctx = ctx + mc[2] * mix_c
            ctx = ctx + mc[5] * _conv_ffn(_modulate(_rms(ctx), mc[3], mc[4]), ffn_w_up[i], ffn_conv[i], ffn_w_down[i])

    return _rms(x) * final_norm
```

```python
import numpy as np
from contextlib import ExitStack
import concourse.bass as bass
import concourse.mybir as mybir
from concourse.bass_utils import run_bass_kernel_spmd

F32 = mybir.dt.float32
BF16 = mybir.dt.bfloat16
ALU = mybir.AluOpType
AF = mybir.ActivationFunctionType

NCORES = 8
D = 1024
SEQ = 8192
CTX = 256
LQ = SEQ // 4
CQ = CTX // 4
DFF = 2816
NJ = DFF // 128
EPS = 1e-6


class Res:
    __slots__ = ("w", "r")

    def __init__(self):
        self.w = None
        self.r = {}


class Prog:
    ENG = ("pe", "act", "dve", "pool", "sp")

    def __init__(self, nc, stack, ndma=24):
        self.nc = nc
        self.ops = {e: [] for e in self.ENG}
        self.cnt = {e: 0 for e in self.ENG}
        self.esem = {e: stack.enter_context(nc.semaphore("s_" + e)) for e in self.ENG}
        self.dsem = [stack.enter_context(nc.semaphore("d_%d" % i)) for i in range(ndma)]
        self.dcnt = [0] * ndma
        self.dnext = 0
        self.seen = {e: {} for e in self.ENG}

    def _sem(self, k):
        if isinstance(k, tuple):
            return self.dsem[k[1]], 16
        return self.esem[k], 1

    def _waits(self, eng, reads, writes, extra=()):
        deps = {}

        def add(k, v):
            if deps.get(k, 0) < v:
                deps[k] = v

        for r in reads:
            if r.w is not None and not (r.w[0] == eng and eng == "pe"):
                add(*r.w)
        for w in writes:
            if w.w is not None and w.w[0] != eng:
                add(*w.w)
            for k, v in w.r.items():
                if k != eng:
                    add(k, v)
        for k, v in extra:
            add(k, v)
        waits = []
        seen = self.seen[eng]
        for k, v in deps.items():
            if seen.get(k, 0) >= v:
                continue
            seen[k] = v
            sem, mul = self._sem(k)
            waits.append((sem, v * mul))
        return waits

    def op(self, eng, fn, reads=(), writes=()):
        waits = self._waits(eng, reads, writes)
        self.cnt[eng] += 1
        seq = self.cnt[eng]
        self.ops[eng].append((waits, fn, (self.esem[eng], 1)))
        for r in reads:
            r.r[eng] = seq
        for w in writes:
            w.w = (eng, seq)
            w.r = {}

    def dma(self, q, out, in_, reads=(), writes=()):
        i = self.dnext
        self.dnext = (self.dnext + 1) % len(self.dsem)
        key = ("d", i)
        extra = ((key, self.dcnt[i]),) if self.dcnt[i] else ()
        waits = self._waits(q, reads, writes, extra)
        self.dcnt[i] += 1
        seq = self.dcnt[i]
        self.ops[q].append((waits, lambda e: e.dma_start(out=out, in_=in_), (self.dsem[i], 16)))
        for r in reads:
            r.r[key] = seq
        for w in writes:
            w.w = (key, seq)
            w.r = {}

    def barrier(self):
        for e in self.ENG:
            waits = []
            seen = self.seen[e]
            for k in self.ENG:
                if k != e and k != "sp" and self.cnt[k] > seen.get(k, 0):
                    seen[k] = self.cnt[k]
                    waits.append((self.esem[k], self.cnt[k]))
            for i, c in enumerate(self.dcnt):
                k = ("d", i)
                if c > seen.get(k, 0):
                    seen[k] = c
                    waits.append((self.dsem[i], 16 * c))
            self.ops[e].append((waits, None, None))

    def finish(self):
        waits = [(self.dsem[i], 16 * c) for i, c in enumerate(self.dcnt) if c]
        self.ops["sp"].append((waits, None, None))

    def emit(self):
        with self.nc.Block() as block:
            def mk(e):
                def run(eng):
                    for waits, fn, inc in self.ops[e]:
                        for sem, val in waits:
                            eng.wait_ge(sem, val)
                        if fn is not None:
                            fn(eng).then_inc(*inc)
                return run
            block.tensor(mk("pe"))
            block.scalar(mk("act"))
            block.vector(mk("dve"))
            block.gpsimd(mk("pool"))
            block.sync(mk("sp"))


class KB:
    def __init__(self):
        self.nc = bass.Bass("TRN2", target_bir_lowering=False)
        self.st = ExitStack()
        self.P = Prog(self.nc, self.st)
        self.banks = [self.st.enter_context(self.nc.psum_tensor("bank%d" % i, [128, 512], F32)) for i in range(8)]
        self.rbank = [Res() for _ in range(8)]
        self.bi = 0
        self.n = 0

    def din(self, name, shape, dt=F32):
        return self.nc.dram_tensor(name, list(shape), dt, kind="ExternalInput").ap()

    def dout(self, name, shape, dt=F32):
        return self.nc.dram_tensor(name, list(shape), dt, kind="ExternalOutput").ap()

    def sb(self, shape, dt=F32, name=None):
        self.n += 1
        t = self.st.enter_context(self.nc.sbuf_tensor(name or ("t%d" % self.n), list(shape), dt))
        return t, Res()

    def bank(self):
        i = self.bi
        self.bi = (self.bi + 1) % 8
        return self.banks[i], self.rbank[i]

    def done(self):
        self.P.finish()
        self.P.emit()
        self.st.close()
        return self.nc


def split(a, b, maxn=512):
    n = b - a
    k = (n + maxn - 1) // maxn
    base, rem = divmod(n, k)
    out = []
    s = a
    for i in range(k):
        e = s + base + (1 if i < rem else 0)
        out.append((s, e))
        s = e
    return out


def consts(K):
    P = K.P
    ones_b, r1 = K.sb([128, 128], BF16, "ones_b")
    ones_f, r2 = K.sb([128, 128], F32, "ones_f")
    P.op("pool", lambda e: e.memset(ones_b[:], 1.0), writes=[r1])
    P.op("pool", lambda e: e.memset(ones_f[:], 1.0), writes=[r2])
    return (ones_b, r1), (ones_f, r2)


def rms_modulate(K, x, rx, a, b, ones_b, r_ones, s1, sh, rmod, h, rh, hoff, scr):
    P = K.P
    (sq, rsq), (t, rt), (rstd, rrs) = scr
    n = b - a
    P.op("act", lambda e: e.activation(out=sq[:, :, 0:n], in_=x[:, :, a:b], func=AF.Square), reads=[rx], writes=[rsq])
    bk, rb = K.bank()
    for k in range(8):
        P.op("pe", lambda e, k=k: e.matmul(bk[:, 0:n], lhsT=ones_b[:], rhs=sq[:, k, 0:n], start=(k == 0), stop=(k == 7)),
             reads=[r_ones, rsq], writes=[rb])
    P.op("act", lambda e: e.activation(out=rstd[:, 0:n], in_=bk[:, 0:n], func=AF.Sqrt, scale=1.0 / D, bias=EPS),
         reads=[rb], writes=[rrs])
    P.op("dve", lambda e: e.reciprocal(out=rstd[:, 0:n], in_=rstd[:, 0:n]), reads=[rrs], writes=[rrs])
    P.op("dve", lambda e: e.tensor_tensor(out=t[:, :, 0:n], in0=x[:, :, a:b],
                                          in1=rstd[:, 0:n].unsqueeze(1).to_broadcast([128, 8, n]), op=ALU.mult),
         reads=[rx, rrs], writes=[rt])
    for k in range(8):
        eng = "pool" if k % 2 else "dve"
        P.op(eng, lambda e, k=k: e.tensor_scalar(out=h[:, k, hoff:hoff + n], in0=t[:, k, 0:n], scalar1=s1(k), scalar2=sh(k),
                                                 op0=ALU.mult, op1=ALU.add),
             reads=[rt, rmod], writes=[rh])


def rms_scratch(K):
    return (K.sb([128, 8, 512], BF16, "rs_sq"), K.sb([128, 8, 512], F32, "rs_t"), K.sb([128, 512], F32, "rs_rstd"))


def build_kmod():
    K = KB()
    P = K.P
    NO = 3072
    condT = K.din("condT", [D, 3])
    W = K.din("W", [D, NO])
    bT = K.din("bT", [128, NO // 128])
    out = K.dout("out", [128, NO // 128, 3])
    cs, rcs = K.sb([128, 8, 3], F32)
    sg, rsg = K.sb([128, 8, 3], F32)
    bs, rbs = K.sb([128, NO // 128], F32)
    os_, ros = K.sb([128, NO // 128, 3], F32)
    P.dma("sp", cs[:], condT.rearrange("(k p) r -> p k r", p=128), writes=[rcs])
    P.dma("sp", bs[:], bT, writes=[rbs])
    P.op("act", lambda e: e.activation(out=sg[:], in_=cs[:], func=AF.Silu), reads=[rcs], writes=[rsg])
    wt = [K.sb([128, 8, 512], F32, "wt%d" % i) for i in range(2)]
    for g in range(NO // 512):
        w, rw = wt[g % 2]
        P.dma("sp", w[:], W[:, g * 512:(g + 1) * 512].rearrange("(k p) c -> p k c", p=128), writes=[rw])
        for jj in range(4):
            j = g * 4 + jj
            bk, rb = K.bank()
            for k in range(8):
                P.op("pe", lambda e, k=k, jj=jj, w=w, bk=bk: e.matmul(bk[:, 0:3], lhsT=w[:, k, jj * 128:(jj + 1) * 128], rhs=sg[:, k, :],
                                                                     start=(k == 0), stop=(k == 7)),
                     reads=[rw, rsg], writes=[rb])
            P.op("dve", lambda e, j=j, bk=bk: e.tensor_scalar(out=os_[:, j, :], in0=bk[:, 0:3], scalar1=bs[:, j:j + 1], scalar2=None,
                                                              op0=ALU.add),
                 reads=[rb, rbs], writes=[ros])
    P.dma("sp", out, os_[:], reads=[ros])
    return K.done()


def layout(with_ctx):
    if with_ctx:
        segs = [(1, 0, CQ + 2), (0, CQ + 2, CQ + 2 + LQ + 2)]
    else:
        segs = [(0, 0, LQ + 2)]
    return segs, segs[-1][2]


def build_kc(rec, with_ctx, final):
    K = KB()
    P = K.P
    segs, NT = layout(with_ctx)
    NTI = NT - 2 * len(segs)
    xT = K.din("xT", [D, NT])
    if rec:
        ofT = K.din("ofT", [D, NT])
        obT = K.din("obT", [D, NT])
        zrT = K.din("zrT", [D, NT])
        nrm = K.din("nrm", [128, 8])
    else:
        oT = K.din("oT", [D, NT])
    w_out = K.din("w_out", [D, D])
    mod = K.din("mod", [128, 2, 6, 8])
    w_up = K.din("w_up", [D, 2 * DFF])
    cw = K.din("cw", [128, 3, NJ])
    w_down = K.din("w_down", [DFF, D])
    hm = K.din("hm", [128, 4])
    fnorm = K.din("fnorm", [128, 8])
    yT = K.dout("yT", [D, NTI])

    (ones_b, r_ob), (ones_f, r_of) = consts(K)
    x, rx = K.sb([128, 8, NT], F32, "x")
    mods, rmod = K.sb([128, 2, 6, 8], F32, "mods")
    mod1, rmod1 = K.sb([128, 2, 6, 8], F32, "mod1")
    cws, rcw = K.sb([128, 3, NJ], F32, "cws")
    hms, rhm = K.sb([128, 4], F32, "hms")
    fns, rfn = K.sb([128, 8], F32, "fns")
    P.dma("sp", x[:], xT.rearrange("(k p) n -> p k n", p=128), writes=[rx])
    P.dma("sp", mods[:], mod, writes=[rmod])
    P.dma("sp", cws[:], cw, writes=[rcw])
    P.dma("sp", hms[:], hm, writes=[rhm])
    P.dma("sp", fns[:], fnorm, writes=[rfn])
    P.op("pool", lambda e: e.tensor_scalar(out=mod1[:], in0=mods[:], scalar1=1.0, scalar2=None, op0=ALU.add), reads=[rmod], writes=[rmod1])

    with ExitStack() as ph:
        def sbp(shape, dt, name):
            return ph.enter_context(K.nc.sbuf_tensor(name, list(shape), dt)), Res()
        MT, rMT = sbp([128, 8, NT], BF16, "MT")
        if rec:
            nrs, rnr = sbp([128, 8], F32, "nrs")
            P.dma("sp", nrs[:], nrm, writes=[rnr])
            bufs = [[sbp([128, NT], F32, "mg%d_%d" % (i, q)) for q in range(3)] for i in range(2)]
            sq, rsq = sbp([128, NT], BF16, "mg_sq")
            rs, rrs = sbp([128, NT], F32, "mg_rs")
            for kt in range(8):
                (f, rf), (b_, rb_), (z, rz) = bufs[kt % 2]
                P.dma("sp", f[:], ofT[kt * 128:(kt + 1) * 128, :], writes=[rf])
                P.dma("sp", b_[:], obT[kt * 128:(kt + 1) * 128, :], writes=[rb_])
                P.dma("sp", z[:], zrT[kt * 128:(kt + 1) * 128, :], writes=[rz])
                P.op("pool", lambda e, f=f, b_=b_: e.tensor_tensor(out=f[:], in0=f[:], in1=b_[:], op=ALU.add), reads=[rf, rb_], writes=[rf])
                P.op("act", lambda e, f=f: e.activation(out=sq[:], in_=f[:], func=AF.Square), reads=[rf], writes=[rsq])
                for (a, b) in split(0, NT):
                    bk, rbk = K.bank()
                    P.op("pe", lambda e, a=a, b=b, bk=bk: e.matmul(bk[:, 0:b - a], lhsT=ones_b[:], rhs=sq[:, a:b], start=True, stop=True),
                         reads=[r_ob, rsq], writes=[rbk])
                    P.op("act", lambda e, a=a, b=b, bk=bk: e.activation(out=rs[:, a:b], in_=bk[:, 0:b - a], func=AF.Sqrt, scale=1.0 / 128, bias=EPS),
                         reads=[rbk], writes=[rrs])
                P.op("dve", lambda e: e.reciprocal(out=rs[:], in_=rs[:]), reads=[rrs], writes=[rrs])
                P.op("dve", lambda e, f=f: e.tensor_tensor(out=f[:], in0=f[:], in1=rs[:], op=ALU.mult), reads=[rf, rrs], writes=[rf])
                P.op("dve", lambda e, f=f, z=z, kt=kt: e.scalar_tensor_tensor(out=MT[:, kt, :], in0=f[:], scalar=nrs[:, kt:kt + 1], in1=z[:],
                                                                              op0=ALU.mult, op1=ALU.mult),
                     reads=[rf, rz, rnr], writes=[rMT])
        else:
            P.dma("pool", MT[:], oT.rearrange("(k p) n -> p k n", p=128), writes=[rMT])
        wo = [sbp([128, 8, 128], BF16, "wo%d" % i) for i in range(2)]
        for m in range(8):
            w, rw = wo[m % 2]
            P.dma("pool", w[:], w_out[:, m * 128:(m + 1) * 128].rearrange("(k p) c -> p k c", p=128), writes=[rw])
            for (sid, s0, s1) in segs:
                for (a, b) in split(s0, s1):
                    bk, rbk = K.bank()
                    for k in range(8):
                        P.op("pe", lambda e, k=k, a=a, b=b, w=w, bk=bk: e.matmul(bk[:, 0:b - a], lhsT=w[:, k, :], rhs=MT[:, k, a:b],
                                                                                start=(k == 0), stop=(k == 7)),
                             reads=[rw, rMT], writes=[rbk])
                    P.op("dve", lambda e, a=a, b=b, m=m, sid=sid, bk=bk: e.scalar_tensor_tensor(
                        out=x[:, m, a:b], in0=bk[:, 0:b - a], scalar=mods[:, sid, 2, m:m + 1], in1=x[:, m, a:b], op0=ALU.mult, op1=ALU.add),
                        reads=[rbk, rmod, rx], writes=[rx])
        P.barrier()

    (lsid, l0, l1) = segs[-1]
    half = LQ // 2
    passes = [[(lsid, l0, l0 + half + 2)], [(lsid, l0 + half, l1)]]
    if with_ctx:
        passes[0].insert(0, segs[0])
    PL = max(sum(p[2] - p[1] for p in ps) for ps in passes)
    h, rh = K.sb([128, 8, PL], BF16, "h")
    aT, raT = K.sb([128, NJ, PL], BF16, "aT")
    scr = rms_scratch(K)
    gb = [K.sb([128, PL], F32, "gb%d" % i) for i in range(2)]
    cb = [K.sb([128, PL], F32, "cb%d" % i) for i in range(2)]
    wu = [K.sb([128, 8, 256], BF16, "wu%d" % i) for i in range(2)]
    wd = [K.sb([128, NJ, 128], BF16, "wd%d" % i) for i in range(2)]
    ecnt = [0]

    def evac_eng():
        ecnt[0] += 1
        return "act" if ecnt[0] % 2 else "dve"

    hc = l0 + half
    xsave, rxs = K.sb([128, 8, 1], F32, "xsave")
    xupd, rxu = K.sb([128, 8, 1], F32, "xupd")
    P.op("pool", lambda e: e.tensor_copy(out=xsave[:], in_=x[:, :, hc:hc + 1]), reads=[rx], writes=[rxs])
    for pi, ps in enumerate(passes):
        offs = []
        o = 0
        for (sid, a, b) in ps:
            offs.append(o)
            o += b - a
        if pi == 1:
            P.op("pool", lambda e: e.tensor_copy(out=xupd[:], in_=x[:, :, hc:hc + 1]), reads=[rx], writes=[rxu])
            P.op("pool", lambda e: e.tensor_copy(out=x[:, :, hc:hc + 1], in_=xsave[:]), reads=[rxs], writes=[rx])
        for (sid, a, b), o in zip(ps, offs):
            for (ba, bb) in split(a, b):
                rms_modulate(K, x, rx, ba, bb, ones_b, r_ob,
                             lambda k, sid=sid: mod1[:, sid, 4, k:k + 1], lambda k, sid=sid: mods[:, sid, 3, k:k + 1],
                             rmod1, h, rh, o + ba - a, scr)
        if pi == 1:
            P.op("pool", lambda e: e.tensor_copy(out=x[:, :, hc:hc + 1], in_=xupd[:]), reads=[rxu], writes=[rx])
        for j in range(NJ):
            w, rw = wu[j % 2]
            P.dma("pool", w[:, :, 0:128], w_up[:, j * 128:(j + 1) * 128].rearrange("(k p) c -> p k c", p=128), writes=[rw])
            P.dma("pool", w[:, :, 128:256], w_up[:, DFF + j * 128:DFF + (j + 1) * 128].rearrange("(k p) c -> p k c", p=128), writes=[rw])
            g, rg = gb[j % 2]
            c, rc = cb[j % 2]
            for (sid, a, b), o in zip(ps, offs):
                n = b - a
                for (ba, bb) in split(0, n):
                    bk, rbk = K.bank()
                    for k in range(8):
                        P.op("pe", lambda e, k=k, ba=ba, bb=bb, o=o, w=w, bk=bk: e.matmul(bk[:, 0:bb - ba], lhsT=w[:, k, 0:128], rhs=h[:, k, o + ba:o + bb],
                                                                                         start=(k == 0), stop=(k == 7)),
                             reads=[rw, rh], writes=[rbk])
                    P.op("act", lambda e, ba=ba, bb=bb, o=o, g=g, bk=bk: e.activation(out=g[:, o + ba:o + bb], in_=bk[:, 0:bb - ba], func=AF.Copy),
                         reads=[rbk], writes=[rg])
                (bs0, bs1) = [(q[1], q[2]) for q in segs if q[0] == sid][0]
                for (col, hi, cond) in ((o, (0 if sid == 1 else 2), a == bs0), (o + n - 1, (0 if sid == 1 else 2) + 1, b == bs1)):
                    if cond:
                        P.op("pool", lambda e, col=col, hi=hi, g=g: e.tensor_scalar(out=g[:, col:col + 1], in0=g[:, col:col + 1],
                                                                                    scalar1=hms[:, hi:hi + 1], scalar2=None, op0=ALU.mult),
                             reads=[rg, rhm], writes=[rg])
                P.op("pool", lambda e, o=o, n=n, g=g, c=c, j=j: e.tensor_scalar(out=c[:, o + 1:o + n - 1], in0=g[:, o:o + n - 2],
                                                                               scalar1=cws[:, 0, j:j + 1], scalar2=None, op0=ALU.mult),
                     reads=[rg, rcw], writes=[rc])
                P.op("dve", lambda e, o=o, n=n, g=g, c=c, j=j: e.scalar_tensor_tensor(out=c[:, o + 1:o + n - 1], in0=g[:, o + 1:o + n - 1],
                                                                                     scalar=cws[:, 1, j:j + 1], in1=c[:, o + 1:o + n - 1],
                                                                                     op0=ALU.mult, op1=ALU.add),
                     reads=[rg, rc, rcw], writes=[rc])
                P.op("dve", lambda e, o=o, n=n, g=g, c=c, j=j: e.scalar_tensor_tensor(out=c[:, o + 1:o + n - 1], in0=g[:, o + 2:o + n],
                                                                                     scalar=cws[:, 2, j:j + 1], in1=c[:, o + 1:o + n - 1],
                                                                                     op0=ALU.mult, op1=ALU.add),
                     reads=[rg, rc, rcw], writes=[rc])
                P.op("act", lambda e, o=o, n=n, c=c: e.activation(out=c[:, o + 1:o + n - 1], in_=c[:, o + 1:o + n - 1], func=AF.Silu),
                     reads=[rc], writes=[rc])
                for (ba, bb) in split(1, n - 1):
                    bk, rbk = K.bank()
                    for k in range(8):
                        P.op("pe", lambda e, k=k, ba=ba, bb=bb, o=o, w=w, bk=bk: e.matmul(bk[:, 0:bb - ba], lhsT=w[:, k, 128:256], rhs=h[:, k, o + ba:o + bb],
                                                                                         start=(k == 0), stop=(k == 7)),
                             reads=[rw, rh], writes=[rbk])
                    P.op("dve", lambda e, ba=ba, bb=bb, o=o, c=c, j=j, bk=bk: e.tensor_tensor(out=aT[:, j, o + ba:o + bb], in0=c[:, o + ba:o + bb],
                                                                                             in1=bk[:, 0:bb - ba], op=ALU.mult),
                         reads=[rc, rbk], writes=[raT])
        for m in range(8):
            w, rw = wd[m % 2]
            P.dma("pool", w[:], w_down[:, m * 128:(m + 1) * 128].rearrange("(j p) c -> p j c", p=128), writes=[rw])
            for (sid, a, b), o in zip(ps, offs):
                n = b - a
                for (ba, bb) in split(1, n - 1):
                    bk, rbk = K.bank()
                    for j in range(NJ):
                        P.op("pe", lambda e, j=j, ba=ba, bb=bb, o=o, w=w, bk=bk: e.matmul(bk[:, 0:bb - ba], lhsT=w[:, j, :], rhs=aT[:, j, o + ba:o + bb],
                                                                                         start=(j == 0), stop=(j == NJ - 1)),
                             reads=[rw, raT], writes=[rbk])
                    P.op("dve", lambda e, ba=ba, bb=bb, a=a, m=m, sid=sid, bk=bk: e.scalar_tensor_tensor(
                        out=x[:, m, a + ba:a + bb], in0=bk[:, 0:bb - ba], scalar=mods[:, sid, 5, m:m + 1], in1=x[:, m, a + ba:a + bb],
                        op0=ALU.mult, op1=ALU.add),
                        reads=[rbk, rmod, rx], writes=[rx])

    yv = yT.rearrange("(k p) n -> p k n", p=128)
    oc = 0
    for (sid, s0, s1) in segs:
        ni = s1 - s0 - 2
        if final:
            (sq, rsq), (t, rt), (rstd, rrs) = scr
            for (a, b) in split(s0 + 1, s1 - 1):
                n = b - a
                P.op("act", lambda e, a=a, b=b, n=n: e.activation(out=sq[:, :, 0:n], in_=x[:, :, a:b], func=AF.Square), reads=[rx], writes=[rsq])
                bk, rbk = K.bank()
                for k in range(8):
                    P.op("pe", lambda e, k=k, n=n, bk=bk: e.matmul(bk[:, 0:n], lhsT=ones_b[:], rhs=sq[:, k, 0:n], start=(k == 0), stop=(k == 7)),
                         reads=[r_ob, rsq], writes=[rbk])
                P.op("act", lambda e, n=n, bk=bk: e.activation(out=rstd[:, 0:n], in_=bk[:, 0:n], func=AF.Sqrt, scale=1.0 / D, bias=EPS),
                     reads=[rbk], writes=[rrs])
                P.op("dve", lambda e, n=n: e.reciprocal(out=rstd[:, 0:n], in_=rstd[:, 0:n]), reads=[rrs], writes=[rrs])
                P.op("dve", lambda e, a=a, b=b, n=n: e.tensor_tensor(out=t[:, :, 0:n], in0=x[:, :, a:b],
                                                                    in1=rstd[:, 0:n].unsqueeze(1).to_broadcast([128, 8, n]), op=ALU.mult),
                     reads=[rx, rrs], writes=[rt])
                P.op("pool", lambda e, n=n: e.tensor_tensor(out=t[:, :, 0:n], in0=t[:, :, 0:n],
                                                           in1=fns[:].unsqueeze(2).to_broadcast([128, 8, n]), op=ALU.mult),
                     reads=[rt, rfn], writes=[rt])
                P.dma("sp", yv[:, :, oc + a - s0 - 1:oc + b - s0 - 1], t[:, :, 0:n], reads=[rt])
        else:
            P.dma("sp", yv[:, :, oc:oc + ni], x[:, :, s0 + 1:s1 - 1], reads=[rx])
        oc += ni
    return K.done()


NA = CQ + LQ
HD = 128
NKEY = CTX + SEQ
NKT = NKEY // 128


def build_ka_att():
    K = KB()
    P = K.P
    xT = K.din("xT", [D, NA])
    mod = K.din("mod", [128, 2, 6, 8])
    w_qkv = K.din("w_qkv", [D, 1536])
    gains = K.din("gains", [128, 2])
    cosT = K.din("cosT", [128, LQ])
    sinT = K.din("sinT", [128, LQ])
    perm = K.din("perm", [128, 128])
    qkvT = K.dout("qkvT", [1536, NA])
    (ones_b, r_ob), (ones_f, r_of) = consts(K)
    x, rx = K.sb([128, 8, NA], F32, "x")
    h, rh = K.sb([128, 8, NA], BF16, "h")
    mods, rmod = K.sb([128, 2, 6, 8], F32, "mods")
    mod1, rmod1 = K.sb([128, 2, 6, 8], F32, "mod1")
    gs, rgs = K.sb([128, 2], F32, "gs")
    cs, rcs = K.sb([128, LQ], F32, "cs")
    sn, rsn = K.sb([128, LQ], F32, "sn")
    pm, rpm = K.sb([128, 128], F32, "pm")
    P.dma("sp", x[:], xT.rearrange("(k p) n -> p k n", p=128), writes=[rx])
    P.dma("sp", mods[:], mod, writes=[rmod])
    P.dma("sp", gs[:], gains, writes=[rgs])
    P.dma("sp", cs[:], cosT, writes=[rcs])
    P.dma("sp", sn[:], sinT, writes=[rsn])
    P.dma("sp", pm[:], perm, writes=[rpm])
    P.op("pool", lambda e: e.tensor_scalar(out=mod1[:], in0=mods[:], scalar1=1.0, scalar2=None, op0=ALU.add), reads=[rmod], writes=[rmod1])
    scr = rms_scratch(K)
    blocks = [(1, 0, CQ)] + [(0, a, b) for (a, b) in split(CQ, NA)]
    for (sid, a, b) in blocks:
        rms_modulate(K, x, rx, a, b, ones_b, r_ob, lambda k, sid=sid: mod1[:, sid, 1, k:k + 1], lambda k, sid=sid: mods[:, sid, 0, k:k + 1],
                     rmod1, h, rh, a, scr)
    wq = [K.sb([128, 8, 128], BF16, "wq%d" % i) for i in range(2)]
    sq = [K.sb([128, 512], BF16, "sq%d" % i) for i in range(2)]
    rs = [K.sb([128, 512], F32, "rs%d" % i) for i in range(2)]
    qn = [K.sb([128, 512], F32, "qn%d" % i) for i in range(2)]
    t1 = [K.sb([128, 512], F32, "t1%d" % i) for i in range(2)]
    t2 = [K.sb([128, 512], F32, "t2%d" % i) for i in range(2)]
    it = 0
    for m in range(12):
        w, rw = wq[m % 2]
        P.dma("pool", w[:], w_qkv[:, m * 128:(m + 1) * 128].rearrange("(k p) c -> p k c", p=128), writes=[rw])
        for (sid, a, b) in blocks:
            n = b - a
            it += 1
            bk, rbk = K.bank()
            for k in range(8):
                P.op("pe", lambda e, k=k, a=a, b=b, w=w, bk=bk: e.matmul(bk[:, 0:b - a], lhsT=w[:, k, :], rhs=h[:, k, a:b], start=(k == 0), stop=(k == 7)),
                     reads=[rw, rh], writes=[rbk])
            (q_, rq_) = qn[it % 2]
            if m >= 10:
                P.op("act", lambda e, n=n, bk=bk, q_=q_: e.activation(out=q_[:, 0:n], in_=bk[:, 0:n], func=AF.Copy), reads=[rbk], writes=[rq_])
                P.dma("sp", qkvT[m * 128:(m + 1) * 128, a:b], q_[:, 0:n], reads=[rq_])
                continue
            (s_, rs_) = sq[it % 2]
            (r_, rr_) = rs[it % 2]
            gi = 0 if m < 8 else 1
            P.op("act", lambda e, n=n, bk=bk, s_=s_: e.activation(out=s_[:, 0:n], in_=bk[:, 0:n], func=AF.Square), reads=[rbk], writes=[rs_])
            b2, rb2 = K.bank()
            P.op("pe", lambda e, n=n, b2=b2, s_=s_: e.matmul(b2[:, 0:n], lhsT=ones_b[:], rhs=s_[:, 0:n], start=True, stop=True),
                 reads=[r_ob, rs_], writes=[rb2])
            P.op("act", lambda e, n=n, b2=b2, r_=r_: e.activation(out=r_[:, 0:n], in_=b2[:, 0:n], func=AF.Sqrt, scale=1.0 / HD, bias=EPS),
                 reads=[rb2], writes=[rr_])
            P.op("dve", lambda e, n=n, r_=r_: e.reciprocal(out=r_[:, 0:n], in_=r_[:, 0:n]), reads=[rr_], writes=[rr_])
            P.op("dve", lambda e, n=n, bk=bk, q_=q_, r_=r_, gi=gi: e.scalar_tensor_tensor(out=q_[:, 0:n], in0=bk[:, 0:n], scalar=gs[:, gi:gi + 1], in1=r_[:, 0:n],
                                                                                       op0=ALU.mult, op1=ALU.mult),
                 reads=[rbk, rr_, rgs], writes=[rq_])
            if sid == 1:
                P.dma("sp", qkvT[m * 128:(m + 1) * 128, a:b], q_[:, 0:n], reads=[rq_])
                continue
            b3, rb3 = K.bank()
            P.op("pe", lambda e, n=n, b3=b3, q_=q_: e.matmul(b3[:, 0:n], lhsT=pm[:], rhs=q_[:, 0:n], start=True, stop=True),
                 reads=[rpm, rq_], writes=[rb3])
            (u1, ru1) = t1[it % 2]
            (u2, ru2) = t2[it % 2]
            P.op("pool", lambda e, n=n, a=a, q_=q_, u1=u1: e.tensor_tensor(out=u1[:, 0:n], in0=q_[:, 0:n], in1=cs[:, a - CQ:a - CQ + n], op=ALU.mult),
                 reads=[rq_, rcs], writes=[ru1])
            P.op("dve", lambda e, n=n, a=a, b3=b3, u2=u2: e.tensor_tensor(out=u2[:, 0:n], in0=b3[:, 0:n], in1=sn[:, a - CQ:a - CQ + n], op=ALU.mult),
                 reads=[rb3, rsn], writes=[ru2])
            P.op("pool", lambda e, n=n, u1=u1, u2=u2: e.tensor_tensor(out=u1[:, 0:n], in0=u1[:, 0:n], in1=u2[:, 0:n], op=ALU.add),
                 reads=[ru1, ru2], writes=[ru1])
            P.dma("sp", qkvT[m * 128:(m + 1) * 128, a:b], u1[:, 0:n], reads=[ru1])
    return K.done()


def build_kb_att(need_ctx):
    K = KB()
    P = K.P
    qT = K.din("qT", [D, NA])
    kT = K.din("kT", [256, NKEY])
    v_tm = K.din("v_tm", [128, NKT, 256])
    oT = K.dout("oT", [D, NA])
    (ones_b, r_ob), (ones_f, r_of) = consts(K)
    q, rq = K.sb([128, 8, NA], BF16, "q")
    k, rk = K.sb([128, 2, NKEY], BF16, "k")
    v, rv = K.sb([128, NKT, 256], BF16, "v")
    P.dma("pool", q[:], qT.rearrange("(h p) n -> p h n", p=128), writes=[rq])
    P.dma("pool", k[:], kT.rearrange("(h p) n -> p h n", p=128), writes=[rk])
    P.dma("pool", v[:], v_tm, writes=[rv])
    pt = [K.sb([128, 512], BF16, "pt%d" % i) for i in range(4)]
    ob = [K.sb([128, 512], F32, "ob%d" % i) for i in range(2)]
    rc = [K.sb([128, 512], F32, "rc%d" % i) for i in range(2)]
    sbank = [(K.banks[i], K.rbank[i]) for i in range(4)]
    accs = [((K.banks[4], K.rbank[4]), (K.banks[5], K.rbank[5])), ((K.banks[6], K.rbank[6]), (K.banks[7], K.rbank[7]))]
    scale = float(HD) ** -0.5
    qblocks = [(a, b, NKT) for (a, b) in split(CQ, NA)]
    if need_ctx:
        qblocks.append((0, CQ, CTX // 128))
    state = {"it": 0, "si": 0}

    def qblock(hq, kvh, a, b, nkt):
        n = b - a
        it = state["it"]
        (acc, racc), (rsum, rrsum) = accs[it % 2]
        o_, ro_ = ob[it % 2]
        c_, rc_ = rc[it % 2]
        state["it"] += 1

        def smm(kt, si):
            bk, rbk = sbank[si % 4]
            P.op("pe", lambda e: e.matmul(bk[:, 0:n], lhsT=k[:, kvh, kt * 128:(kt + 1) * 128], rhs=q[:, hq, a:b], start=True, stop=True),
                 reads=[rk, rq], writes=[rbk])

        def step(kt, si):
            bk, rbk = sbank[si % 4]
            p_, rp_ = pt[si % 4]
            P.op("act", lambda e: e.activation(out=p_[:, 0:n], in_=bk[:, 0:n], func=AF.Exp, scale=scale), reads=[rbk], writes=[rp_])
            P.op("pe", lambda e: e.matmul(acc[:, 0:n], lhsT=v[:, kt, kvh * 128:(kvh + 1) * 128], rhs=p_[:, 0:n],
                                          start=(kt == 0), stop=(kt == nkt - 1)),
                 reads=[rv, rp_], writes=[racc])
            P.op("pe", lambda e: e.matmul(rsum[:, 0:n], lhsT=ones_b[:], rhs=p_[:, 0:n], start=(kt == 0), stop=(kt == nkt - 1)),
                 reads=[r_ob, rp_], writes=[rrsum])

        smm(0, state["si"])
        for kt in range(nkt):
            if kt + 1 < nkt:
                smm(kt + 1, state["si"] + 1)
            step(kt, state["si"])
            state["si"] += 1
        P.op("dve", lambda e: e.reciprocal(out=c_[:, 0:n], in_=rsum[:, 0:n]), reads=[rrsum], writes=[rc_])
        P.op("dve", lambda e: e.tensor_tensor(out=o_[:, 0:n], in0=acc[:, 0:n], in1=c_[:, 0:n], op=ALU.mult), reads=[racc, rc_], writes=[ro_])
        P.dma("sp", oT[hq * 128:(hq + 1) * 128, a:b], o_[:, 0:n], reads=[ro_])

    for hq in range(8):
        for (a, b, nkt) in qblocks:
            qblock(hq, hq // 4, a, b, nkt)
    if not need_ctx:
        z, rz = K.sb([128, 8, CQ], F32, "z")
        P.op("pool", lambda e: e.memset(z[:], 0.0), writes=[rz])
        P.dma("sp", oT.rearrange("(h p) n -> p h n", p=128)[:, :, 0:CQ], z[:], reads=[rz])
    return K.done()


REC_IN = 3632
NF = 3584
CH = 64
NCHUNK = NKEY // CH


def build_ka_rec():
    K = KB()
    P = K.P
    segs, NT = layout(True)
    NTI = NT - 4
    xT = K.din("xT", [D, NT])
    mod = K.din("mod", [128, 2, 6, 8])
    w_in = K.din("w_in", [D, REC_IN])
    cw = K.din("cw", [128, 3, 12])
    hm = K.din("hm", [128, 4])
    dnp = K.din("dnp", [8, 2])
    w2 = K.din("w2", [16, 2, 256])
    b2T = K.din("b2T", [128, 2, 2])
    featT = K.dout("featT", [NF, NTI])
    gT = K.dout("gT", [16, NTI])
    laT = K.dout("laT", [512, NTI])
    (ones_b, r_ob), (ones_f, r_of) = consts(K)
    x, rx = K.sb([128, 8, NT], F32, "x")
    h, rh = K.sb([128, 8, NT], BF16, "h")
    mods, rmod = K.sb([128, 2, 6, 8], F32, "mods")
    mod1, rmod1 = K.sb([128, 2, 6, 8], F32, "mod1")
    cws, rcw = K.sb([128, 3, 12], F32, "cws")
    hms, rhm = K.sb([128, 4], F32, "hms")
    dns, rdn = K.sb([8, 2], F32, "dns")
    nA, rnA = K.sb([8, 1], F32, "nA")
    w2s, rw2 = K.sb([16, 2, 256], F32, "w2s")
    b2s, rb2 = K.sb([128, 2, 2], F32, "b2s")
    nb2, rnb2 = K.sb([128, 2, 2], F32, "nb2")
    P.dma("sp", x[:], xT.rearrange("(k p) n -> p k n", p=128), writes=[rx])
    P.dma("sp", mods[:], mod, writes=[rmod])
    P.dma("sp", cws[:], cw, writes=[rcw])
    P.dma("sp", hms[:], hm, writes=[rhm])
    P.dma("sp", dns[:], dnp, writes=[rdn])
    P.dma("sp", w2s[:], w2, writes=[rw2])
    P.dma("sp", b2s[:], b2T, writes=[rb2])
    P.op("pool", lambda e: e.tensor_scalar(out=mod1[:], in0=mods[:], scalar1=1.0, scalar2=None, op0=ALU.add), reads=[rmod], writes=[rmod1])
    P.op("pool", lambda e: e.tensor_scalar(out=nb2[:], in0=b2s[:], scalar1=-1.0, scalar2=None, op0=ALU.mult), reads=[rb2], writes=[rnb2])
    P.op("act", lambda e: e.activation(out=nA[:], in_=dns[:, 1:2], func=AF.Exp), reads=[rdn], writes=[rnA])
    P.op("dve", lambda e: e.tensor_scalar(out=nA[:], in0=nA[:], scalar1=-1.0, scalar2=None, op0=ALU.mult), reads=[rnA], writes=[rnA])
    scr = rms_scratch(K)
    for (sid, s0, s1) in segs:
        for (a, b) in split(s0, s1):
            rms_modulate(K, x, rx, a, b, ones_b, r_ob, lambda k, sid=sid: mod1[:, sid, 1, k:k + 1], lambda k, sid=sid: mods[:, sid, 0, k:k + 1],
                         rmod1, h, rh, a, scr)
    wq = [K.sb([128, 8, 128], BF16, "wq%d" % i) for i in range(2)]
    pre = [K.sb([128, NT], F32, "pre%d" % i) for i in range(2)]
    cb = [K.sb([128, NT], F32, "cb%d" % i) for i in range(2)]
    sqb = [K.sb([128, 512], F32, "sqb%d" % i) for i in range(2)]
    rsb = [K.sb([128, 512], F32, "rsb%d" % i) for i in range(2)]
    ocol = {1: 0, 0: CQ}

    def load_w(i, col, width=128):
        w, rw = wq[i % 2]
        P.dma("pool", w[:, :, 0:width], w_in[:, col:col + width].rearrange("(k p) c -> p k c", p=128), writes=[rw])
        return w, rw

    def proj(w, rw, width, a, b):
        bk, rbk = K.bank()
        for k in range(8):
            P.op("pe", lambda e, k=k: e.matmul(bk[0:width, 0:b - a], lhsT=w[:, k, 0:width], rhs=h[:, k, a:b], start=(k == 0), stop=(k == 7)),
                 reads=[rw, rh], writes=[rbk])
        return bk, rbk

    def conv_tile(m):
        w, rw = load_w(m, m * 128)
        g, rg = pre[m % 2]
        c, rc = cb[m % 2]
        for (sid, s0, s1) in segs:
            for i, (a, b) in enumerate(split(s0, s1)):
                bk, rbk = proj(w, rw, 128, a, b)
                if i % 2:
                    P.op("dve", lambda e, a=a, b=b, bk=bk: e.tensor_copy(out=g[:, a:b], in_=bk[:, 0:b - a]), reads=[rbk], writes=[rg])
                else:
                    P.op("act", lambda e, a=a, b=b, bk=bk: e.activation(out=g[:, a:b], in_=bk[:, 0:b - a], func=AF.Copy), reads=[rbk], writes=[rg])
        for (sid, s0, s1) in segs:
            for (col, hi) in ((s0, (0 if sid == 1 else 2)), (s1 - 1, (0 if sid == 1 else 2) + 1)):
                P.op("pool", lambda e, col=col, hi=hi: e.tensor_scalar(out=g[:, col:col + 1], in0=g[:, col:col + 1], scalar1=hms[:, hi:hi + 1],
                                                                       scalar2=None, op0=ALU.mult), reads=[rg, rhm], writes=[rg])
        for (sid, s0, s1) in segs:
            P.op("pool", lambda e, s0=s0, s1=s1: e.tensor_scalar(out=c[:, s0 + 1:s1 - 1], in0=g[:, s0:s1 - 2], scalar1=cws[:, 0, m:m + 1],
                                                                 scalar2=None, op0=ALU.mult), reads=[rg, rcw], writes=[rc])
            P.op("dve", lambda e, s0=s0, s1=s1: e.scalar_tensor_tensor(out=c[:, s0 + 1:s1 - 1], in0=g[:, s0 + 1:s1 - 1], scalar=cws[:, 1, m:m + 1],
                                                                       in1=c[:, s0 + 1:s1 - 1], op0=ALU.mult, op1=ALU.add),
                 reads=[rg, rc, rcw], writes=[rc])
            P.op("dve", lambda e, s0=s0, s1=s1: e.scalar_tensor_tensor(out=c[:, s0 + 1:s1 - 1], in0=g[:, s0 + 2:s1], scalar=cws[:, 2, m:m + 1],
                                                                        in1=c[:, s0 + 1:s1 - 1], op0=ALU.mult, op1=ALU.add),
                 reads=[rg, rc, rcw], writes=[rc])
            P.op("act", lambda e, s0=s0, s1=s1: e.activation(out=c[:, s0 + 1:s1 - 1], in_=c[:, s0 + 1:s1 - 1], func=AF.Silu), reads=[rc], writes=[rc])
        if m < 8:
            sc, bi = (128.0, 128.0 * EPS) if m < 4 else (1.0, EPS)
            it = 0
            for (sid, s0, s1) in segs:
                for (a, b) in split(s0 + 1, s1 - 1):
                    n = b - a
                    sq_, rsq_ = sqb[it % 2]
                    rs_, rrs_ = rsb[it % 2]
                    it += 1

                    def blk(a=a, b=b, n=n, sq_=sq_, rsq_=rsq_, rs_=rs_, rrs_=rrs_):
                        P.op("act", lambda e: e.activation(out=sq_[:, 0:n], in_=c[:, a:b], func=AF.Square), reads=[rc], writes=[rsq_])
                        bk, rbk = K.bank()
                        P.op("pe", lambda e: e.matmul(bk[:, 0:n], lhsT=ones_f[:], rhs=sq_[:, 0:n], start=True, stop=True), reads=[r_of, rsq_], writes=[rbk])
                        P.op("act", lambda e: e.activation(out=rs_[:, 0:n], in_=bk[:, 0:n], func=AF.Sqrt, scale=sc, bias=bi), reads=[rbk], writes=[rrs_])
                        P.op("dve", lambda e: e.reciprocal(out=rs_[:, 0:n], in_=rs_[:, 0:n]), reads=[rrs_], writes=[rrs_])
                        P.op("dve", lambda e: e.tensor_tensor(out=c[:, a:b], in0=c[:, a:b], in1=rs_[:, 0:n], op=ALU.mult), reads=[rc, rrs_], writes=[rc])
                    blk()
        for (sid, s0, s1) in segs:
            ni = s1 - s0 - 2
            P.dma("sp", featT[m * 128:(m + 1) * 128, ocol[sid]:ocol[sid] + ni], c[:, s0 + 1:s1 - 1], reads=[rc])

    for m in range(12):
        conv_tile(m)

    plain = [(1536 + i * 128, 1536 + i * 128, "silu") for i in range(4)]
    plain += [(2064 + i * 128, 2048 + i * 128, "scale") for i in range(2)]
    plain += [(2320 + i * 128, 2304 + i * 128, "copy") for i in range(2)]
    plain += [(2576 + i * 128, 2560 + i * 128, "copy") for i in range(4)]
    plain += [(3088 + i * 128, 3072 + i * 128, "silu") for i in range(4)]

    def plain_tile(i, wcol, orow, kind):
        w, rw = load_w(i, wcol)
        c, rc = cb[i % 2]
        for (sid, s0, s1) in segs:
            for (a, b) in split(s0 + 1, s1 - 1):
                bk, rbk = proj(w, rw, 128, a, b)
                if kind == "silu":
                    P.op("act", lambda e, a=a, b=b, bk=bk: e.activation(out=c[:, a:b], in_=bk[:, 0:b - a], func=AF.Silu), reads=[rbk], writes=[rc])
                elif kind == "scale":
                    P.op("dve", lambda e, a=a, b=b, bk=bk: e.tensor_scalar(out=c[:, a:b], in0=bk[:, 0:b - a], scalar1=0.125, scalar2=None, op0=ALU.mult),
                         reads=[rbk], writes=[rc])
                else:
                    P.op("dve", lambda e, a=a, b=b, bk=bk: e.tensor_copy(out=c[:, a:b], in_=bk[:, 0:b - a]), reads=[rbk], writes=[rc])
        for (sid, s0, s1) in segs:
            ni = s1 - s0 - 2
            P.dma("sp", featT[orow:orow + 128, ocol[sid]:ocol[sid] + ni], c[:, s0 + 1:s1 - 1], reads=[rc])

    for i, (wcol, orow, kind) in enumerate(plain):
        plain_tile(i, wcol, orow, kind)

    gst, rgst = K.sb([16, NT], F32, "gst")
    bst, rbst = K.sb([16, NT], F32, "bst")
    ggs = [K.sb([16, NT], F32, "ggs%d" % d) for d in range(2)]
    wa, rwa = load_w(0, 2048, 8)
    for (sid, s0, s1) in segs:
        for (a, b) in split(s0 + 1, s1 - 1):
            bk, rbk = proj(wa, rwa, 8, a, b)
            P.op("act", lambda e, a=a, b=b, bk=bk: e.activation(out=gst[0:8, a:b], in_=bk[0:8, 0:b - a], func=AF.Exp, bias=dns[:, 0:1]),
                 reads=[rbk, rdn], writes=[rgst])
    P.op("act", lambda e: e.activation(out=gst[0:8, :], in_=gst[0:8, :], func=AF.Ln, bias=1.0), reads=[rgst], writes=[rgst])
    P.op("dve", lambda e: e.tensor_scalar(out=gst[0:8, :], in0=gst[0:8, :], scalar1=nA[:, 0:1], scalar2=None, op0=ALU.mult), reads=[rgst, rnA], writes=[rgst])
    wb_, rwb_ = load_w(1, 2056, 8)
    for (sid, s0, s1) in segs:
        for (a, b) in split(s0 + 1, s1 - 1):
            bk, rbk = proj(wb_, rwb_, 8, a, b)
            P.op("act", lambda e, a=a, b=b, bk=bk: e.activation(out=bst[0:8, a:b], in_=bk[0:8, 0:b - a], func=AF.Exp, scale=-1.0), reads=[rbk], writes=[rbst])
    P.op("dve", lambda e: e.tensor_scalar(out=bst[0:8, :], in0=bst[0:8, :], scalar1=1.0, scalar2=None, op0=ALU.add), reads=[rbst], writes=[rbst])
    P.op("dve", lambda e: e.reciprocal(out=bst[0:8, :], in_=bst[0:8, :]), reads=[rbst], writes=[rbst])
    for (sid, s0, s1) in segs:
        ni = s1 - s0 - 2
        P.dma("sp", gT[0:8, ocol[sid]:ocol[sid] + ni], gst[0:8, s0 + 1:s1 - 1], reads=[rgst])
        P.dma("sp", gT[8:16, ocol[sid]:ocol[sid] + ni], bst[0:8, s0 + 1:s1 - 1], reads=[rbst])
    for d in range(2):
        wg, rwg = load_w(d, 3600 + 16 * d, 16)
        gg_, rgg_ = ggs[d]
        for (sid, s0, s1) in segs:
            for (a, b) in split(s0 + 1, s1 - 1):
                bk, rbk = proj(wg, rwg, 16, a, b)
                P.op("dve", lambda e, a=a, b=b, bk=bk, gg_=gg_: e.tensor_copy(out=gg_[0:16, a:b], in_=bk[0:16, 0:b - a]), reads=[rbk], writes=[rgg_])
        for ft in range(2):
            c, rc = cb[ft % 2]
            for (sid, s0, s1) in segs:
                for (a, b) in split(s0 + 1, s1 - 1):
                    bk, rbk = K.bank()
                    P.op("pe", lambda e, a=a, b=b, bk=bk, gg_=gg_, d=d, ft=ft: e.matmul(bk[:, 0:b - a], lhsT=w2s[:, d, ft * 128:(ft + 1) * 128], rhs=gg_[0:16, a:b],
                                                                                       start=True, stop=True), reads=[rw2, rgg_], writes=[rbk])
                    P.op("act", lambda e, a=a, b=b, bk=bk, c=c, d=d, ft=ft: e.activation(out=c[:, a:b], in_=bk[:, 0:b - a], func=AF.Exp, scale=-1.0,
                                                                                        bias=nb2[:, d, ft:ft + 1]), reads=[rbk, rnb2], writes=[rc])
            P.op("act", lambda e, c=c: e.activation(out=c[:], in_=c[:], func=AF.Ln, bias=1.0), reads=[rc], writes=[rc])
            P.op("pool", lambda e, c=c: e.tensor_scalar(out=c[:], in0=c[:], scalar1=-1.0 / 16.0, scalar2=None, op0=ALU.mult), reads=[rc], writes=[rc])
            for (sid, s0, s1) in segs:
                ni = s1 - s0 - 2
                P.dma("sp", laT[d * 256 + ft * 128:d * 256 + (ft + 1) * 128, ocol[sid]:ocol[sid] + ni], c[:, s0 + 1:s1 - 1], reads=[rc])
    return K.done()


GC = 4
GW = GC * CH
NG = NCHUNK // GC


def build_kb_rec(NG=NG):
    K = KB()
    P = K.P
    T = NKEY
    dq = [K.din("dq%d" % d, [128, T]) for d in range(2)]
    dk = [K.din("dk%d" % d, [128, T]) for d in range(2)]
    dktm = [K.din("dktm%d" % d, [64, NCHUNK, 128]) for d in range(2)]
    dvtm = [K.din("dvtm%d" % d, [64, NCHUNK, 128]) for d in range(2)]
    dg = [K.din("dg%d" % d, [64, NCHUNK]) for d in range(2)]
    dbt = [K.din("dbt%d" % d, [64, NCHUNK]) for d in range(2)]
    gq = [K.din("gq%d" % d, [64, T]) for d in range(2)]
    gk = [K.din("gk%d" % d, [64, T]) for d in range(2)]
    gktm = [K.din("gktm%d" % d, [64, NCHUNK, 64]) for d in range(2)]
    gvtm = [K.din("gvtm%d" % d, [64, NCHUNK, 128]) for d in range(2)]
    gla = [K.din("gla%d" % d, [64, NCHUNK, 64]) for d in range(2)]
    cm = K.din("cm", [64, 6, 64])
    odn = [K.dout("odn%d" % d, [128, T]) for d in range(2)]
    ogl = [K.dout("ogl%d" % d, [128, T]) for d in range(2)]
    (ones_b, r_ob), (ones_f, r_of) = consts(K)
    acc_dn = [(K.banks[4 + d], K.rbank[4 + d]) for d in range(2)]
    acc_gl = [(K.banks[6 + d], K.rbank[6 + d]) for d in range(2)]
    rot = {"i": 0}

    def bank():
        i = rot["i"]
        rot["i"] = (i + 1) % 4
        return K.banks[i], K.rbank[i]

    cms, rcm = K.sb([64, 6, 64], F32, "cms")
    P.dma("sp", cms[:], cm, writes=[rcm])
    U = cms[:, 0, :]
    Us = cms[:, 1, :]
    Ls = cms[:, 2, :]
    NegU = cms[:, 3, :]
    NegLs = cms[:, 4, :]
    I64 = cms[:, 5, :]

    def bc(ap2, n=GC):
        return ap2.unsqueeze(1).to_broadcast([64, n, 64])

    def flat(t):
        return t[:].rearrange("p a b -> p (a b)")

    gcol, bcol, Gcol, ekd, gl, bg = [], [], [], [], [], []
    for d in range(2):
        g_, rg_ = K.sb([64, NCHUNK], F32, "gcol%d" % d)
        b_, rb_ = K.sb([64, NCHUNK], F32, "bcol%d" % d)
        G_, rG_ = K.sb([64, NCHUNK], F32, "Gcol%d" % d)
        e_, re_ = K.sb([64, NCHUNK], F32, "ekd%d" % d)
        l_, rl_ = K.sb([128, NCHUNK], F32, "gl%d" % d)
        x_, rx_ = K.sb([64, NCHUNK], F32, "bg%d" % d)
        P.dma("sp", g_[:], dg[d], writes=[rg_])
        P.dma("sp", b_[:], dbt[d], writes=[rb_])

        def pre(g_=g_, rg_=rg_, b_=b_, rb_=rb_, G_=G_, rG_=rG_, e_=e_, re_=re_, l_=l_, rl_=rl_, x_=x_, rx_=rx_):
            b1, rb1 = bank()
            P.op("pe", lambda e: e.matmul(b1[0:64, 0:NCHUNK], lhsT=U, rhs=g_[:], start=True, stop=True), reads=[rcm, rg_], writes=[rb1])
            P.op("act", lambda e: e.activation(out=G_[:], in_=b1[0:64, 0:NCHUNK], func=AF.Copy), reads=[rb1], writes=[rG_])
            b2, rb2 = bank()
            P.op("pe", lambda e: e.matmul(b2[:, 0:NCHUNK], lhsT=ones_f[0:64, :], rhs=g_[:], start=True, stop=True), reads=[r_of, rg_], writes=[rb2])
            P.op("act", lambda e: e.activation(out=l_[:], in_=b2[:, 0:NCHUNK], func=AF.Exp), reads=[rb2], writes=[rl_])
            P.op("dve", lambda e: e.tensor_tensor(out=e_[:], in0=b2[0:64, 0:NCHUNK], in1=G_[:], op=ALU.subtract), reads=[rb2, rG_], writes=[re_])
            P.op("act", lambda e: e.activation(out=e_[:], in_=e_[:], func=AF.Exp), reads=[re_], writes=[re_])
            P.op("act", lambda e: e.activation(out=x_[:], in_=G_[:], func=AF.Exp), reads=[rG_], writes=[rx_])
            P.op("dve", lambda e: e.tensor_tensor(out=x_[:], in0=x_[:], in1=b_[:], op=ALU.mult), reads=[rx_, rb_], writes=[rx_])
        pre()
        gcol.append((g_, rg_)); bcol.append((b_, rb_)); Gcol.append((G_, rG_)); ekd.append((e_, re_)); gl.append((l_, rl_)); bg.append((x_, rx_))

    def tl(shape, name):
        return K.sb(shape, F32, name)
    DL = [[{"q": tl([128, GW], "dLq%d%d" % (d, p)), "k": tl([128, GW], "dLk%d%d" % (d, p)), "ktm": tl([64, GC, 128], "dLkt%d%d" % (d, p)),
            "vtm": tl([64, GC, 128], "dLvt%d%d" % (d, p))} for p in range(2)] for d in range(2)]
    DT = [{"R": tl([64, GC, 64], "R%d" % d), "R2": tl([64, GC, 64], "R2%d" % d), "eG": tl([128, GW], "eG%d" % d), "kbT": tl([128, GW], "kbT%d" % d),
           "D1": tl([64, GC, 64], "D1%d" % d), "D2": tl([64, GC, 64], "D2%d" % d), "E1s": tl([64, GC, 64], "E1s%d" % d),
           "AT": tl([64, GC, 64], "AT%d" % d), "A": tl([64, GC, 64], "A%d" % d), "Pt": tl([64, GC, 64], "Pt%d" % d),
           "M": [tl([64, GC, 64], "M%d%d" % (d, i)) for i in range(2)], "MT": [tl([64, GC, 64], "MT%d%d" % (d, i)) for i in range(2)],
           "vb": tl([64, GC, 128], "vb%d" % d), "kbg": tl([64, GC, 128], "kbg%d" % d)} for d in range(2)]
    DS = [[{"IT": tl([64, GC, 64], "IT%d%d" % (d, p)), "qd": tl([128, GW], "qd%d%d" % (d, p)), "kdec": tl([64, GC, 128], "kdec%d%d" % (d, p)),
            "u": tl([64, GC, 128], "u%d%d" % (d, p)), "wT": tl([128, GC, 64], "wT%d%d" % (d, p))} for p in range(2)] for d in range(2)]
    Sdn = [[tl([128, 128], "Sdn%d%d" % (d, i)) for i in range(2)] for d in range(2)]
    vnb = [[tl([64, 128], "vn%d%d" % (d, i)) for i in range(2)] for d in range(2)]
    ost = [[tl([128, GW], "ost%d%d" % (m, d)) for d in range(2)] for m in range(2)]
    GL = [[{"q": tl([64, GW], "gLq%d%d" % (d, p)), "k": tl([64, GW], "gLk%d%d" % (d, p)), "ktm": tl([64, GC, 64], "gLkt%d%d" % (d, p)),
            "la": tl([64, GC, 64], "gLla%d%d" % (d, p))} for p in range(2)] for d in range(2)]
    GV = [[tl([64, GC, 128], "gLv%d%d" % (d, p)) for p in range(3)] for d in range(2)]
    GT = [{"b": tl([64, GC, 64], "gb%d" % d), "dd": tl([64, GC, 64], "gdd%d" % d), "kst": tl([64, GC, 64], "gkst%d" % d),
           "bT": tl([64, GC, 64], "gbT%d" % d), "e1": tl([64, GC, 64], "ge1%d" % d), "eq": tl([64, GC, 64], "geq%d" % d),
           "ek": tl([64, GC, 64], "gek%d" % d), "ei": tl([64, GC, 64], "gei%d" % d), "qtl": tl([64, GW], "gqtl%d" % d),
           "ktl": tl([64, GW], "gktl%d" % d)} for d in range(2)]
    GS = [[{"qi": tl([64, GW], "gqi%d%d" % (d, p)), "att": tl([64, GC, 64], "gatt%d%d" % (d, p)), "KV": tl([64, GC, 128], "gKV%d%d" % (d, p)),
            "al": tl([64, GC], "gal%d%d" % (d, p))} for p in range(2)] for d in range(2)]
    Sgl = [[tl([64, 128], "Sgl%d%d" % (d, i)) for i in range(2)] for d in range(2)]
    for d in range(2):
        for (t_, r_) in (Sdn[d][0], Sgl[d][0]):
            P.op("pool", lambda e, t_=t_: e.memset(t_[:], 0.0), writes=[r_])
    scur = {"dn": [0, 0], "gl": [0, 0], "vn": [0, 0]}

    def loads(gi):
        n0 = gi * GC
        t0 = n0 * CH
        p = gi % 2
        for d in range(2):
            L = DL[d][p]
            P.dma("sp", L["q"][0][:], dq[d][:, t0:t0 + GW], writes=[L["q"][1]])
            P.dma("sp", L["k"][0][:], dk[d][:, t0:t0 + GW], writes=[L["k"][1]])
            P.dma("sp", L["ktm"][0][:], dktm[d][:, n0:n0 + GC, :], writes=[L["ktm"][1]])
            P.dma("sp", L["vtm"][0][:], dvtm[d][:, n0:n0 + GC, :], writes=[L["vtm"][1]])
            G = GL[d][p]
            P.dma("sp", G["q"][0][:], gq[d][:, t0:t0 + GW], writes=[G["q"][1]])
            P.dma("sp", G["k"][0][:], gk[d][:, t0:t0 + GW], writes=[G["k"][1]])
            P.dma("sp", G["ktm"][0][:], gktm[d][:, n0:n0 + GC, :], writes=[G["ktm"][1]])
            P.dma("sp", G["la"][0][:], gla[d][:, n0:n0 + GC, :], writes=[G["la"][1]])
            gv_, rgv_ = GV[d][gi % 3]
            P.dma("sp", gv_[:], gvtm[d][:, n0:n0 + GC, :], writes=[rgv_])

    def cs(n):
        return slice(n * CH, (n + 1) * CH)

    def v3(ap2):
        return ap2.rearrange("p (a b) -> p a b", a=GC)

    def dn_prep(d, gi):
        n0 = gi * GC
        p = gi % 2
        L, Tm, S2 = DL[d][p], DT[d], DS[d][p]
        (qT, rqT), (kT, rkT), (ktm, rktm), (vtm, rvtm) = L["q"], L["k"], L["ktm"], L["vtm"]
        (R, rR), (R2, rR2), (eG, reG), (kbT, rkbT) = Tm["R"], Tm["R2"], Tm["eG"], Tm["kbT"]
        (D1, rD1), (D2, rD2), (E1s, rE1s), (AT, rAT), (A_, rA_), (Pt, rPt) = Tm["D1"], Tm["D2"], Tm["E1s"], Tm["AT"], Tm["A"], Tm["Pt"]
        (vb, rvb), (kbg, rkbg) = Tm["vb"], Tm["kbg"]
        (IT, rIT), (qd, rqd), (kdec, rkdec), (u, ru), (wT, rwT) = S2["IT"], S2["qd"], S2["kdec"], S2["u"], S2["wT"]
        (g_, rg_), (b_, rb_), (G_, rG_), (e_, re_), (x_, rx_) = gcol[d], bcol[d], Gcol[d], ekd[d], bg[d]
        gs = slice(n0, n0 + GC)
        P.op("pool", lambda e: e.tensor_tensor(out=R[:], in0=g_[:, gs].unsqueeze(2).to_broadcast([64, GC, 64]), in1=bc(U), op=ALU.mult),
             reads=[rg_, rcm], writes=[rR])
        P.op("pool", lambda e: e.tensor_tensor(out=R2[:], in0=b_[:, gs].unsqueeze(2).to_broadcast([64, GC, 64]), in1=bc(I64), op=ALU.mult),
             reads=[rb_, rcm], writes=[rR2])
        bA, rbA = bank()
        P.op("pe", lambda e: e.matmul(bA[:, 0:GW], lhsT=ones_f[0:64, :], rhs=flat(R), start=True, stop=True), reads=[r_of, rR], writes=[rbA])
        bB, rbB = bank()
        P.op("pe", lambda e: e.matmul(bB[:, 0:GW], lhsT=ones_f[0:64, :], rhs=flat(R2), start=True, stop=True), reads=[r_of, rR2], writes=[rbB])
        P.op("act", lambda e: e.activation(out=eG[:], in_=bA[:, 0:GW], func=AF.Exp), reads=[rbA], writes=[reG])
        P.op("dve", lambda e: e.tensor_tensor(out=kbT[:], in0=kT[:], in1=bB[:, 0:GW], op=ALU.mult), reads=[rkT, rbB], writes=[rkbT])
        P.op("dve", lambda e: e.tensor_tensor(out=D1[:], in0=v3(bA[0:64, 0:GW]), in1=G_[:, gs].unsqueeze(2).to_broadcast([64, GC, 64]), op=ALU.subtract),
             reads=[rbA, rG_], writes=[rD1])
        yield
        P.op("pool", lambda e: e.tensor_tensor(out=D2[:], in0=D1[:], in1=bc(Ls), op=ALU.mult), reads=[rD1, rcm], writes=[rD2])
        P.op("pool", lambda e: e.tensor_tensor(out=D2[:], in0=D2[:], in1=bc(NegLs), op=ALU.add), reads=[rD2, rcm], writes=[rD2])
        P.op("pool", lambda e: e.tensor_tensor(out=D1[:], in0=D1[:], in1=bc(U), op=ALU.mult), reads=[rD1, rcm], writes=[rD1])
        P.op("dve", lambda e: e.tensor_tensor(out=D1[:], in0=D1[:], in1=bc(NegU), op=ALU.add), reads=[rD1, rcm], writes=[rD1])
        P.op("act", lambda e: e.activation(out=D1[:], in_=D1[:], func=AF.Exp), reads=[rD1], writes=[rD1])
        P.op("act", lambda e: e.activation(out=D2[:], in_=D2[:], func=AF.Exp), reads=[rD2], writes=[rD2])
        P.op("pool", lambda e: e.tensor_tensor(out=E1s[:], in0=D1[:], in1=bc(Us), op=ALU.mult), reads=[rD1, rcm], writes=[rE1s])
        P.op("dve", lambda e: e.tensor_tensor(out=qd[:], in0=qT[:], in1=eG[:], op=ALU.mult), reads=[rqT, reG], writes=[rqd])
        yield
        bC, rbC = bank()
        for n in range(GC):
            P.op("pe", lambda e, n=n: e.matmul(bC[0:64, cs(n)], lhsT=kT[:, cs(n)], rhs=kbT[:, cs(n)], start=True, stop=True), reads=[rkT, rkbT], writes=[rbC])
        P.op("dve", lambda e: e.tensor_tensor(out=AT[:], in0=v3(bC[0:64, 0:GW]), in1=E1s[:], op=ALU.mult), reads=[rbC, rE1s], writes=[rAT])
        bD, rbD = bank()
        for n in range(GC):
            P.op("pe", lambda e, n=n: e.matmul(bD[0:64, cs(n)], lhsT=kbT[:, cs(n)], rhs=kT[:, cs(n)], start=True, stop=True), reads=[rkT, rkbT], writes=[rbD])
        P.op("dve", lambda e: e.tensor_tensor(out=A_[:], in0=v3(bD[0:64, 0:GW]), in1=D2[:], op=ALU.mult), reads=[rbD, rD2], writes=[rA_])
        yield
        bE, rbE = bank()
        for n in range(GC):
            P.op("pe", lambda e, n=n: e.matmul(bE[0:64, cs(n)], lhsT=kT[:, cs(n)], rhs=qT[:, cs(n)], start=True, stop=True), reads=[rkT, rqT], writes=[rbE])
        P.op("dve", lambda e: e.tensor_tensor(out=IT[:], in0=v3(bE[0:64, 0:GW]), in1=D1[:], op=ALU.mult), reads=[rbE, rD1], writes=[rIT])
        P.op("pool", lambda e: e.tensor_tensor(out=Pt[:], in0=bc(I64), in1=AT[:], op=ALU.subtract), reads=[rcm, rAT], writes=[rPt])
        yield
        (M0, rM0), (MT0, rMT0) = Tm["M"][0], Tm["MT"][0]
        b1, rb1 = bank()
        for n in range(GC):
            P.op("pe", lambda e, n=n: e.matmul(b1[0:64, cs(n)], lhsT=AT[:, n, :], rhs=A_[:, n, :], start=True, stop=True), reads=[rAT, rA_], writes=[rb1])
        P.op("act", lambda e: e.activation(out=M0[:], in_=v3(b1[0:64, 0:GW]), func=AF.Copy), reads=[rb1], writes=[rM0])
        b2, rb2 = bank()
        for n in range(GC):
            P.op("pe", lambda e, n=n: e.matmul(b2[0:64, cs(n)], lhsT=A_[:, n, :], rhs=AT[:, n, :], start=True, stop=True), reads=[rAT, rA_], writes=[rb2])
        P.op("dve", lambda e: e.tensor_copy(out=MT0[:], in_=v3(b2[0:64, 0:GW])), reads=[rb2], writes=[rMT0])
        yield
        for k in range(1, 6):
            (Mc, rMc), (MTc, rMTc) = Tm["M"][(k - 1) % 2], Tm["MT"][(k - 1) % 2]
            (Mn, rMn), (MTn, rMTn) = Tm["M"][k % 2], Tm["MT"][k % 2]

            def step(k=k, Mc=Mc, rMc=rMc, MTc=MTc, rMTc=rMTc, Mn=Mn, rMn=rMn, MTn=MTn, rMTn=rMTn):
                b1, rb1 = bank()
                for n in range(GC):
                    P.op("pe", lambda e, n=n: e.matmul(b1[0:64, cs(n)], lhsT=Mc[:, n, :], rhs=Pt[:, n, :], start=True, stop=True), reads=[rMc, rPt], writes=[rb1])
                P.op("dve", lambda e: e.tensor_tensor(out=Pt[:], in0=Pt[:], in1=v3(b1[0:64, 0:GW]), op=ALU.add), reads=[rPt, rb1], writes=[rPt])
                if k < 5:
                    b2, rb2 = bank()
                    for n in range(GC):
                        P.op("pe", lambda e, n=n: e.matmul(b2[0:64, cs(n)], lhsT=MTc[:, n, :], rhs=Mc[:, n, :], start=True, stop=True), reads=[rMc, rMTc], writes=[rb2])
                    P.op("act", lambda e: e.activation(out=Mn[:], in_=v3(b2[0:64, 0:GW]), func=AF.Copy), reads=[rb2], writes=[rMn])
                if k < 4:
                    b3, rb3 = bank()
                    for n in range(GC):
                        P.op("pe", lambda e, n=n: e.matmul(b3[0:64, cs(n)], lhsT=Mc[:, n, :], rhs=MTc[:, n, :], start=True, stop=True), reads=[rMc, rMTc], writes=[rb3])
                    P.op("act", lambda e: e.activation(out=MTn[:], in_=v3(b3[0:64, 0:GW]), func=AF.Copy), reads=[rb3], writes=[rMTn])
            step()
            yield
        P.op("pool", lambda e: e.tensor_tensor(out=vb[:], in0=vtm[:], in1=b_[:, gs].unsqueeze(2).to_broadcast([64, GC, 128]), op=ALU.mult),
             reads=[rvtm, rb_], writes=[rvb])
        P.op("pool", lambda e: e.tensor_tensor(out=kbg[:], in0=ktm[:], in1=x_[:, gs].unsqueeze(2).to_broadcast([64, GC, 128]), op=ALU.mult),
             reads=[rktm, rx_], writes=[rkbg])
        P.op("pool", lambda e: e.tensor_tensor(out=kdec[:], in0=ktm[:], in1=e_[:, gs].unsqueeze(2).to_broadcast([64, GC, 128]), op=ALU.mult),
             reads=[rktm, re_], writes=[rkdec])
        bu, rbu = bank()
        for n in range(GC):
            P.op("pe", lambda e, n=n: e.matmul(bu[0:64, n * 128:(n + 1) * 128], lhsT=Pt[:, n, :], rhs=vb[:, n, :], start=True, stop=True), reads=[rPt, rvb], writes=[rbu])
        P.op("act", lambda e: e.activation(out=flat(u), in_=bu[0:64, 0:GC * 128], func=AF.Copy), reads=[rbu], writes=[ru])
        bw, rbw = bank()
        for n in range(GC):
            P.op("pe", lambda e, n=n: e.matmul(bw[:, cs(n)], lhsT=kbg[:, n, :], rhs=Pt[:, n, :], start=True, stop=True), reads=[rPt, rkbg], writes=[rbw])
        P.op("dve", lambda e: e.tensor_copy(out=flat(wT), in_=bw[:, 0:GW]), reads=[rbw], writes=[rwT])
        yield

    def dn_seq(d, gi):
        n0 = gi * GC
        t0 = n0 * CH
        S2 = DS[d][gi % 2]
        (IT, rIT), (qd, rqd), (kdec, rkdec), (u, ru), (wT, rwT) = S2["IT"], S2["qd"], S2["kdec"], S2["u"], S2["wT"]
        (l_, rl_) = gl[d]
        (oacc, roacc) = acc_dn[d]
        for n in range(GC):
            def chunk(n=n):
                c = scur["dn"][d]
                (Sc, rSc), (Sn, rSn) = Sdn[d][c], Sdn[d][1 - c]
                scur["dn"][d] = 1 - c
                (vn, rvn) = vnb[d][scur["vn"][d]]
                scur["vn"][d] ^= 1
                b1, rb1 = bank()
                P.op("pe", lambda e: e.matmul(b1[0:64, 0:128], lhsT=wT[:, n, :], rhs=Sc[:], start=True, stop=True), reads=[rwT, rSc], writes=[rb1])
                P.op("dve", lambda e: e.tensor_tensor(out=vn[:], in0=u[:, n, :], in1=b1[0:64, 0:128], op=ALU.subtract), reads=[ru, rb1], writes=[rvn])
                b3, rb3 = bank()
                P.op("pe", lambda e: e.matmul(b3[:, 0:128], lhsT=kdec[:, n, :], rhs=vn[:], start=True, stop=True), reads=[rkdec, rvn], writes=[rb3])
                P.op("pe", lambda e: e.matmul(oacc[:, cs(n)], lhsT=Sc[:], rhs=qd[:, cs(n)], start=True, stop=False), reads=[rSc, rqd], writes=[roacc])
                P.op("pe", lambda e: e.matmul(oacc[:, cs(n)], lhsT=vn[:], rhs=IT[:, n, :], start=False, stop=True), reads=[rvn, rIT], writes=[roacc])
                P.op("dve", lambda e: e.scalar_tensor_tensor(out=Sn[:], in0=Sc[:], scalar=l_[:, n0 + n:n0 + n + 1], in1=b3[:, 0:128], op0=ALU.mult, op1=ALU.add),
                     reads=[rSc, rl_, rb3], writes=[rSn])
            chunk()
            yield
        (o_, ro_) = ost[0][d]
        P.op("act", lambda e: e.activation(out=o_[:], in_=oacc[:, 0:GW], func=AF.Copy), reads=[roacc], writes=[ro_])
        P.dma("sp", odn[d][:, t0:t0 + GW], o_[:], reads=[ro_])
        yield

    def gl_prep(d, gi):
        n0 = gi * GC
        p = gi % 2
        G, Tm, S2 = GL[d][p], GT[d], GS[d][p]
        (qT, rqT), (kT, rkT), (ktm, rktm), (la, rla) = G["q"], G["k"], G["ktm"], G["la"]
        (gv_, rgv_) = GV[d][gi % 3]
        (b_, rb_), (dd, rdd), (kst, rkst), (bT, rbT), (e1, re1), (eq, req), (ek, rek), (ei, rei), (qtl, rqtl), (ktl, rktl) = (
            Tm["b"], Tm["dd"], Tm["kst"], Tm["bT"], Tm["e1"], Tm["eq"], Tm["ek"], Tm["ei"], Tm["qtl"], Tm["ktl"])
        (qi, rqi), (att, ratt), (KV, rKV), (al, ral) = S2["qi"], S2["att"], S2["KV"], S2["al"]
        bb, rbb = bank()
        P.op("pe", lambda e: e.matmul(bb[0:64, 0:GW], lhsT=U, rhs=flat(la), start=True, stop=True), reads=[rcm, rla], writes=[rbb])
        bl, rbl = bank()
        P.op("pe", lambda e: e.matmul(bl[0:64, 0:GW], lhsT=ones_f[0:64, 0:64], rhs=flat(la), start=True, stop=True), reads=[r_of, rla], writes=[rbl])
        bt, rbt = bank()
        for n in range(GC):
            P.op("pe", lambda e, n=n: e.matmul(bt[0:64, cs(n)], lhsT=la[:, n, :], rhs=U, start=True, stop=True), reads=[rcm, rla], writes=[rbt])
        P.op("act", lambda e: e.activation(out=flat(b_), in_=bb[0:64, 0:GW], func=AF.Copy), reads=[rbb], writes=[rb_])
        P.op("dve", lambda e: e.tensor_tensor(out=flat(dd), in0=bl[0:64, 0:GW], in1=flat(b_), op=ALU.subtract), reads=[rbl, rb_], writes=[rdd])
        P.op("act", lambda e: e.activation(out=dd[:], in_=dd[:], func=AF.Exp), reads=[rdd], writes=[rdd])
        P.op("pool", lambda e: e.tensor_tensor(out=kst[:], in0=ktm[:], in1=dd[:], op=ALU.mult), reads=[rktm, rdd], writes=[rkst])
        P.op("act", lambda e: e.activation(out=flat(bT), in_=bt[0:64, 0:GW], func=AF.Copy), reads=[rbt], writes=[rbT])
        yield
        P.op("pool", lambda e: e.tensor_tensor(out=e1[:], in0=bT[:], in1=bT[:, :, 32:33].to_broadcast([64, GC, 64]), op=ALU.subtract), reads=[rbT], writes=[re1])
        P.op("act", lambda e: e.activation(out=eq[:], in_=e1[:], func=AF.Exp), reads=[re1], writes=[req])
        P.op("act", lambda e: e.activation(out=ek[:], in_=e1[:], func=AF.Exp, scale=-1.0), reads=[re1], writes=[rek])
        P.op("act", lambda e: e.activation(out=ei[:], in_=bT[:], func=AF.Exp), reads=[rbT], writes=[rei])
        P.op("pool", lambda e: e.tensor_tensor(out=qtl[:], in0=qT[:], in1=flat(eq), op=ALU.mult), reads=[rqT, req], writes=[rqtl])
        P.op("dve", lambda e: e.tensor_tensor(out=ktl[:], in0=kT[:], in1=flat(ek), op=ALU.mult), reads=[rkT, rek], writes=[rktl])
        P.op("pool", lambda e: e.tensor_tensor(out=qi[:], in0=qT[:], in1=flat(ei), op=ALU.mult), reads=[rqT, rei], writes=[rqi])
        P.op("act", lambda e: e.activation(out=al[:], in_=bT[:, :, 63], func=AF.Exp), reads=[rbT], writes=[ral])
        yield
        ba, rba = bank()
        for n in range(GC):
            P.op("pe", lambda e, n=n: e.matmul(ba[0:64, cs(n)], lhsT=ktl[:, cs(n)], rhs=qtl[:, cs(n)], start=True, stop=True), reads=[rktl, rqtl], writes=[rba])
        P.op("dve", lambda e: e.tensor_tensor(out=att[:], in0=v3(ba[0:64, 0:GW]), in1=bc(U), op=ALU.mult), reads=[rba, rcm], writes=[ratt])
        bkv, rbkv = bank()
        for n in range(GC):
            P.op("pe", lambda e, n=n: e.matmul(bkv[0:64, n * 128:(n + 1) * 128], lhsT=kst[:, n, :], rhs=gv_[:, n, :], start=True, stop=True), reads=[rkst, rgv_], writes=[rbkv])
        P.op("act", lambda e: e.activation(out=flat(KV), in_=bkv[0:64, 0:GC * 128], func=AF.Copy), reads=[rbkv], writes=[rKV])
        yield

    def gl_seq(d, gi):
        n0 = gi * GC
        t0 = n0 * CH
        S2 = GS[d][gi % 2]
        (qi, rqi), (att, ratt), (KV, rKV), (al, ral) = S2["qi"], S2["att"], S2["KV"], S2["al"]
        (gv_, rgv_) = GV[d][gi % 3]
        (oacc, roacc) = acc_gl[d]
        for n in range(GC):
            def chunk(n=n):
                c = scur["gl"][d]
                (Sc, rSc), (Sn, rSn) = Sgl[d][c], Sgl[d][1 - c]
                scur["gl"][d] = 1 - c
                P.op("pe", lambda e: e.matmul(oacc[:, cs(n)], lhsT=Sc[:], rhs=qi[:, cs(n)], start=True, stop=False), reads=[rSc, rqi], writes=[roacc])
                P.op("pe", lambda e: e.matmul(oacc[:, cs(n)], lhsT=gv_[:, n, :], rhs=att[:, n, :], start=False, stop=True), reads=[rgv_, ratt], writes=[roacc])
                P.op("pool", lambda e: e.tensor_scalar(out=Sn[:], in0=Sc[:], scalar1=al[:, n:n + 1], scalar2=None, op0=ALU.mult), reads=[rSc, ral], writes=[rSn])
                P.op("pool", lambda e: e.tensor_tensor(out=Sn[:], in0=Sn[:], in1=KV[:, n, :], op=ALU.add), reads=[rSn, rKV], writes=[rSn])
            chunk()
            yield
        (o_, ro_) = ost[1][d]
        P.op("dve", lambda e: e.tensor_copy(out=o_[:], in_=oacc[:, 0:GW]), reads=[roacc], writes=[ro_])
        P.dma("sp", ogl[d][:, t0:t0 + GW], o_[:], reads=[ro_])
        yield

    def rr(gens):
        gens = list(gens)
        while gens:
            nxt = []
            for g in gens:
                try:
                    next(g)
                    nxt.append(g)
                except StopIteration:
                    pass
            gens = nxt

    loads(0)
    for s_ in range(NG + 1):
        if s_ + 1 < NG:
            loads(s_ + 1)
        gens = []
        if s_ >= 1:
            gens += [dn_seq(0, s_ - 1), dn_seq(1, s_ - 1), gl_seq(0, s_ - 1), gl_seq(1, s_ - 1)]
        if s_ < NG:
            gens += [dn_prep(0, s_), dn_prep(1, s_), gl_prep(0, s_), gl_prep(1, s_)]
        rr(gens)
    return K.done()


def shard_cols(lat, ctxa):
    latp = np.pad(lat, ((0, 0), (1, 1), (0, 0)))
    ctxp = np.pad(ctxa, ((0, 0), (1, 1), (0, 0))) if ctxa is not None else None
    out = []
    for c in range(NCORES):
        b, q = divmod(c, 4)
        parts = []
        if ctxp is not None:
            parts.append(ctxp[b, q * CQ:q * CQ + CQ + 2])
        parts.append(latp[b, q * LQ:q * LQ + LQ + 2])
        out.append(np.ascontiguousarray(np.concatenate(parts, 0).T))
    return out


def unshard_cols(shards, with_ctx):
    F = shards[0].shape[0]
    lat = np.empty((2, SEQ, F), shards[0].dtype)
    ctxa = np.empty((2, CTX, F), shards[0].dtype) if with_ctx else None
    for c in range(NCORES):
        b, q = divmod(c, 4)
        s = shards[c].T
        o = 0
        if with_ctx:
            ctxa[b, q * CQ:(q + 1) * CQ] = s[0:CQ]
            o = CQ
        lat[b, q * LQ:(q + 1) * LQ] = s[o:o + LQ]
    return lat, ctxa


def halo_mask(c):
    q = c % 4
    v = np.array([q != 0, q != 3, q != 0, q != 3], np.float32)
    return np.ascontiguousarray(np.broadcast_to(v, (128, 4)))


def fm(v, nt):
    return np.ascontiguousarray(v.reshape(nt, 128).T)


def mod_for_core(modT, c):
    b = c // 4
    out = np.empty((128, 2, 6, 8), np.float32)
    for s, r in ((0, b), (1, 2)):
        out[:, s] = modT[:, r].reshape(6, 8, 128).transpose(2, 0, 1)
    return out


_CACHE = {}


def get_nc(key, fn, *args):
    if key not in _CACHE:
        _CACHE[key] = fn(*args)
    return _CACHE[key]


def run(nc, in_maps):
    res = run_bass_kernel_spmd(nc, in_maps, core_ids=list(range(NCORES)))
    return res.results


def run_kmod(c, c_ctx, mod_w, mod_b):
    nc = get_nc("kmod", build_kmod)
    condT = np.ascontiguousarray(np.concatenate([c, c_ctx[None]], 0).T)
    in_maps = []
    for core in range(NCORES):
        l, half = divmod(core, 2)
        in_maps.append({"condT": condT, "W": np.ascontiguousarray(mod_w[l][:, half * 3072:(half + 1) * 3072]),
                        "bT": fm(mod_b[l][half * 3072:(half + 1) * 3072], 24)})
    res = run(nc, in_maps)
    modT = np.empty((4, 6144, 3), np.float32)
    for core in range(NCORES):
        l, half = divmod(core, 2)
        modT[l, half * 3072:(half + 1) * 3072] = res[core]["out"].transpose(1, 0, 2).reshape(3072, 3)
    return modT


def run_kc(rec, with_ctx, final, xl, xc, mix_in, w_out, modT, w_up, conv, w_down, fnorm, extra=None):
    nc = get_nc(("kc", rec, with_ctx, final), build_kc, rec, with_ctx, final)
    xs = shard_cols(xl, xc if with_ctx else None)
    in_maps = []
    cwv = np.ascontiguousarray(conv.reshape(3, NJ, 128).transpose(2, 0, 1))
    for c in range(NCORES):
        m = {"xT": xs[c], "w_out": w_out, "mod": mod_for_core(modT, c), "w_up": w_up, "cw": cwv, "w_down": w_down,
             "hm": halo_mask(c), "fnorm": fm(fnorm, 8)}
        for k, v in mix_in.items():
            m[k] = v[c]
        if extra:
            m.update(extra)
        in_maps.append(m)
    res = run(nc, in_maps)
    return unshard_cols([r["yT"] for r in res], with_ctx)


def shard_nohalo(lat, ctxa):
    out = []
    for c in range(NCORES):
        b, q = divmod(c, 4)
        out.append(np.ascontiguousarray(np.concatenate([ctxa[b, q * CQ:(q + 1) * CQ], lat[b, q * LQ:(q + 1) * LQ]], 0).T))
    return out


def rope_tables():
    t = np.arange(SEQ)
    inv = 10000.0 ** (-np.arange(32, dtype=np.float32) / 32).astype(np.float32)
    row = (t // 64).astype(np.float32)[:, None] * inv
    col = (t % 64).astype(np.float32)[:, None] * inv
    row = row.astype(np.float32)
    col = col.astype(np.float32)
    cos = np.concatenate([np.cos(row), np.cos(row), np.cos(col), np.cos(col)], 1).astype(np.float32)
    sin = np.concatenate([-np.sin(row), np.sin(row), -np.sin(col), np.sin(col)], 1).astype(np.float32)
    perm = np.zeros((128, 128), np.float32)
    for m in range(128):
        blk = m // 32
        partner = m + 32 if blk % 2 == 0 else m - 32
        perm[partner, m] = 1.0
    return [np.ascontiguousarray(cos[q * LQ:(q + 1) * LQ].T) for q in range(4)], \
           [np.ascontiguousarray(sin[q * LQ:(q + 1) * LQ].T) for q in range(4)], perm


def run_att(xl, xc, modT, w_qkv, q_norm, k_norm, need_ctx):
    nca = get_nc("ka_att", build_ka_att)
    ncb = get_nc(("kb_att", need_ctx), build_kb_att, need_ctx)
    cosq, sinq, perm = rope_tables()
    xs = shard_nohalo(xl, xc)
    gains = np.ascontiguousarray(np.stack([q_norm, k_norm], 1))
    res = run(nca, [{"xT": xs[c], "mod": mod_for_core(modT, c), "w_qkv": w_qkv, "gains": gains, "cosT": cosq[c % 4], "sinT": sinq[c % 4],
                     "perm": perm} for c in range(NCORES)])
    qkv = [r["qkvT"] for r in res]
    in_maps = []
    for b in range(2):
        cs = [qkv[4 * b + q] for q in range(4)]
        kv = np.concatenate([x_[1024:1536, 0:CQ] for x_ in cs] + [x_[1024:1536, CQ:] for x_ in cs], 1)
        kT = np.ascontiguousarray(kv[0:256])
        v_tm = np.ascontiguousarray(kv[256:512].T.reshape(NKT, 128, 256).transpose(1, 0, 2))
        for q in range(4):
            in_maps.append({"qT": np.ascontiguousarray(cs[q][0:1024]), "kT": kT, "v_tm": v_tm})
    res = run(ncb, in_maps)
    return unshard_cols([r["oT"] for r in res], True)


def rec_consts():
    i = np.arange(64)
    U = (i[:, None] <= i[None, :]).astype(np.float32)
    Us = (i[:, None] < i[None, :]).astype(np.float32)
    Ls = (i[:, None] > i[None, :]).astype(np.float32)
    NegU = (U - 1.0) * 1.0e4
    NegLs = (Ls - 1.0) * 1.0e4
    Ls = -Ls
    I = np.eye(64, dtype=np.float32)
    return np.ascontiguousarray(np.stack([U, Us, Ls, NegU, NegLs, I], 1).astype(np.float32))


def run_rec(xl, xc, modT, w_in, conv, a_log, dt_bias, w2, b2):
    nca = get_nc("ka_rec", build_ka_rec)
    ncb = get_nc("kb_rec", build_kb_rec)
    xs = shard_cols(xl, xc)
    cwv = np.ascontiguousarray(conv.reshape(3, 12, 128).transpose(2, 0, 1))
    dnp = np.ascontiguousarray(np.stack([dt_bias.reshape(8), a_log.reshape(8)], 1))
    w2v = np.ascontiguousarray(w2.transpose(1, 0, 2))
    b2T = np.ascontiguousarray(b2.reshape(2, 2, 128).transpose(2, 0, 1))
    res = run(nca, [{"xT": xs[c], "mod": mod_for_core(modT, c), "w_in": w_in, "cw": cwv, "hm": halo_mask(c), "dnp": dnp, "w2": w2v, "b2T": b2T}
                    for c in range(NCORES)])
    def full(name, b):
        cs_ = [res[4 * b + q][name] for q in range(4)]
        return np.concatenate([x_[:, 0:CQ] for x_ in cs_] + [x_[:, CQ:] for x_ in cs_], 1)
    flip = np.concatenate([np.arange(CTX - 1, -1, -1), CTX + np.arange(SEQ - 1, -1, -1)])
    idx = [np.arange(NKEY), flip]
    cmv = rec_consts()

    def tm(a):
        return np.ascontiguousarray(a.T.reshape(NCHUNK, CH, a.shape[0]).transpose(1, 0, 2))

    def col(v):
        return np.ascontiguousarray(v.reshape(NCHUNK, CH).T)

    in_maps = []
    feats = []
    for b in range(2):
        F_ = full("featT", b)
        G_ = full("gT", b)
        L_ = full("laT", b)
        feats.append(F_)
        for hh in range(4):
            m = {"cm": cmv}
            for d in range(2):
                ix = idx[d]
                q_ = F_[hh * 128:(hh + 1) * 128][:, ix]
                k_ = F_[512 + hh * 128:512 + (hh + 1) * 128][:, ix]
                v_ = F_[1024 + hh * 128:1024 + (hh + 1) * 128][:, ix]
                m["dq%d" % d] = np.ascontiguousarray(q_)
                m["dk%d" % d] = np.ascontiguousarray(k_)
                m["dktm%d" % d] = tm(k_)
                m["dvtm%d" % d] = tm(v_)
                m["dg%d" % d] = col(G_[d * 4 + hh][ix])
                m["dbt%d" % d] = col(G_[8 + d * 4 + hh][ix])
                gq_ = F_[2048 + hh * 64:2048 + (hh + 1) * 64][:, ix]
                gk_ = F_[2304 + hh * 64:2304 + (hh + 1) * 64][:, ix]
                gv_ = F_[2560 + hh * 128:2560 + (hh + 1) * 128][:, ix]
                la_ = L_[d * 256 + hh * 64:d * 256 + (hh + 1) * 64][:, ix]
                m["gq%d" % d] = np.ascontiguousarray(gq_)
                m["gk%d" % d] = np.ascontiguousarray(gk_)
                m["gktm%d" % d] = tm(gk_)
                m["gvtm%d" % d] = tm(gv_)
                m["gla%d" % d] = tm(la_)
            in_maps.append(m)
    res2 = run(ncb, in_maps)
    of = np.empty((2, NKEY, 1024), np.float32)
    ob = np.empty((2, NKEY, 1024), np.float32)
    zr = np.empty((2, NKEY, 1024), np.float32)
    for b in range(2):
        for hh in range(4):
            r = res2[4 * b + hh]
            of[b, :, hh * 128:(hh + 1) * 128] = r["odn0"].T
            ob[b, :, hh * 128:(hh + 1) * 128] = r["odn1"].T[flip]
            of[b, :, 512 + hh * 128:512 + (hh + 1) * 128] = r["ogl0"].T
            ob[b, :, 512 + hh * 128:512 + (hh + 1) * 128] = r["ogl1"].T[flip]
        zr[b, :, 0:512] = feats[b][1536:2048].T
        zr[b, :, 512:1024] = feats[b][3072:3584].T
    return of, ob, zr


def kernel(x, c, ctx, c_ctx, mod_w, mod_b, rec_w_in, rec_conv, dn_a_log, dn_dt_bias, dn_norm, gla_w2, gla_b2, gla_norm, rec_w_out,
           att_w_qkv, att_q_norm, att_k_norm, att_w_out, ffn_w_up, ffn_conv, ffn_w_down, final_norm):
    f = lambda a: np.ascontiguousarray(np.asarray(a, dtype=np.float32))
    (x, c, ctx, c_ctx, mod_w, mod_b, rec_w_in, rec_conv, dn_a_log, dn_dt_bias, dn_norm, gla_w2, gla_b2, gla_norm, rec_w_out,
     att_w_qkv, att_q_norm, att_k_norm, att_w_out, ffn_w_up, ffn_conv, ffn_w_down, final_norm) = map(f, (
        x, c, ctx, c_ctx, mod_w, mod_b, rec_w_in, rec_conv, dn_a_log, dn_dt_bias, dn_norm, gla_w2, gla_b2, gla_norm, rec_w_out,
        att_w_qkv, att_q_norm, att_k_norm, att_w_out, ffn_w_up, ffn_conv, ffn_w_down, final_norm))
    modT = run_kmod(c, c_ctx, mod_w, mod_b)
    xl, xc = x, ctx
    for i in range(4):
        last = i == 3
        if i % 2 == 0:
            e = i // 2
            of, ob, zr = run_rec(xl, xc, modT[i], rec_w_in[e], rec_conv[e], dn_a_log[e], dn_dt_bias[e], gla_w2[e], gla_b2[e])
            mix_in = {"ofT": shard_cols(of[:, CTX:], of[:, :CTX]), "obT": shard_cols(ob[:, CTX:], ob[:, :CTX]),
                      "zrT": shard_cols(zr[:, CTX:], zr[:, :CTX])}
            nrm = np.ascontiguousarray(np.stack([dn_norm[e]] * 4 + [gla_norm[e]] * 4, 1))
            xl, xc = run_kc(True, True, False, xl, xc, mix_in, rec_w_out[e], modT[i], ffn_w_up[i], ffn_conv[i], ffn_w_down[i], final_norm,
                            {"nrm": nrm})
        else:
            o = i // 2
            Ol, Oc = run_att(xl, xc, modT[i], att_w_qkv[o], att_q_norm[o], att_k_norm[o], not last)
            with_ctx = not last
            mix_in = {"oT": shard_cols(Ol, Oc if with_ctx else None)}
            xl, xc2 = run_kc(False, with_ctx, last, xl, xc, mix_in, att_w_out[o], modT[i], ffn_w_up[i], ffn_conv[i], ffn_w_down[i], final_norm)
            if with_ctx:
                xc = xc2
    return xl
```

```python
import numpy as np
from contextlib import ExitStack
import concourse.bass as bass
import concourse.mybir as mybir
from concourse.bass_utils import run_bass_kernel_spmd

F32 = mybir.dt.float32
BF16 = mybir.dt.bfloat16
ALU = mybir.AluOpType
AF = mybir.ActivationFunctionType

NCORES = 8
D = 1024
SEQ = 8192
CTX = 256
LQ = SEQ // 4
CQ = CTX // 4
DFF = 2816
NJ = DFF // 128
EPS = 1e-6


class Res:
    __slots__ = ("w", "r", "x")

    def __init__(self, x=False):
        self.w = None
        self.r = {}
        self.x = x


class Prog:
    ENG = ("pe", "act", "dve", "pool", "sp")

    def __init__(self, nc, stack, ndma=24):
        self.nc = nc
        self.ops = {e: [] for e in self.ENG}
        self.cnt = {e: 0 for e in self.ENG}
        self.esem = {e: stack.enter_context(nc.semaphore("s_" + e)) for e in self.ENG}
        self.dsem = [stack.enter_context(nc.semaphore("d_%d" % i)) for i in range(ndma)]
        self.dcnt = [0] * ndma
        self.dnext = 0
        self.seen = {e: {} for e in self.ENG}

    def _sem(self, k):
        if isinstance(k, tuple):
            return self.dsem[k[1]], 16
        return self.esem[k], 1

    def _waits(self, eng, reads, writes, extra=(), is_dma=False):
        deps = {}

        def add(k, v):
            if deps.get(k, 0) < v:
                deps[k] = v

        for r in reads:
            if r.w is not None and not (r.w[0] == eng and eng == "pe"):
                add(*r.w)
            if r.x:
                for k, v in r.r.items():
                    if k != eng:
                        add(k, v)
        for w in writes:
            if w.w is not None and (is_dma or w.w[0] != eng):
                add(*w.w)
            for k, v in w.r.items():
                if is_dma or k != eng:
                    add(k, v)
        for k, v in extra:
            add(k, v)
        waits = []
        seen = self.seen[eng]
        for k, v in deps.items():
            if seen.get(k, 0) >= v:
                continue
            seen[k] = v
            sem, mul = self._sem(k)
            waits.append((sem, v * mul))
        return waits

    def op(self, eng, fn, reads=(), writes=()):
        waits = self._waits(eng, reads, writes)
        self.cnt[eng] += 1
        seq = self.cnt[eng]
        self.ops[eng].append((waits, fn, (self.esem[eng], 1)))
        for r in reads:
            r.r[eng] = seq
        for w in writes:
            w.w = (eng, seq)
            w.r = {}

    def dma(self, q, out, in_, reads=(), writes=(), slow=False):
        i = self.dnext
        self.dnext = (self.dnext + 1) % len(self.dsem)
        key = ("d", i)
        extra = ((key, self.dcnt[i]),) if self.dcnt[i] else ()
        waits = self._waits(q, reads, writes, extra, is_dma=True)
        self.dcnt[i] += 1
        seq = self.dcnt[i]
        if slow:
            self.ops[q].append((waits, lambda e: e.dma_start(out=out, in_=in_, allow_slow_non_contiguous=True), (self.dsem[i], 16)))
        else:
            self.ops[q].append((waits, lambda e: e.dma_start(out=out, in_=in_), (self.dsem[i], 16)))
        for r in reads:
            r.r[key] = seq
        for w in writes:
            w.w = (key, seq)
            w.r = {}

    def barrier(self):
        for e in self.ENG:
            waits = []
            seen = self.seen[e]
            for k in self.ENG:
                if k != e and k != "sp" and self.cnt[k] > seen.get(k, 0):
                    seen[k] = self.cnt[k]
                    waits.append((self.esem[k], self.cnt[k]))
            for i, c in enumerate(self.dcnt):
                k = ("d", i)
                if c > seen.get(k, 0):
                    seen[k] = c
                    waits.append((self.dsem[i], 16 * c))
            self.ops[e].append((waits, None, None))

    def finish(self):
        waits = [(self.dsem[i], 16 * c) for i, c in enumerate(self.dcnt) if c]
        self.ops["sp"].append((waits, None, None))

    def emit(self):
        with self.nc.Block() as block:
            def mk(e):
                def run(eng):
                    for waits, fn, inc in self.ops[e]:
                        for sem, val in waits:
                            eng.wait_ge(sem, val)
                        if fn is not None:
                            r_ = fn(eng)
                            if inc is not None:
                                r_.then_inc(*inc)
                return run
            block.tensor(mk("pe"))
            block.scalar(mk("act"))
            block.vector(mk("dve"))
            block.gpsimd(mk("pool"))
            block.sync(mk("sp"))


class KB:
    def __init__(self, ext=None):
        self.nc = bass.Bass("TRN2", target_bir_lowering=False, num_devices=NCORES)
        self.st = ExitStack()
        self.cur = self.st
        self.P = Prog(self.nc, self.st)
        self.banks = [self.st.enter_context(self.nc.psum_tensor("bank%d" % i, [128, 512], F32)) for i in range(8)]
        self.rbank = [Res(True) for _ in range(8)]
        self.bi = 0
        self.n = 0
        self.ext = ext or {}

    def din(self, name, shape, dt=F32):
        return self.nc.dram_tensor(name, list(shape), dt, kind="ExternalInput").ap()

    def dout(self, name, shape, dt=F32):
        return self.nc.dram_tensor(name, list(shape), dt, kind="ExternalOutput").ap()

    def dram(self, name, shape, dt=F32):
        kind = {"in": "ExternalInput", "out": "ExternalOutput"}.get(self.ext.get(name), "Internal")
        return self.nc.dram_tensor(name, list(shape), dt, kind=kind).ap()

    def sb(self, shape, dt=F32, name=None):
        self.n += 1
        t = self.cur.enter_context(self.nc.sbuf_tensor("%s_%d" % (name or "t", self.n), list(shape), dt))
        return t, Res()

    def gsb(self, shape, dt=F32, name=None):
        self.n += 1
        t = self.st.enter_context(self.nc.sbuf_tensor("%s_%d" % (name or "g", self.n), list(shape), dt))
        return t, Res()

    def begin(self):
        self.cur = ExitStack()
        self.bi = 0

    def end(self):
        self.P.barrier()
        self.cur.close()
        self.cur = self.st

    def bank(self):
        i = self.bi
        self.bi = (self.bi + 1) % 8
        return self.banks[i], self.rbank[i]

    def done(self):
        self.P.finish()
        self.P.emit()
        self.st.close()
        return self.nc


def split(a, b, maxn=512):
    n = b - a
    k = (n + maxn - 1) // maxn
    base, rem = divmod(n, k)
    out = []
    s = a
    for i in range(k):
        e = s + base + (1 if i < rem else 0)
        out.append((s, e))
        s = e
    return out


class GS:
    pass


QN = {0: LQ, 1: CQ}
PADC = 16


def halo_ap(X, sid, q):
    return X[sid][:, PADC - 1 + q * QN[sid]:PADC + (q + 1) * QN[sid] + 1]


def inte_ap(X, sid, q):
    return X[sid][:, PADC + q * QN[sid]:PADC + (q + 1) * QN[sid]]


class WLoader:
    def __init__(self, K, shape, n=2, name="stg"):
        self.K = K
        self.stg = [K.sb(shape, F32, "%s%d" % (name, i)) for i in range(n)]
        self.i = 0

    def load(self, dst_ap, rdst, src_ap, view):
        P = self.K.P
        (t, rt) = self.stg[self.i % len(self.stg)]
        q = "sp" if self.i % 2 == 0 else "act"
        eng = "pool" if self.i % 2 == 0 else "act"
        self.i += 1
        sv = view(t)
        P.dma(q, sv, src_ap, writes=[rt])
        if eng == "act":
            P.op("act", lambda e: e.activation(out=dst_ap, in_=sv, func=AF.Copy), reads=[rt], writes=[rdst])
        else:
            P.op("pool", lambda e: e.tensor_copy(out=dst_ap, in_=sv), reads=[rt], writes=[rdst])


def consts(K):
    P = K.P
    ones_b, r1 = K.gsb([128, 128], BF16, "ones_b")
    ones_f, r2 = K.gsb([128, 128], F32, "ones_f")
    P.op("pool", lambda e: e.memset(ones_b[:], 1.0), writes=[r1])
    P.op("pool", lambda e: e.memset(ones_f[:], 1.0), writes=[r2])
    return (ones_b, r1), (ones_f, r2)


def rms_modulate(K, x, rx, a, b, ones_b, r_ones, s1, sh, rmod, h, rh, hoff, scr):
    P = K.P
    (sq, rsq), (t, rt), (rstd, rrs) = scr
    n = b - a
    P.op("act", lambda e: e.activation(out=sq[:, :, 0:n], in_=x[:, :, a:b], func=AF.Square), reads=[rx], writes=[rsq])
    bk, rb = K.bank()
    for k in range(8):
        P.op("pe", lambda e, k=k: e.matmul(bk[:, 0:n], lhsT=ones_b[:], rhs=sq[:, k, 0:n], start=(k == 0), stop=(k == 7)),
             reads=[r_ones, rsq], writes=[rb])
    P.op("act", lambda e: e.activation(out=rstd[:, 0:n], in_=bk[:, 0:n], func=AF.Sqrt, scale=1.0 / D, bias=EPS),
         reads=[rb], writes=[rrs])
    P.op("dve", lambda e: e.reciprocal(out=rstd[:, 0:n], in_=rstd[:, 0:n]), reads=[rrs], writes=[rrs])
    P.op("dve", lambda e: e.tensor_tensor(out=t[:, :, 0:n], in0=x[:, :, a:b],
                                          in1=rstd[:, 0:n].unsqueeze(1).to_broadcast([128, 8, n]), op=ALU.mult),
         reads=[rx, rrs], writes=[rt])
    for k in range(8):
        eng = "pool" if k % 2 else "dve"
        P.op(eng, lambda e, k=k: e.tensor_scalar(out=h[:, k, hoff:hoff + n], in0=t[:, k, 0:n], scalar1=s1(k), scalar2=sh(k),
                                                 op0=ALU.mult, op1=ALU.add),
             reads=[rt, rmod], writes=[rh])


def rms_scratch(K):
    return (K.sb([128, 8, 512], BF16, "rs_sq"), K.sb([128, 8, 512], F32, "rs_t"), K.sb([128, 512], F32, "rs_rstd"))


def emit_mod(K, G):
    P = K.P
    I = G.I
    K.begin()
    cs, rcs = K.sb([128, 8, 2], F32)
    sg, rsg = K.sb([128, 8, 2], F32)
    bs, rbs = K.sb([128, 4, 48], F32)
    P.dma("sp", cs[:], I["condT"].rearrange("(k p) r -> p k r", p=128), writes=[rcs])
    P.dma("sp", bs[:], I["mod_bT"], writes=[rbs])
    P.op("act", lambda e: e.activation(out=sg[:], in_=cs[:], func=AF.Silu), reads=[rcs], writes=[rsg])
    wt = [K.sb([128, 8, 512], F32, "wt%d" % i) for i in range(3)]
    it = 0
    for l in range(4):
        for g in range(12):
            w, rw = wt[it % 3]
            it += 1
            P.dma("sp" if it % 2 else "act", w[:], I["mod_w"][l][:, g * 512:(g + 1) * 512].rearrange("(k p) c -> p k c", p=128), writes=[rw])
            for jj in range(4):
                j = g * 4 + jj
                bk, rb = K.bank()
                for k in range(8):
                    P.op("pe", lambda e, k=k, jj=jj, w=w, bk=bk: e.matmul(bk[:, 0:2], lhsT=w[:, k, jj * 128:(jj + 1) * 128], rhs=sg[:, k, :],
                                                                         start=(k == 0), stop=(k == 7)),
                         reads=[rw, rsg], writes=[rb])
                P.op("dve", lambda e, j=j, l=l, bk=bk: e.tensor_scalar(out=G.mods[:, l, :, j // 8, j % 8], in0=bk[:, 0:2], scalar1=bs[:, l, j:j + 1],
                                                                       scalar2=None, op0=ALU.add),
                     reads=[rb, rbs], writes=[G.rmod])
    P.op("pool", lambda e: e.tensor_scalar(out=G.mod1[:], in0=G.mods[:], scalar1=1.0, scalar2=None, op0=ALU.add), reads=[G.rmod], writes=[G.rmod1])
    K.end()


def layout(with_ctx):
    if with_ctx:
        segs = [(1, 0, CQ + 2), (0, CQ + 2, CQ + 2 + LQ + 2)]
    else:
        segs = [(0, 0, LQ + 2)]
    return segs, segs[-1][2]


def emit_kc(K, G, l, q, rec, with_ctx, final):
    P = K.P
    I = G.I
    S = G.S
    K.begin()
    segs, NT = layout(with_ctx)
    Xin, Xout = S["X"][l % 2], S["X"][(l + 1) % 2]
    MIX = S["MIX"]
    w_out = I["rec_w_out"][l // 2] if rec else I["att_w_out"][l // 2]
    w_up = I["ffn_w_up"][l]
    cw = I["ffn_cw"][l]
    w_down = I["ffn_w_down"][l]
    (ones_b, r_ob), (ones_f, r_of) = (G.ones_b, G.r_ob), (G.ones_f, G.r_of)
    mods, rmod, mod1, rmod1 = G.mods[:, l], G.rmod, G.mod1[:, l], G.rmod1
    x, rx = K.sb([128, 8, NT], F32, "x")
    cws, rcw = K.sb([128, 3, NJ], F32, "cws")
    fns, rfn = K.sb([128, 8], F32, "fns")
    for (sid, s0, s1) in segs:
        P.dma("sp", x[:, :, s0:s1], halo_ap(Xin, sid, q).rearrange("(k p) n -> p k n", p=128), writes=[rx])
    P.dma("sp", cws[:], cw, writes=[rcw])
    P.dma("sp", fns[:], I["fnormT"], writes=[rfn])

    with ExitStack() as ph:
        def sbp(shape, dt, name):
            K.n += 1
            return ph.enter_context(K.nc.sbuf_tensor("%s_%d" % (name, K.n), list(shape), dt)), Res()
        MT, rMT = sbp([128, 8, NT], BF16, "MT")
        if rec:
            nrs, rnr = sbp([128, 8], F32, "nrs")
            P.dma("sp", nrs[:], I["nrm"][l // 2], writes=[rnr])
            bufs = [[sbp([128, NT], F32, "mg%d_%d" % (i, q)) for q in range(3)] for i in range(2)]
            sq, rsq = sbp([128, NT], BF16, "mg_sq")
            rs, rrs = sbp([128, NT], F32, "mg_rs")
            for kt in range(8):
                (f, rf), (b_, rb_), (z, rz) = bufs[kt % 2]
                for (sid, s0, s1) in segs:
                    P.dma("sp", f[:, s0:s1], halo_ap(MIX[0], sid, q)[kt * 128:(kt + 1) * 128, :], writes=[rf])
                    P.dma("act", b_[:, s0:s1], halo_ap(MIX[1], sid, q)[kt * 128:(kt + 1) * 128, :], writes=[rb_])
                    P.dma("sp", z[:, s0:s1], halo_ap(MIX[2], sid, q)[kt * 128:(kt + 1) * 128, :], writes=[rz])
                P.op("pool", lambda e, f=f, b_=b_: e.tensor_tensor(out=f[:], in0=f[:], in1=b_[:], op=ALU.add), reads=[rf, rb_], writes=[rf])
                P.op("act", lambda e, f=f: e.activation(out=sq[:], in_=f[:], func=AF.Square), reads=[rf], writes=[rsq])
                for (a, b) in split(0, NT):
                    bk, rbk = K.bank()
                    P.op("pe", lambda e, a=a, b=b, bk=bk: e.matmul(bk[:, 0:b - a], lhsT=ones_b[:], rhs=sq[:, a:b], start=True, stop=True),
                         reads=[r_ob, rsq], writes=[rbk])
                    P.op("act", lambda e, a=a, b=b, bk=bk: e.activation(out=rs[:, a:b], in_=bk[:, 0:b - a], func=AF.Sqrt, scale=1.0 / 128, bias=EPS),
                         reads=[rbk], writes=[rrs])
                P.op("dve", lambda e: e.reciprocal(out=rs[:], in_=rs[:]), reads=[rrs], writes=[rrs])
                P.op("dve", lambda e, f=f: e.tensor_tensor(out=f[:], in0=f[:], in1=rs[:], op=ALU.mult), reads=[rf, rrs], writes=[rf])
                P.op("dve", lambda e, f=f, z=z, kt=kt: e.scalar_tensor_tensor(out=MT[:, kt, :], in0=f[:], scalar=nrs[:, kt:kt + 1], in1=z[:],
                                                                              op0=ALU.mult, op1=ALU.mult),
                     reads=[rf, rz, rnr], writes=[rMT])
        else:
            abuf = [sbp([128, NT], F32, "mga%d" % i) for i in range(3)]
            for kt in range(8):
                (f, rf) = abuf[kt % 3]
                for (sid, s0, s1) in segs:
                    P.dma("sp" if kt % 2 else "act", f[:, s0:s1], halo_ap(MIX[0], sid, q)[kt * 128:(kt + 1) * 128, :], writes=[rf])
                if kt % 2:
                    P.op("act", lambda e, f=f, kt=kt: e.activation(out=MT[:, kt, :], in_=f[:], func=AF.Copy), reads=[rf], writes=[rMT])
                else:
                    P.op("pool", lambda e, f=f, kt=kt: e.tensor_copy(out=MT[:, kt, :], in_=f[:]), reads=[rf], writes=[rMT])
        wo = [sbp([128, 8, 128], BF16, "wo%d" % i) for i in range(2)]
        keep = K.cur
        K.cur = ph
        wl1 = WLoader(K, [128, 8, 128], 2, "wos")
        K.cur = keep
        for m in range(8):
            w, rw = wo[m % 2]
            wl1.load(w[:], rw, w_out[:, m * 128:(m + 1) * 128].rearrange("(k p) c -> p k c", p=128), lambda t: t[:])
            for (sid, s0, s1) in segs:
                for (a, b) in split(s0, s1):
                    bk, rbk = K.bank()
                    for k in range(8):
                        P.op("pe", lambda e, k=k, a=a, b=b, w=w, bk=bk: e.matmul(bk[:, 0:b - a], lhsT=w[:, k, :], rhs=MT[:, k, a:b],
                                                                                start=(k == 0), stop=(k == 7)),
                             reads=[rw, rMT], writes=[rbk])
                    P.op("dve", lambda e, a=a, b=b, m=m, sid=sid, bk=bk: e.scalar_tensor_tensor(
                        out=x[:, m, a:b], in0=bk[:, 0:b - a], scalar=mods[:, sid, 2, m:m + 1], in1=x[:, m, a:b], op0=ALU.mult, op1=ALU.add),
                        reads=[rbk, rmod, rx], writes=[rx])
        P.barrier()

    (lsid, l0, l1) = segs[-1]
    half = LQ // 2
    passes = [[(lsid, l0, l0 + half + 2)], [(lsid, l0 + half, l1)]]
    if with_ctx:
        passes[0].insert(0, segs[0])
    PL = max(sum(p[2] - p[1] for p in ps) for ps in passes)
    h, rh = K.sb([128, 8, PL], BF16, "h")
    aT, raT = K.sb([128, NJ, PL], BF16, "aT")
    ecnt = [0]

    def evac_eng():
        ecnt[0] += 1
        return "act" if ecnt[0] % 2 else "dve"

    hc = l0 + half
    xsave, rxs = K.sb([128, 8, 1], F32, "xsave")
    xupd, rxu = K.sb([128, 8, 1], F32, "xupd")
    P.op("pool", lambda e: e.tensor_copy(out=xsave[:], in_=x[:, :, hc:hc + 1]), reads=[rx], writes=[rxs])
    for pi, ps in enumerate(passes):
        offs = []
        o = 0
        for (sid, a, b) in ps:
            offs.append(o)
            o += b - a
        outer = K.cur
        K.cur = ExitStack()
        scr = rms_scratch(K)
        if pi == 1:
            P.op("pool", lambda e: e.tensor_copy(out=xupd[:], in_=x[:, :, hc:hc + 1]), reads=[rx], writes=[rxu])
            P.op("pool", lambda e: e.tensor_copy(out=x[:, :, hc:hc + 1], in_=xsave[:]), reads=[rxs], writes=[rx])
        for (sid, a, b), o in zip(ps, offs):
            for (ba, bb) in split(a, b):
                rms_modulate(K, x, rx, ba, bb, ones_b, r_ob,
                             lambda k, sid=sid: mod1[:, sid, 4, k:k + 1], lambda k, sid=sid: mods[:, sid, 3, k:k + 1],
                             rmod1, h, rh, o + ba - a, scr)
        if pi == 1:
            P.op("pool", lambda e: e.tensor_copy(out=x[:, :, hc:hc + 1], in_=xupd[:]), reads=[rxu], writes=[rx])
        P.barrier()
        K.cur.close()
        K.cur = ExitStack()
        gb = [K.sb([128, PL], F32, "gb%d" % i) for i in range(2)]
        cb = [K.sb([128, PL], F32, "cb%d" % i) for i in range(2)]
        vb = [K.sb([128, PL], BF16, "vb%d" % i) for i in range(2)]
        wu = [K.sb([128, 8, 256], BF16, "wu%d" % i) for i in range(3)]
        wd = [K.sb([128, NJ, 128], BF16, "wd%d" % i) for i in range(2)]
        HJ = NJ // 2
        wl = WLoader(K, [128, HJ * 128], 3, "wst")

        def load_wu(j):
            w, rw = wu[j % 3]
            wl.load(w[:, :, 0:128], rw, w_up[:, j * 128:(j + 1) * 128].rearrange("(k p) c -> p k c", p=128),
                    lambda t: t[:, 0:1024].rearrange("p (k c) -> p k c", k=8))
            wl.load(w[:, :, 128:256], rw, w_up[:, DFF + j * 128:DFF + (j + 1) * 128].rearrange("(k p) c -> p k c", p=128),
                    lambda t: t[:, 0:1024].rearrange("p (k c) -> p k c", k=8))

        def gate_stage(j):
            w, rw = wu[j % 3]
            g, rg = gb[j % 2]
            for (sid, a, b), o in zip(ps, offs):
                n = b - a
                for (ba, bb) in split(0, n):
                    bk, rbk = K.bank()
                    for k in range(8):
                        P.op("pe", lambda e, k=k, ba=ba, bb=bb, o=o, w=w, bk=bk: e.matmul(bk[:, 0:bb - ba], lhsT=w[:, k, 0:128], rhs=h[:, k, o + ba:o + bb],
                                                                                         start=(k == 0), stop=(k == 7)),
                             reads=[rw, rh], writes=[rbk])
                    P.op("act", lambda e, ba=ba, bb=bb, o=o, g=g, bk=bk: e.activation(out=g[:, o + ba:o + bb], in_=bk[:, 0:bb - ba], func=AF.Copy),
                         reads=[rbk], writes=[rg])

        def rest_stage(j):
            w, rw = wu[j % 3]
            g, rg = gb[j % 2]
            c, rc = cb[j % 2]
            v_, rv_ = vb[j % 2]
            for (sid, a, b), o in zip(ps, offs):
                n = b - a
                (bs0, bs1) = [(q_[1], q_[2]) for q_ in segs if q_[0] == sid][0]
                for (col, cond) in ((o, a == bs0 and q == 0), (o + n - 1, b == bs1 and q == 3)):
                    if cond:
                        P.op("pool", lambda e, col=col: e.memset(g[:, col:col + 1], 0.0), reads=[rg], writes=[rg])
                P.op("dve", lambda e, o=o, n=n: e.tensor_scalar(out=c[:, o + 1:o + n - 1], in0=g[:, o:o + n - 2],
                                                               scalar1=cws[:, 0, j:j + 1], scalar2=None, op0=ALU.mult),
                     reads=[rg, rcw], writes=[rc])
                P.op("dve", lambda e, o=o, n=n: e.scalar_tensor_tensor(out=c[:, o + 1:o + n - 1], in0=g[:, o + 1:o + n - 1],
                                                                      scalar=cws[:, 1, j:j + 1], in1=c[:, o + 1:o + n - 1],
                                                                      op0=ALU.mult, op1=ALU.add),
                     reads=[rg, rc, rcw], writes=[rc])
                P.op("dve", lambda e, o=o, n=n: e.scalar_tensor_tensor(out=c[:, o + 1:o + n - 1], in0=g[:, o + 2:o + n],
                                                                      scalar=cws[:, 2, j:j + 1], in1=c[:, o + 1:o + n - 1],
                                                                      op0=ALU.mult, op1=ALU.add),
                     reads=[rg, rc, rcw], writes=[rc])
                P.op("act", lambda e, o=o, n=n: e.activation(out=c[:, o + 1:o + n - 1], in_=c[:, o + 1:o + n - 1], func=AF.Silu),
                     reads=[rc], writes=[rc])
                for (ba, bb) in split(1, n - 1):
                    bk, rbk = K.bank()
                    for k in range(8):
                        P.op("pe", lambda e, k=k, ba=ba, bb=bb, o=o, bk=bk: e.matmul(bk[:, 0:bb - ba], lhsT=w[:, k, 128:256], rhs=h[:, k, o + ba:o + bb],
                                                                                    start=(k == 0), stop=(k == 7)),
                             reads=[rw, rh], writes=[rbk])
                    P.op("act", lambda e, ba=ba, bb=bb, o=o, bk=bk: e.activation(out=v_[:, o + ba:o + bb], in_=bk[:, 0:bb - ba], func=AF.Copy),
                         reads=[rbk], writes=[rv_])
                P.op("dve", lambda e, o=o, n=n: e.tensor_tensor(out=aT[:, j, o + 1:o + n - 1], in0=c[:, o + 1:o + n - 1], in1=v_[:, o + 1:o + n - 1], op=ALU.mult),
                     reads=[rc, rv_], writes=[raT])

        load_wu(0)
        load_wu(1)
        gate_stage(0)
        for j in range(NJ):
            if j + 2 < NJ:
                load_wu(j + 2)
            if j + 1 < NJ:
                gate_stage(j + 1)
            rest_stage(j)
        for m in range(8):
            w, rw = wd[m % 2]
            for hf in range(2):
                wl.load(w[:, hf * HJ:(hf + 1) * HJ, :], rw, w_down[hf * HJ * 128:(hf + 1) * HJ * 128, m * 128:(m + 1) * 128].rearrange("(j p) c -> p j c", p=128),
                        lambda t: t[:].rearrange("p (j c) -> p j c", j=HJ))
            for (sid, a, b), o in zip(ps, offs):
                n = b - a
                for (ba, bb) in split(1, n - 1):
                    bk, rbk = K.bank()
                    for j in range(NJ):
                        P.op("pe", lambda e, j=j, ba=ba, bb=bb, o=o, w=w, bk=bk: e.matmul(bk[:, 0:bb - ba], lhsT=w[:, j, :], rhs=aT[:, j, o + ba:o + bb],
                                                                                         start=(j == 0), stop=(j == NJ - 1)),
                             reads=[rw, raT], writes=[rbk])
                    P.op("dve", lambda e, ba=ba, bb=bb, a=a, m=m, sid=sid, bk=bk: e.scalar_tensor_tensor(
                        out=x[:, m, a + ba:a + bb], in0=bk[:, 0:bb - ba], scalar=mods[:, sid, 5, m:m + 1], in1=x[:, m, a + ba:a + bb],
                        op0=ALU.mult, op1=ALU.add),
                        reads=[rbk, rmod, rx], writes=[rx])
        P.barrier()
        K.cur.close()
        K.cur = outer

    for (sid, s0, s1) in segs:
        if final:
            yv = I["yT"].rearrange("(k p) n -> p k n", p=128)
            (sq, rsq), (t, rt), (rstd, rrs) = rms_scratch(K)
            for (a, b) in split(s0 + 1, s1 - 1):
                n = b - a

                def fblk(a=a, b=b, n=n):
                    P.op("act", lambda e: e.activation(out=sq[:, :, 0:n], in_=x[:, :, a:b], func=AF.Square), reads=[rx], writes=[rsq])
                    bk, rbk = K.bank()
                    for k in range(8):
                        P.op("pe", lambda e, k=k: e.matmul(bk[:, 0:n], lhsT=ones_b[:], rhs=sq[:, k, 0:n], start=(k == 0), stop=(k == 7)),
                             reads=[r_ob, rsq], writes=[rbk])
                    P.op("act", lambda e: e.activation(out=rstd[:, 0:n], in_=bk[:, 0:n], func=AF.Sqrt, scale=1.0 / D, bias=EPS), reads=[rbk], writes=[rrs])
                    P.op("dve", lambda e: e.reciprocal(out=rstd[:, 0:n], in_=rstd[:, 0:n]), reads=[rrs], writes=[rrs])
                    P.op("dve", lambda e: e.tensor_tensor(out=t[:, :, 0:n], in0=x[:, :, a:b],
                                                          in1=rstd[:, 0:n].unsqueeze(1).to_broadcast([128, 8, n]), op=ALU.mult),
                         reads=[rx, rrs], writes=[rt])
                    P.op("pool", lambda e: e.tensor_tensor(out=t[:, :, 0:n], in0=t[:, :, 0:n],
                                                           in1=fns[:].unsqueeze(2).to_broadcast([128, 8, n]), op=ALU.mult),
                         reads=[rt, rfn], writes=[rt])
                    P.dma("sp", yv[:, :, q * LQ + a - s0 - 1:q * LQ + b - s0 - 1], t[:, :, 0:n], reads=[rt])
                fblk()
        else:
            P.dma("sp", inte_ap(Xout, sid, q).rearrange("(k p) n -> p k n", p=128), x[:, :, s0 + 1:s1 - 1], reads=[rx])
    K.end()


NA = CQ + LQ
HD = 128
NKEY = CTX + SEQ
NKT = NKEY // 128


def emit_ka_att(K, G, l, q):
    P = K.P
    I = G.I
    S = G.S
    K.begin()
    Xin = S["X"][l % 2]
    w_qkv = I["att_w_qkv"][l // 2]
    (ones_b, r_ob), (ones_f, r_of) = (G.ones_b, G.r_ob), (G.ones_f, G.r_of)
    mods, rmod, mod1, rmod1 = G.mods[:, l], G.rmod, G.mod1[:, l], G.rmod1
    x, rx = K.sb([128, 8, NA], F32, "x")
    h, rh = K.sb([128, 8, NA], BF16, "h")
    gs, rgs = K.sb([128, 2], F32, "gs")
    cs, rcs = K.sb([128, LQ], F32, "cs")
    sn, rsn = K.sb([128, LQ], F32, "sn")
    pm, rpm = K.sb([128, 128], F32, "pm")
    P.dma("sp", x[:, :, 0:CQ], inte_ap(Xin, 1, q).rearrange("(k p) n -> p k n", p=128), writes=[rx])
    P.dma("sp", x[:, :, CQ:NA], inte_ap(Xin, 0, q).rearrange("(k p) n -> p k n", p=128), writes=[rx])
    P.dma("sp", gs[:], I["gains"][l // 2], writes=[rgs])
    P.dma("act", cs[:], I["cosT"][q], writes=[rcs])
    P.dma("act", sn[:], I["sinT"][q], writes=[rsn])
    P.dma("sp", pm[:], I["perm"], writes=[rpm])
    base = {1: 0, 0: CQ}
    koff = {1: 0, 0: CTX}

    def dst(m, sid, a, b):
        c0 = q * QN[sid] + a - base[sid]
        if m < 8:
            return S["Q"][sid][m * 128:(m + 1) * 128, c0:c0 + b - a]
        return S["KT"][(m - 8) * 128:(m - 7) * 128, koff[sid] + c0:koff[sid] + c0 + b - a]

    scr = rms_scratch(K)
    blocks = [(1, 0, CQ)] + [(0, a, b) for (a, b) in split(CQ, NA)]
    for (sid, a, b) in blocks:
        rms_modulate(K, x, rx, a, b, ones_b, r_ob, lambda k, sid=sid: mod1[:, sid, 1, k:k + 1], lambda k, sid=sid: mods[:, sid, 0, k:k + 1],
                     rmod1, h, rh, a, scr)
    wq = [K.sb([128, 8, 128], BF16, "wq%d" % i) for i in range(2)]
    sq = [K.sb([128, 512], BF16, "sq%d" % i) for i in range(2)]
    rs = [K.sb([128, 512], F32, "rs%d" % i) for i in range(2)]
    qn = [K.sb([128, 512], F32, "qn%d" % i) for i in range(2)]
    t1 = [K.sb([128, 512], F32, "t1%d" % i) for i in range(2)]
    t2 = [K.sb([128, 512], F32, "t2%d" % i) for i in range(2)]
    o16 = [K.sb([128, 512], BF16, "o16%d" % i) for i in range(2)]
    it = 0
    wl = WLoader(K, [128, 8, 256], 2, "wqs")
    for m in range(10):
        w, rw = wq[m % 2]
        wl.load(w[:], rw, w_qkv[:, m * 128:(m + 1) * 128].rearrange("(k p) c -> p k c", p=128), lambda t: t[:, :, 0:128])
        for (sid, a, b) in blocks:
            n = b - a
            it += 1
            bk, rbk = K.bank()
            for k in range(8):
                P.op("pe", lambda e, k=k, a=a, b=b, w=w, bk=bk: e.matmul(bk[:, 0:b - a], lhsT=w[:, k, :], rhs=h[:, k, a:b], start=(k == 0), stop=(k == 7)),
                     reads=[rw, rh], writes=[rbk])
            (q_, rq_) = qn[it % 2]
            (s_, rs_) = sq[it % 2]
            (r_, rr_) = rs[it % 2]
            gi = 0 if m < 8 else 1
            P.op("act", lambda e, n=n, bk=bk, s_=s_: e.activation(out=s_[:, 0:n], in_=bk[:, 0:n], func=AF.Square), reads=[rbk], writes=[rs_])
            b2, rb2 = K.bank()
            P.op("pe", lambda e, n=n, b2=b2, s_=s_: e.matmul(b2[:, 0:n], lhsT=ones_b[:], rhs=s_[:, 0:n], start=True, stop=True),
                 reads=[r_ob, rs_], writes=[rb2])
            P.op("act", lambda e, n=n, b2=b2, r_=r_: e.activation(out=r_[:, 0:n], in_=b2[:, 0:n], func=AF.Sqrt, scale=1.0 / HD, bias=EPS),
                 reads=[rb2], writes=[rr_])
            P.op("dve", lambda e, n=n, r_=r_: e.reciprocal(out=r_[:, 0:n], in_=r_[:, 0:n]), reads=[rr_], writes=[rr_])
            (ob_, rob_) = o16[it % 2]
            if sid == 1:
                P.op("dve", lambda e, n=n, bk=bk, ob_=ob_, r_=r_, gi=gi: e.scalar_tensor_tensor(out=ob_[:, 0:n], in0=bk[:, 0:n], scalar=gs[:, gi:gi + 1],
                                                                                             in1=r_[:, 0:n], op0=ALU.mult, op1=ALU.mult),
                     reads=[rbk, rr_, rgs], writes=[rob_])
                P.dma("sp", dst(m, sid, a, b), ob_[:, 0:n], reads=[rob_])
                continue
            P.op("dve", lambda e, n=n, bk=bk, q_=q_, r_=r_, gi=gi: e.scalar_tensor_tensor(out=q_[:, 0:n], in0=bk[:, 0:n], scalar=gs[:, gi:gi + 1], in1=r_[:, 0:n],
                                                                                       op0=ALU.mult, op1=ALU.mult),
                 reads=[rbk, rr_, rgs], writes=[rq_])
            b3, rb3 = K.bank()
            P.op("pe", lambda e, n=n, b3=b3, q_=q_: e.matmul(b3[:, 0:n], lhsT=pm[:], rhs=q_[:, 0:n], start=True, stop=True),
                 reads=[rpm, rq_], writes=[rb3])
            (u1, ru1) = t1[it % 2]
            (u2, ru2) = t2[it % 2]
            P.op("pool", lambda e, n=n, a=a, q_=q_, u1=u1: e.tensor_tensor(out=u1[:, 0:n], in0=q_[:, 0:n], in1=cs[:, a - CQ:a - CQ + n], op=ALU.mult),
                 reads=[rq_, rcs], writes=[ru1])
            P.op("dve", lambda e, n=n, a=a, b3=b3, u2=u2: e.tensor_tensor(out=u2[:, 0:n], in0=b3[:, 0:n], in1=sn[:, a - CQ:a - CQ + n], op=ALU.mult),
                 reads=[rb3, rsn], writes=[ru2])
            P.op("pool", lambda e, n=n, u1=u1, u2=u2, ob_=ob_: e.tensor_tensor(out=ob_[:, 0:n], in0=u1[:, 0:n], in1=u2[:, 0:n], op=ALU.add),
                 reads=[ru1, ru2], writes=[rob_])
            P.dma("sp", dst(m, sid, a, b), ob_[:, 0:n], reads=[rob_])
    wv, rwv = K.sb([128, 8, 256], BF16, "wv")
    wl.load(wv[:], rwv, w_qkv[:, 1280:1536].rearrange("(k p) c -> p k c", p=128), lambda t: t[:])
    vst = [K.sb([128, 256], BF16, "vst%d" % i) for i in range(3)]
    tblocks = [(1, 0, CQ)] + [(0, CQ + i * 128, CQ + (i + 1) * 128) for i in range(LQ // 128)]
    for i, (sid, a, b) in enumerate(tblocks):
        n = b - a
        bk, rbk = K.bank()
        for k in range(8):
            P.op("pe", lambda e, k=k, a=a, b=b, n=n, bk=bk: e.matmul(bk[0:n, 0:256], lhsT=h[:, k, a:b], rhs=wv[:, k, :], start=(k == 0), stop=(k == 7)),
                 reads=[rh, rwv], writes=[rbk])
        v_, rv_ = vst[i % 3]
        if i % 2:
            P.op("dve", lambda e, n=n, bk=bk, v_=v_: e.tensor_copy(out=v_[0:n, :], in_=bk[0:n, 0:256]), reads=[rbk], writes=[rv_])
        else:
            P.op("act", lambda e, n=n, bk=bk, v_=v_: e.activation(out=v_[0:n, :], in_=bk[0:n, 0:256], func=AF.Copy), reads=[rbk], writes=[rv_])
        t0 = koff[sid] + q * QN[sid] + a - base[sid]
        P.dma("sp", S["VT"][t0:t0 + n, :], v_[0:n, :], reads=[rv_])
    K.end()


def emit_kb_att(K, G, q, need_ctx):
    P = K.P
    S = G.S
    K.begin()
    (ones_b, r_ob), (ones_f, r_of) = (G.ones_b, G.r_ob), (G.ones_f, G.r_of)
    q_sb, rq = K.sb([128, 8, NA], BF16, "q")
    k, rk = K.sb([128, 2, NKEY], BF16, "k")
    v, rv = K.sb([128, NKT, 256], BF16, "v")
    P.dma("sp", q_sb[:, :, 0:CQ], S["Q"][1][:, q * CQ:(q + 1) * CQ].rearrange("(h p) n -> p h n", p=128), writes=[rq])
    P.dma("sp", q_sb[:, :, CQ:NA], S["Q"][0][:, q * LQ:(q + 1) * LQ].rearrange("(h p) n -> p h n", p=128), writes=[rq])
    P.dma("act", k[:], S["KT"].rearrange("(h p) n -> p h n", p=128), writes=[rk])
    P.dma("sp", v[:], S["VT"].rearrange("(t p) f -> p t f", p=128), writes=[rv])
    oseg = {1: (0, CQ), 0: (CQ, NA)}

    def odst(hq, a, b):
        sid = 1 if a < CQ else 0
        c0 = PADC + q * QN[sid] + a - oseg[sid][0]
        return S["MIX"][0][sid][hq * 128:(hq + 1) * 128, c0:c0 + b - a]

    pt = [K.sb([128, 512], BF16, "pt%d" % i) for i in range(4)]
    ob = [K.sb([128, 512], F32, "ob%d" % i) for i in range(2)]
    rc = [K.sb([128, 512], F32, "rc%d" % i) for i in range(2)]
    rsacc = [[K.sb([128, 512], F32, "rsacc%d%d" % (i, j)) for j in range(2)] for i in range(2)]
    sbank = [(K.banks[i], K.rbank[i]) for i in range(4)]
    accs = [((K.banks[4], K.rbank[4]), (K.banks[5], K.rbank[5])), ((K.banks[6], K.rbank[6]), (K.banks[7], K.rbank[7]))]
    scale = float(HD) ** -0.5
    qblocks = [(a, b, NKT) for (a, b) in split(CQ, NA)]
    if need_ctx:
        qblocks.append((0, CQ, CTX // 128))
    state = {"it": 0, "si": 0}

    def qblock(hq, kvh, a, b, nkt):
        n = b - a
        it = state["it"]
        (acc, racc), (rsum, rrsum) = accs[it % 2]
        o_, ro_ = ob[it % 2]
        c_, rc_ = rc[it % 2]
        ra = rsacc[it % 2]
        state["it"] += 1

        def smm(kt, si):
            bk, rbk = sbank[si % 4]
            P.op("pe", lambda e: e.matmul(bk[:, 0:n], lhsT=k[:, kvh, kt * 128:(kt + 1) * 128], rhs=q_sb[:, hq, a:b], start=True, stop=True),
                 reads=[rk, rq], writes=[rbk])

        def step(kt, si):
            bk, rbk = sbank[si % 4]
            p_, rp_ = pt[si % 4]
            P.op("act", lambda e: e.activation(out=p_[:, 0:n], in_=bk[:, 0:n], func=AF.Exp, scale=scale), reads=[rbk], writes=[rp_])
            P.op("pe", lambda e: e.matmul(acc[:, 0:n], lhsT=v[:, kt, kvh * 128:(kvh + 1) * 128], rhs=p_[:, 0:n],
                                          start=(kt == 0), stop=(kt == nkt - 1)),
                 reads=[rv, rp_], writes=[racc])
            (a_, ra_) = ra[kt % 2]
            eng = "pool" if kt % 2 == 0 else "dve"
            if kt < 2:
                P.op(eng, lambda e: e.tensor_copy(out=a_[:, 0:n], in_=p_[:, 0:n]), reads=[rp_], writes=[ra_])
            else:
                P.op(eng, lambda e: e.tensor_tensor(out=a_[:, 0:n], in0=a_[:, 0:n], in1=p_[:, 0:n], op=ALU.add), reads=[ra_, rp_], writes=[ra_])

        smm(0, state["si"])
        for kt in range(nkt):
            if kt + 1 < nkt:
                smm(kt + 1, state["si"] + 1)
            step(kt, state["si"])
            state["si"] += 1
        nh = min(2, nkt)
        for hf in range(nh):
            (a_, ra_) = ra[hf]
            P.op("pe", lambda e, a_=a_, hf=hf: e.matmul(rsum[:, 0:n], lhsT=ones_f[:], rhs=a_[:, 0:n], start=(hf == 0), stop=(hf == nh - 1)),
                 reads=[r_of, ra_], writes=[rrsum])
        P.op("dve", lambda e: e.reciprocal(out=c_[:, 0:n], in_=rsum[:, 0:n]), reads=[rrsum], writes=[rc_])
        P.op("dve", lambda e: e.tensor_tensor(out=o_[:, 0:n], in0=acc[:, 0:n], in1=c_[:, 0:n], op=ALU.mult), reads=[racc, rc_], writes=[ro_])
        P.dma("sp", odst(hq, a, b), o_[:, 0:n], reads=[ro_])

    for hq in range(8):
        for (a, b, nkt) in qblocks:
            qblock(hq, hq // 4, a, b, nkt)
    K.end()


REC_IN = 3632
NF = 3584
CH = 64
NCHUNK = NKEY // CH


def emit_ka_rec(K, G, l, q):
    P = K.P
    I = G.I
    S = G.S
    e_ = l // 2
    K.begin()
    segs, NT = layout(True)
    Xin = S["X"][l % 2]
    w_in = I["rec_w_in"][e_]
    (ones_b, r_ob), (ones_f, r_of) = (G.ones_b, G.r_ob), (G.ones_f, G.r_of)
    mods, rmod, mod1, rmod1 = G.mods[:, l], G.rmod, G.mod1[:, l], G.rmod1
    h, rh = K.sb([128, 8, NT], BF16, "h")
    cws, rcw = K.sb([128, 3, 12], F32, "cws")
    dtb, rdtb = K.sb([128, 8], F32, "dtb")
    nA, rnA = K.sb([128, 8], F32, "nA")
    w2s, rw2 = K.sb([16, 2, 256], F32, "w2s")
    b2bc, rb2 = K.sb([128, 2, 256], F32, "b2bc")
    idt, ridt = K.sb([128, 128], F32, "idt")
    outer = K.cur
    K.cur = ExitStack()
    x, rx = K.sb([128, 8, NT], F32, "x")
    for (sid, s0, s1) in segs:
        P.dma("sp", x[:, :, s0:s1], halo_ap(Xin, sid, q).rearrange("(k p) n -> p k n", p=128), writes=[rx])
    P.dma("sp", cws[:], I["rec_cw"][e_], writes=[rcw])
    P.dma("sp", dtb[:], I["dt_bias"][e_].partition_broadcast(128), writes=[rdtb])
    P.dma("sp", nA[:], I["a_log"][e_].partition_broadcast(128), writes=[rnA])
    P.dma("sp", w2s[:], I["w2"][e_], writes=[rw2])
    P.dma("sp", b2bc[:].rearrange("p d f -> p (d f)"), I["b2"][e_].partition_broadcast(128), writes=[rb2])
    P.dma("sp", idt[:], I["ident"], writes=[ridt])
    P.op("act", lambda e: e.activation(out=nA[:], in_=nA[:], func=AF.Exp), reads=[rnA], writes=[rnA])
    P.op("dve", lambda e: e.tensor_scalar(out=nA[:], in0=nA[:], scalar1=-1.0, scalar2=None, op0=ALU.mult), reads=[rnA], writes=[rnA])
    scr = rms_scratch(K)
    for (sid, s0, s1) in segs:
        for (a, b) in split(s0, s1):
            rms_modulate(K, x, rx, a, b, ones_b, r_ob, lambda k, sid=sid: mod1[:, sid, 1, k:k + 1], lambda k, sid=sid: mods[:, sid, 0, k:k + 1],
                         rmod1, h, rh, a, scr)
    P.barrier()
    K.cur.close()
    K.cur = outer
    wq = [K.sb([128, 8, 128], BF16, "wq%d" % i) for i in range(2)]
    pre = [K.sb([128, NT], F32, "pre%d" % i) for i in range(2)]
    cb = [K.sb([128, NT], F32, "cb%d" % i) for i in range(2)]
    sqb = [K.sb([128, 512], F32, "sqb%d" % i) for i in range(2)]
    rsb = [K.sb([128, 512], F32, "rsb%d" % i) for i in range(2)]
    tmst = [K.sb([128, 17, 128], F32, "tmst%d" % i) for i in range(2)]
    koff = {1: 0, 0: CTX}
    NTB = LQ // 128
    (c_s0, c_s1) = [(s0, s1) for (sid, s0, s1) in segs if sid == 1][0]
    (l_s0, l_s1) = [(s0, s1) for (sid, s0, s1) in segs if sid == 0][0]
    tblocks = [(c_s0 + 1, c_s0 + 1 + CQ)] + [(l_s0 + 1 + i * 128, l_s0 + 1 + (i + 1) * 128) for i in range(NTB)]

    def seqcols(sid):
        c0 = koff[sid] + q * QN[sid]
        return c0, c0 + QN[sid]

    def tm_out(D_, col, width, st):
        (t_, rt_) = st
        c0, c1 = seqcols(1)
        P.dma("sp", D_[c0:c1, col:col + width], t_[0:CQ, 0, 0:width], reads=[rt_])
        c0, c1 = seqcols(0)
        P.dma("sp", D_[c0:c1, col:col + width].rearrange("(t p) f -> p t f", p=128), t_[:, 1:17, 0:width], reads=[rt_])

    wl = WLoader(K, [128, 8, 512], 2, "wis")

    def load_w(i, col, width=128):
        w, rw = wq[i % 2]
        wl.load(w[:, :, 0:width], rw, w_in[:, col:col + width].rearrange("(k p) c -> p k c", p=128), lambda t: t[:, :, 0:width])
        return w, rw

    def proj(w, rw, width, a, b):
        bk, rbk = K.bank()
        for k in range(8):
            P.op("pe", lambda e, k=k: e.matmul(bk[0:width, 0:b - a], lhsT=w[:, k, 0:width], rhs=h[:, k, a:b], start=(k == 0), stop=(k == 7)),
                 reads=[rw, rh], writes=[rbk])
        return bk, rbk

    def conv_tile(m):
        w, rw = load_w(m, m * 128)
        g, rg = pre[m % 2]
        c, rc = cb[m % 2]
        for (sid, s0, s1) in segs:
            for i, (a, b) in enumerate(split(s0, s1)):
                bk, rbk = proj(w, rw, 128, a, b)
                if i % 2:
                    P.op("dve", lambda e, a=a, b=b, bk=bk: e.tensor_copy(out=g[:, a:b], in_=bk[:, 0:b - a]), reads=[rbk], writes=[rg])
                else:
                    P.op("act", lambda e, a=a, b=b, bk=bk: e.activation(out=g[:, a:b], in_=bk[:, 0:b - a], func=AF.Copy), reads=[rbk], writes=[rg])
        for (sid, s0, s1) in segs:
            for (col, cond) in ((s0, q == 0), (s1 - 1, q == 3)):
                if cond:
                    P.op("pool", lambda e, col=col: e.memset(g[:, col:col + 1], 0.0), reads=[rg], writes=[rg])
        for (sid, s0, s1) in segs:
            P.op("pool", lambda e, s0=s0, s1=s1: e.tensor_scalar(out=c[:, s0 + 1:s1 - 1], in0=g[:, s0:s1 - 2], scalar1=cws[:, 0, m:m + 1],
                                                                 scalar2=None, op0=ALU.mult), reads=[rg, rcw], writes=[rc])
            P.op("dve", lambda e, s0=s0, s1=s1: e.scalar_tensor_tensor(out=c[:, s0 + 1:s1 - 1], in0=g[:, s0 + 1:s1 - 1], scalar=cws[:, 1, m:m + 1],
                                                                       in1=c[:, s0 + 1:s1 - 1], op0=ALU.mult, op1=ALU.add),
                 reads=[rg, rc, rcw], writes=[rc])
            P.op("dve", lambda e, s0=s0, s1=s1: e.scalar_tensor_tensor(out=c[:, s0 + 1:s1 - 1], in0=g[:, s0 + 2:s1], scalar=cws[:, 2, m:m + 1],
                                                                       in1=c[:, s0 + 1:s1 - 1], op0=ALU.mult, op1=ALU.add),
                 reads=[rg, rc, rcw], writes=[rc])
            P.op("act", lambda e, s0=s0, s1=s1: e.activation(out=c[:, s0 + 1:s1 - 1], in_=c[:, s0 + 1:s1 - 1], func=AF.Silu), reads=[rc], writes=[rc])
        if m < 8:
            sc, bi = (128.0, 128.0 * EPS) if m < 4 else (1.0, EPS)
            it = 0
            for (sid, s0, s1) in segs:
                for (a, b) in split(s0 + 1, s1 - 1):
                    n = b - a
                    sq_, rsq_ = sqb[it % 2]
                    rs_, rrs_ = rsb[it % 2]
                    it += 1

                    def blk(a=a, b=b, n=n, sq_=sq_, rsq_=rsq_, rs_=rs_, rrs_=rrs_):
                        P.op("act", lambda e: e.activation(out=sq_[:, 0:n], in_=c[:, a:b], func=AF.Square), reads=[rc], writes=[rsq_])
                        bk, rbk = K.bank()
                        P.op("pe", lambda e: e.matmul(bk[:, 0:n], lhsT=ones_f[:], rhs=sq_[:, 0:n], start=True, stop=True), reads=[r_of, rsq_], writes=[rbk])
                        P.op("act", lambda e: e.activation(out=rs_[:, 0:n], in_=bk[:, 0:n], func=AF.Sqrt, scale=sc, bias=bi), reads=[rbk], writes=[rrs_])
                        P.op("dve", lambda e: e.reciprocal(out=rs_[:, 0:n], in_=rs_[:, 0:n]), reads=[rrs_], writes=[rrs_])
                        P.op("dve", lambda e: e.tensor_tensor(out=c[:, a:b], in0=c[:, a:b], in1=rs_[:, 0:n], op=ALU.mult), reads=[rc, rrs_], writes=[rc])
                    blk()
            for (sid, s0, s1) in segs:
                c0, c1 = seqcols(sid)
                P.dma("sp", S["FQK"][m * 128:(m + 1) * 128, c0:c1], c[:, s0 + 1:s1 - 1], reads=[rc])
        if m >= 4:
            st = tmst[m % 2]
            (t_, rt_) = st
            for i, (a, b) in enumerate(tblocks):
                n = b - a
                bk, rbk = K.bank()
                P.op("pe", lambda e, a=a, b=b, n=n, bk=bk: e.transpose(out=bk[0:n, 0:128], in_=c[:, a:b], identity=idt[:]), reads=[rc, ridt], writes=[rbk])
                if i % 2:
                    P.op("dve", lambda e, i=i, n=n, bk=bk: e.tensor_copy(out=t_[0:n, i, :], in_=bk[0:n, 0:128]), reads=[rbk], writes=[rt_])
                else:
                    P.op("act", lambda e, i=i, n=n, bk=bk: e.activation(out=t_[0:n, i, :], in_=bk[0:n, 0:128], func=AF.Copy), reads=[rbk], writes=[rt_])
            tm_out(S["DKV"], (m - 4) * 128, 128, st)

    for m in range(12):
        conv_tile(m)

    plain = [(1536 + i * 128, ("zr", i * 128), "silu") for i in range(4)]
    plain += [(3088 + i * 128, ("zr", 512 + i * 128), "silu") for i in range(4)]
    plain += [(2064 + i * 128, ("fqk", 1024 + i * 128), "scale") for i in range(2)]
    plain += [(2320 + i * 128, ("fqk", 1280 + i * 128), "copy") for i in range(2)]

    def plain_tile(i, wcol, orow, kind):
        w, rw = load_w(i, wcol)
        c, rc = cb[i % 2]
        for (sid, s0, s1) in segs:
            for (a, b) in split(s0 + 1, s1 - 1):
                bk, rbk = proj(w, rw, 128, a, b)
                if kind == "silu":
                    P.op("act", lambda e, a=a, b=b, bk=bk: e.activation(out=c[:, a:b], in_=bk[:, 0:b - a], func=AF.Silu), reads=[rbk], writes=[rc])
                elif kind == "scale":
                    P.op("dve", lambda e, a=a, b=b, bk=bk: e.tensor_scalar(out=c[:, a:b], in0=bk[:, 0:b - a], scalar1=0.125, scalar2=None, op0=ALU.mult),
                         reads=[rbk], writes=[rc])
                else:
                    P.op("dve", lambda e, a=a, b=b, bk=bk: e.tensor_copy(out=c[:, a:b], in_=bk[:, 0:b - a]), reads=[rbk], writes=[rc])
        for (sid, s0, s1) in segs:
            if orow[0] == "zr":
                P.dma("sp", inte_ap(S["MIX"][2], sid, q)[orow[1]:orow[1] + 128, :], c[:, s0 + 1:s1 - 1], reads=[rc])
            else:
                c0, c1 = seqcols(sid)
                P.dma("sp", S["FQK"][orow[1]:orow[1] + 128, c0:c1], c[:, s0 + 1:s1 - 1], reads=[rc])

    for i, (wcol, orow, kind) in enumerate(plain):
        plain_tile(i, wcol, orow, kind)

    wg, rwg = K.sb([128, 8, 768], BF16, "wg")
    wl.load(wg[:, :, 0:256], rwg, w_in[:, 2320:2576].rearrange("(k p) c -> p k c", p=128), lambda t: t[:, :, 0:256])
    wl.load(wg[:, :, 256:768], rwg, w_in[:, 2576:3088].rearrange("(k p) c -> p k c", p=128), lambda t: t[:, :, 0:512])
    gst = [K.sb([128, 768], F32, "gst%d" % i) for i in range(3)]
    for i, (a, b) in enumerate(tblocks):
        n = b - a
        (t_, rt_) = gst[i % 3]
        for part, (c0_, wd) in enumerate(((0, 256), (256, 512))):
            bk, rbk = K.bank()
            for k in range(8):
                P.op("pe", lambda e, k=k, a=a, b=b, n=n, bk=bk, c0_=c0_, wd=wd: e.matmul(bk[0:n, 0:wd], lhsT=h[:, k, a:b], rhs=wg[:, k, c0_:c0_ + wd],
                                                                                        start=(k == 0), stop=(k == 7)),
                     reads=[rh, rwg], writes=[rbk])
            if part:
                P.op("dve", lambda e, n=n, bk=bk, wd=wd, c0_=c0_, t_=t_: e.tensor_copy(out=t_[0:n, c0_:c0_ + wd], in_=bk[0:n, 0:wd]), reads=[rbk], writes=[rt_])
            else:
                P.op("act", lambda e, n=n, bk=bk, wd=wd, c0_=c0_, t_=t_: e.activation(out=t_[0:n, c0_:c0_ + wd], in_=bk[0:n, 0:wd], func=AF.Copy),
                     reads=[rbk], writes=[rt_])
        r0 = (seqcols(1)[0]) if i == 0 else (seqcols(0)[0] + (i - 1) * 128)
        P.dma("sp", S["GKV"][r0:r0 + n, :], t_[0:n, :], reads=[rt_])

    wab, rwab = load_w(0, 2048, 16)
    gbs, rgbs = K.sb([128, 17, 16], F32, "gbs")
    bk, rbk = K.bank()
    for i, (a, b) in enumerate(tblocks):
        n = b - a
        for k in range(8):
            P.op("pe", lambda e, k=k, a=a, b=b, n=n, i=i, bk=bk, wab=wab: e.matmul(bk[0:n, i * 16:(i + 1) * 16], lhsT=h[:, k, a:b], rhs=wab[:, k, 0:16], start=(k == 0), stop=(k == 7)),
                 reads=[rh, rwab], writes=[rbk])
    for (p0, p1, i0, i1) in ((0, CQ, 0, 1), (0, 128, 1, 17)):
        nb = i1 - i0
        src = bk[p0:p1, i0 * 16:i1 * 16].rearrange("p (t c) -> p t c", c=16)

        def gates(p0=p0, p1=p1, i0=i0, i1=i1, nb=nb, src=src):
            gv_ = gbs[p0:p1, i0:i1, 0:8]
            bv_ = gbs[p0:p1, i0:i1, 8:16]
            P.op("dve", lambda e: e.tensor_tensor(out=gv_, in0=src[:, :, 0:8], in1=dtb[p0:p1, :].unsqueeze(1).to_broadcast([p1 - p0, nb, 8]), op=ALU.add),
                 reads=[rbk, rdtb], writes=[rgbs])
            P.op("act", lambda e: e.activation(out=gv_, in_=gv_, func=AF.Exp), reads=[rgbs], writes=[rgbs])
            P.op("act", lambda e: e.activation(out=gv_, in_=gv_, func=AF.Ln, bias=1.0), reads=[rgbs], writes=[rgbs])
            P.op("dve", lambda e: e.tensor_tensor(out=gv_, in0=gv_, in1=nA[p0:p1, :].unsqueeze(1).to_broadcast([p1 - p0, nb, 8]), op=ALU.mult),
                 reads=[rgbs, rnA], writes=[rgbs])
            P.op("act", lambda e: e.activation(out=bv_, in_=src[:, :, 8:16], func=AF.Exp, scale=-1.0), reads=[rbk], writes=[rgbs])
            P.op("dve", lambda e: e.tensor_scalar(out=bv_, in0=bv_, scalar1=1.0, scalar2=None, op0=ALU.add), reads=[rgbs], writes=[rgbs])
            P.op("dve", lambda e: e.reciprocal(out=bv_, in_=bv_), reads=[rgbs], writes=[rgbs])
        gates()
    tm_out(S["GB"], 0, 16, (gbs, rgbs))

    ggs1 = K.sb([16, NT], F32, "ggs")
    lst = [K.sb([128, 256], F32, "lst%d" % i) for i in range(3)]
    for d in range(2):
        wgd, rwgd = load_w(d + 1, 3600 + 16 * d, 16)
        gg_, rgg_ = ggs1
        for (sid, s0, s1) in segs:
            for (a, b) in split(s0 + 1, s1 - 1):
                bk, rbk = proj(wgd, rwgd, 16, a, b)
                P.op("dve", lambda e, a=a, b=b, bk=bk, gg_=gg_: e.tensor_copy(out=gg_[0:16, a:b], in_=bk[0:16, 0:b - a]), reads=[rbk], writes=[rgg_])
        for i, (a, b) in enumerate(tblocks):
            n = b - a
            (v_, rv_) = lst[i % 3]
            bk, rbk = K.bank()
            P.op("pe", lambda e, a=a, b=b, n=n, bk=bk, gg_=gg_, d=d: e.matmul(bk[0:n, 0:256], lhsT=gg_[0:16, a:b], rhs=w2s[:, d, :], start=True, stop=True),
                 reads=[rgg_, rw2], writes=[rbk])
            P.op("dve", lambda e, n=n, bk=bk, d=d, v_=v_: e.tensor_tensor(out=v_[0:n, :], in0=bk[0:n, 0:256], in1=b2bc[0:n, d, :], op=ALU.add),
                 reads=[rbk, rb2], writes=[rv_])
            P.op("act", lambda e, n=n, v_=v_: e.activation(out=v_[0:n, :], in_=v_[0:n, :], func=AF.Exp, scale=-1.0), reads=[rv_], writes=[rv_])
            P.op("act", lambda e, n=n, v_=v_: e.activation(out=v_[0:n, :], in_=v_[0:n, :], func=AF.Ln, bias=1.0), reads=[rv_], writes=[rv_])
            P.op("pool", lambda e, n=n, v_=v_: e.tensor_scalar(out=v_[0:n, :], in0=v_[0:n, :], scalar1=-1.0 / 16.0, scalar2=None, op0=ALU.mult),
                 reads=[rv_], writes=[rv_])
            r0 = (seqcols(1)[0]) if i == 0 else (seqcols(0)[0] + (i - 1) * 128)
            P.dma("sp", S["LA"][r0:r0 + n, d * 256:(d + 1) * 256], v_[0:n, :], reads=[rv_])
    K.end()


GC = 4
GW = GC * CH
NG = NCHUNK // GC


def emit_kb_rec(K, G, hh, NG=NG):
    P = K.P
    I = G.I
    S = G.S
    K.begin()
    T = NKEY
    FQK = S["FQK"]

    def tmv(ap2):
        return ap2.rearrange("(n j) f -> j n f", j=CH)
    dq = [FQK[hh * 128:(hh + 1) * 128, :]] * 2
    dk = [FQK[512 + hh * 128:512 + (hh + 1) * 128, :]] * 2
    dktm = [tmv(S["DKV"][:, hh * 128:(hh + 1) * 128])] * 2
    dvtm = [tmv(S["DKV"][:, 512 + hh * 128:512 + (hh + 1) * 128])] * 2
    gq = [FQK[1024 + hh * 64:1024 + (hh + 1) * 64, :]] * 2
    gk = [FQK[1280 + hh * 64:1280 + (hh + 1) * 64, :]] * 2
    gktm = [tmv(S["GKV"][:, hh * 64:(hh + 1) * 64])] * 2
    gvtm = [tmv(S["GKV"][:, 256 + hh * 128:256 + (hh + 1) * 128])] * 2
    gla = [tmv(S["LA"][:, d * 256 + hh * 64:d * 256 + (hh + 1) * 64]) for d in range(2)]
    order = [list(range(NG)), [0] + list(range(NG - 1, 0, -1))]
    corder = [list(range(GC)), list(range(GC - 1, -1, -1))]
    mid_idx = [32, 31]
    last_idx = [63, 0]

    def odst(mixer, d, gmem):
        r0 = mixer * 512 + hh * 128
        if gmem == 0:
            return S["MIX"][d][1][r0:r0 + 128, PADC:PADC + CTX]
        c0 = PADC + (gmem - 1) * GW
        return S["MIX"][d][0][r0:r0 + 128, c0:c0 + GW]

    (ones_b, r_ob), (ones_f, r_of) = (G.ones_b, G.r_ob), (G.ones_f, G.r_of)
    acc_dn = [(K.banks[4 + d], K.rbank[4 + d]) for d in range(2)]
    acc_gl = [(K.banks[6 + d], K.rbank[6 + d]) for d in range(2)]
    rot = {"i": 0}

    def bank():
        i = rot["i"]
        rot["i"] = (i + 1) % 4
        return K.banks[i], K.rbank[i]

    cms, rcm = K.sb([64, 2, 6, 64], F32, "cms")
    P.dma("sp", cms[:], I["cm"], writes=[rcm])
    CMU = [cms[:, d, 0, :] for d in range(2)]
    CMUs = [cms[:, d, 1, :] for d in range(2)]
    CMLs = [cms[:, d, 2, :] for d in range(2)]
    CMNegU = [cms[:, d, 3, :] for d in range(2)]
    CMNegLs = [cms[:, d, 4, :] for d in range(2)]
    I64 = cms[:, 0, 5, :]

    def bc(ap2, n=GC):
        return ap2.unsqueeze(1).to_broadcast([64, n, 64])

    def flat(t):
        return t[:].rearrange("p a b -> p (a b)")

    gcol, bcol, Gcol, ekd, gl, bg = [], [], [], [], [], []
    gb_all, rgb_all = K.sb([64, NCHUNK, 16], F32, "gb_all")
    P.dma("sp", gb_all[:], tmv(S["GB"]), writes=[rgb_all])
    for d in range(2):
        g_, rg_ = gb_all[:, :, d * 4 + hh], rgb_all
        b_, rb_ = gb_all[:, :, 8 + d * 4 + hh], rgb_all
        G_, rG_ = K.sb([64, NCHUNK], F32, "Gcol%d" % d)
        e_, re_ = K.sb([64, NCHUNK], F32, "ekd%d" % d)
        l_, rl_ = K.sb([128, NCHUNK], F32, "gl%d" % d)
        x_, rx_ = K.sb([64, NCHUNK], F32, "bg%d" % d)
        U = CMU[d]

        def pre(g_=g_, rg_=rg_, b_=b_, rb_=rb_, G_=G_, rG_=rG_, e_=e_, re_=re_, l_=l_, rl_=rl_, x_=x_, rx_=rx_, U=U):
            b1, rb1 = bank()
            P.op("pe", lambda e: e.matmul(b1[0:64, 0:NCHUNK], lhsT=U, rhs=g_[:], start=True, stop=True), reads=[rcm, rg_], writes=[rb1])
            P.op("act", lambda e: e.activation(out=G_[:], in_=b1[0:64, 0:NCHUNK], func=AF.Copy), reads=[rb1], writes=[rG_])
            b2, rb2 = bank()
            P.op("pe", lambda e: e.matmul(b2[:, 0:NCHUNK], lhsT=ones_f[0:64, :], rhs=g_[:], start=True, stop=True), reads=[r_of, rg_], writes=[rb2])
            P.op("act", lambda e: e.activation(out=l_[:], in_=b2[:, 0:NCHUNK], func=AF.Exp), reads=[rb2], writes=[rl_])
            P.op("dve", lambda e: e.tensor_tensor(out=e_[:], in0=b2[0:64, 0:NCHUNK], in1=G_[:], op=ALU.subtract), reads=[rb2, rG_], writes=[re_])
            P.op("act", lambda e: e.activation(out=e_[:], in_=e_[:], func=AF.Exp), reads=[re_], writes=[re_])
            P.op("act", lambda e: e.activation(out=x_[:], in_=G_[:], func=AF.Exp), reads=[rG_], writes=[rx_])
            P.op("dve", lambda e: e.tensor_tensor(out=x_[:], in0=x_[:], in1=b_[:], op=ALU.mult), reads=[rx_, rb_], writes=[rx_])
        pre()
        gcol.append((g_, rg_)); bcol.append((b_, rb_)); Gcol.append((G_, rG_)); ekd.append((e_, re_)); gl.append((l_, rl_)); bg.append((x_, rx_))

    def tl(shape, name):
        return K.sb(shape, F32, name)
    DL = [[{"q": tl([128, GW], "dLq%d%d" % (d, p)), "k": tl([128, GW], "dLk%d%d" % (d, p)), "ktm": tl([64, GC, 128], "dLkt%d%d" % (d, p)),
            "vtm": tl([64, GC, 128], "dLvt%d%d" % (d, p))} for p in range(2)] for d in range(2)]
    DT = [{"R": tl([64, GC, 64], "R%d" % d), "R2": tl([64, GC, 64], "R2%d" % d), "eG": tl([128, GW], "eG%d" % d), "kbT": tl([128, GW], "kbT%d" % d),
           "D1": tl([64, GC, 64], "D1%d" % d), "D2": tl([64, GC, 64], "D2%d" % d), "E1s": tl([64, GC, 64], "E1s%d" % d),
           "AT": tl([64, GC, 64], "AT%d" % d), "A": tl([64, GC, 64], "A%d" % d), "Pt": tl([64, GC, 64], "Pt%d" % d),
           "M": [tl([64, GC, 64], "M%d%d" % (d, i)) for i in range(2)], "MT": [tl([64, GC, 64], "MT%d%d" % (d, i)) for i in range(2)],
           "vb": tl([64, GC, 128], "vb%d" % d), "kbg": tl([64, GC, 128], "kbg%d" % d)} for d in range(2)]
    DS = [[{"IT": tl([64, GC, 64], "IT%d%d" % (d, p)), "qd": tl([128, GW], "qd%d%d" % (d, p)), "kdec": tl([64, GC, 128], "kdec%d%d" % (d, p)),
            "u": tl([64, GC, 128], "u%d%d" % (d, p)), "wT": tl([128, GC, 64], "wT%d%d" % (d, p))} for p in range(2)] for d in range(2)]
    Sdn = [[tl([128, 128], "Sdn%d%d" % (d, i)) for i in range(2)] for d in range(2)]
    vnb = [[tl([64, 128], "vn%d%d" % (d, i)) for i in range(2)] for d in range(2)]
    ost = [[tl([128, GW], "ost%d%d" % (m, d)) for d in range(2)] for m in range(2)]
    GL = [[{"q": tl([64, GW], "gLq%d%d" % (d, p)), "k": tl([64, GW], "gLk%d%d" % (d, p)), "ktm": tl([64, GC, 64], "gLkt%d%d" % (d, p)),
            "la": tl([64, GC, 64], "gLla%d%d" % (d, p))} for p in range(2)] for d in range(2)]
    GV = [[tl([64, GC, 128], "gLv%d%d" % (d, p)) for p in range(3)] for d in range(2)]
    GT = [{"b": tl([64, GC, 64], "gb%d" % d), "dd": tl([64, GC, 64], "gdd%d" % d), "kst": tl([64, GC, 64], "gkst%d" % d),
           "bT": tl([64, GC, 64], "gbT%d" % d), "e1": tl([64, GC, 64], "ge1%d" % d), "eq": tl([64, GC, 64], "geq%d" % d),
           "ek": tl([64, GC, 64], "gek%d" % d), "ei": tl([64, GC, 64], "gei%d" % d), "qtl": tl([64, GW], "gqtl%d" % d),
           "ktl": tl([64, GW], "gktl%d" % d)} for d in range(2)]
    GS = [[{"qi": tl([64, GW], "gqi%d%d" % (d, p)), "att": tl([64, GC, 64], "gatt%d%d" % (d, p)), "KV": tl([64, GC, 128], "gKV%d%d" % (d, p)),
            "al": tl([64, GC], "gal%d%d" % (d, p))} for p in range(2)] for d in range(2)]
    Sgl = [[tl([64, 128], "Sgl%d%d" % (d, i)) for i in range(2)] for d in range(2)]
    for d in range(2):
        for (t_, r_) in (Sdn[d][0], Sgl[d][0]):
            P.op("pool", lambda e, t_=t_: e.memset(t_[:], 0.0), writes=[r_])
    scur = {"dn": [0, 0], "gl": [0, 0], "vn": [0, 0]}

    def loads(gi):
        p = gi % 2
        for d in range(2):
            n0 = order[d][gi] * GC
            t0 = n0 * CH
            L = DL[d][p]
            P.dma("sp", L["q"][0][:], dq[d][:, t0:t0 + GW], writes=[L["q"][1]])
            P.dma("sp", L["k"][0][:], dk[d][:, t0:t0 + GW], writes=[L["k"][1]])
            P.dma("sp", L["ktm"][0][:], dktm[d][:, n0:n0 + GC, :], writes=[L["ktm"][1]])
            P.dma("sp", L["vtm"][0][:], dvtm[d][:, n0:n0 + GC, :], writes=[L["vtm"][1]])
            G = GL[d][p]
            P.dma("sp", G["q"][0][:], gq[d][:, t0:t0 + GW], writes=[G["q"][1]])
            P.dma("sp", G["k"][0][:], gk[d][:, t0:t0 + GW], writes=[G["k"][1]])
            P.dma("sp", G["ktm"][0][:], gktm[d][:, n0:n0 + GC, :], writes=[G["ktm"][1]])
            P.dma("sp", G["la"][0][:], gla[d][:, n0:n0 + GC, :], writes=[G["la"][1]])
            gv_, rgv_ = GV[d][gi % 3]
            P.dma("sp", gv_[:], gvtm[d][:, n0:n0 + GC, :], writes=[rgv_])

    def cs(n):
        return slice(n * CH, (n + 1) * CH)

    def v3(ap2):
        return ap2.rearrange("p (a b) -> p a b", a=GC)

    def dn_prep(d, gi):
        n0 = order[d][gi] * GC
        p = gi % 2
        U, Us, Ls, NegU, NegLs = CMU[d], CMUs[d], CMLs[d], CMNegU[d], CMNegLs[d]
        L, Tm, S2 = DL[d][p], DT[d], DS[d][p]
        (qT, rqT), (kT, rkT), (ktm, rktm), (vtm, rvtm) = L["q"], L["k"], L["ktm"], L["vtm"]
        (R, rR), (R2, rR2), (eG, reG), (kbT, rkbT) = Tm["R"], Tm["R2"], Tm["eG"], Tm["kbT"]
        (D1, rD1), (D2, rD2), (E1s, rE1s), (AT, rAT), (A_, rA_), (Pt, rPt) = Tm["D1"], Tm["D2"], Tm["E1s"], Tm["AT"], Tm["A"], Tm["Pt"]
        (vb, rvb), (kbg, rkbg) = Tm["vb"], Tm["kbg"]
        (IT, rIT), (qd, rqd), (kdec, rkdec), (u, ru), (wT, rwT) = S2["IT"], S2["qd"], S2["kdec"], S2["u"], S2["wT"]
        (g_, rg_), (b_, rb_), (G_, rG_), (e_, re_), (x_, rx_) = gcol[d], bcol[d], Gcol[d], ekd[d], bg[d]
        gs = slice(n0, n0 + GC)
        P.op("dve", lambda e: e.tensor_tensor(out=R[:], in0=g_[:, gs].unsqueeze(2).to_broadcast([64, GC, 64]), in1=bc(U), op=ALU.mult),
             reads=[rg_, rcm], writes=[rR])
        P.op("dve", lambda e: e.tensor_tensor(out=R2[:], in0=b_[:, gs].unsqueeze(2).to_broadcast([64, GC, 64]), in1=bc(I64), op=ALU.mult),
             reads=[rb_, rcm], writes=[rR2])
        bA, rbA = bank()
        P.op("pe", lambda e: e.matmul(bA[:, 0:GW], lhsT=ones_f[0:64, :], rhs=flat(R), start=True, stop=True), reads=[r_of, rR], writes=[rbA])
        bB, rbB = bank()
        P.op("pe", lambda e: e.matmul(bB[:, 0:GW], lhsT=ones_f[0:64, :], rhs=flat(R2), start=True, stop=True), reads=[r_of, rR2], writes=[rbB])
        P.op("act", lambda e: e.activation(out=eG[:], in_=bA[:, 0:GW], func=AF.Exp), reads=[rbA], writes=[reG])
        P.op("dve", lambda e: e.tensor_tensor(out=kbT[:], in0=kT[:], in1=bB[:, 0:GW], op=ALU.mult), reads=[rkT, rbB], writes=[rkbT])
        P.op("dve", lambda e: e.tensor_tensor(out=D1[:], in0=v3(bA[0:64, 0:GW]), in1=G_[:, gs].unsqueeze(2).to_broadcast([64, GC, 64]), op=ALU.subtract),
             reads=[rbA, rG_], writes=[rD1])
        yield
        P.op("dve", lambda e: e.tensor_tensor(out=D2[:], in0=D1[:], in1=bc(Ls), op=ALU.mult), reads=[rD1, rcm], writes=[rD2])
        P.op("dve", lambda e: e.tensor_tensor(out=D2[:], in0=D2[:], in1=bc(NegLs), op=ALU.add), reads=[rD2, rcm], writes=[rD2])
        P.op("dve", lambda e: e.tensor_tensor(out=D1[:], in0=D1[:], in1=bc(U), op=ALU.mult), reads=[rD1, rcm], writes=[rD1])
        P.op("dve", lambda e: e.tensor_tensor(out=D1[:], in0=D1[:], in1=bc(NegU), op=ALU.add), reads=[rD1, rcm], writes=[rD1])
        P.op("act", lambda e: e.activation(out=D1[:], in_=D1[:], func=AF.Exp), reads=[rD1], writes=[rD1])
        P.op("act", lambda e: e.activation(out=D2[:], in_=D2[:], func=AF.Exp), reads=[rD2], writes=[rD2])
        P.op("dve", lambda e: e.tensor_tensor(out=E1s[:], in0=D1[:], in1=bc(Us), op=ALU.mult), reads=[rD1, rcm], writes=[rE1s])
        P.op("dve", lambda e: e.tensor_tensor(out=qd[:], in0=qT[:], in1=eG[:], op=ALU.mult), reads=[rqT, reG], writes=[rqd])
        if d == 0 and gi == 0 and hh == 0:
            G.dump("E1", D1[:], rD1); G.dump("E2s", D2[:], rD2); G.dump("E1s", E1s[:], rE1s); G.dump("kbT", kbT[:], rkbT); G.dump("qd", qd[:], rqd)
        yield
        bC, rbC = bank()
        for n in range(GC):
            P.op("pe", lambda e, n=n: e.matmul(bC[0:64, cs(n)], lhsT=kT[:, cs(n)], rhs=kbT[:, cs(n)], start=True, stop=True), reads=[rkT, rkbT], writes=[rbC])
        P.op("dve", lambda e: e.tensor_tensor(out=AT[:], in0=v3(bC[0:64, 0:GW]), in1=E1s[:], op=ALU.mult), reads=[rbC, rE1s], writes=[rAT])
        bD, rbD = bank()
        for n in range(GC):
            P.op("pe", lambda e, n=n: e.matmul(bD[0:64, cs(n)], lhsT=kbT[:, cs(n)], rhs=kT[:, cs(n)], start=True, stop=True), reads=[rkT, rkbT], writes=[rbD])
        P.op("dve", lambda e: e.tensor_tensor(out=A_[:], in0=v3(bD[0:64, 0:GW]), in1=D2[:], op=ALU.mult), reads=[rbD, rD2], writes=[rA_])
        yield
        bE, rbE = bank()
        for n in range(GC):
            P.op("pe", lambda e, n=n: e.matmul(bE[0:64, cs(n)], lhsT=kT[:, cs(n)], rhs=qT[:, cs(n)], start=True, stop=True), reads=[rkT, rqT], writes=[rbE])
        P.op("dve", lambda e: e.tensor_tensor(out=IT[:], in0=v3(bE[0:64, 0:GW]), in1=D1[:], op=ALU.mult), reads=[rbE, rD1], writes=[rIT])
        P.op("dve", lambda e: e.tensor_tensor(out=Pt[:], in0=bc(I64), in1=AT[:], op=ALU.subtract), reads=[rcm, rAT], writes=[rPt])
        yield
        (M0, rM0), (MT0, rMT0) = Tm["M"][0], Tm["MT"][0]
        b1, rb1 = bank()
        for n in range(GC):
            P.op("pe", lambda e, n=n: e.matmul(b1[0:64, cs(n)], lhsT=AT[:, n, :], rhs=A_[:, n, :], start=True, stop=True), reads=[rAT, rA_], writes=[rb1])
        P.op("act", lambda e: e.activation(out=M0[:], in_=v3(b1[0:64, 0:GW]), func=AF.Copy), reads=[rb1], writes=[rM0])
        b2, rb2 = bank()
        for n in range(GC):
            P.op("pe", lambda e, n=n: e.matmul(b2[0:64, cs(n)], lhsT=A_[:, n, :], rhs=AT[:, n, :], start=True, stop=True), reads=[rAT, rA_], writes=[rb2])
        P.op("dve", lambda e: e.tensor_copy(out=MT0[:], in_=v3(b2[0:64, 0:GW])), reads=[rb2], writes=[rMT0])
        yield
        for k in range(1, 6):
            (Mc, rMc), (MTc, rMTc) = Tm["M"][(k - 1) % 2], Tm["MT"][(k - 1) % 2]
            (Mn, rMn), (MTn, rMTn) = Tm["M"][k % 2], Tm["MT"][k % 2]

            def step(k=k, Mc=Mc, rMc=rMc, MTc=MTc, rMTc=rMTc, Mn=Mn, rMn=rMn, MTn=MTn, rMTn=rMTn):
                b1, rb1 = bank()
                for n in range(GC):
                    P.op("pe", lambda e, n=n: e.matmul(b1[0:64, cs(n)], lhsT=Mc[:, n, :], rhs=Pt[:, n, :], start=True, stop=True), reads=[rMc, rPt], writes=[rb1])
                P.op("dve", lambda e: e.tensor_tensor(out=Pt[:], in0=Pt[:], in1=v3(b1[0:64, 0:GW]), op=ALU.add), reads=[rPt, rb1], writes=[rPt])
                if k < 5:
                    b2, rb2 = bank()
                    for n in range(GC):
                        P.op("pe", lambda e, n=n: e.matmul(b2[0:64, cs(n)], lhsT=MTc[:, n, :], rhs=Mc[:, n, :], start=True, stop=True), reads=[rMc, rMTc], writes=[rb2])
                    P.op("act", lambda e: e.activation(out=Mn[:], in_=v3(b2[0:64, 0:GW]), func=AF.Copy), reads=[rb2], writes=[rMn])
                if k < 4:
                    b3, rb3 = bank()
                    for n in range(GC):
                        P.op("pe", lambda e, n=n: e.matmul(b3[0:64, cs(n)], lhsT=Mc[:, n, :], rhs=MTc[:, n, :], start=True, stop=True), reads=[rMc, rMTc], writes=[rb3])
                    P.op("act", lambda e: e.activation(out=MTn[:], in_=v3(b3[0:64, 0:GW]), func=AF.Copy), reads=[rb3], writes=[rMTn])
            step()
            yield
        P.op("dve", lambda e: e.tensor_tensor(out=vb[:], in0=vtm[:], in1=b_[:, gs].unsqueeze(2).to_broadcast([64, GC, 128]), op=ALU.mult),
             reads=[rvtm, rb_], writes=[rvb])
        P.op("dve", lambda e: e.tensor_tensor(out=kbg[:], in0=ktm[:], in1=x_[:, gs].unsqueeze(2).to_broadcast([64, GC, 128]), op=ALU.mult),
             reads=[rktm, rx_], writes=[rkbg])
        P.op("dve", lambda e: e.tensor_tensor(out=kdec[:], in0=ktm[:], in1=e_[:, gs].unsqueeze(2).to_broadcast([64, GC, 128]), op=ALU.mult),
             reads=[rktm, re_], writes=[rkdec])
        bu, rbu = bank()
        for n in range(GC):
            P.op("pe", lambda e, n=n: e.matmul(bu[0:64, n * 128:(n + 1) * 128], lhsT=Pt[:, n, :], rhs=vb[:, n, :], start=True, stop=True), reads=[rPt, rvb], writes=[rbu])
        P.op("act", lambda e: e.activation(out=flat(u), in_=bu[0:64, 0:GC * 128], func=AF.Copy), reads=[rbu], writes=[ru])
        bw, rbw = bank()
        for n in range(GC):
            P.op("pe", lambda e, n=n: e.matmul(bw[:, cs(n)], lhsT=kbg[:, n, :], rhs=Pt[:, n, :], start=True, stop=True), reads=[rPt, rkbg], writes=[rbw])
        P.op("dve", lambda e: e.tensor_copy(out=flat(wT), in_=bw[:, 0:GW]), reads=[rbw], writes=[rwT])
        if d == 0 and gi == 0 and hh == 0:
            G.dump("AT", AT[:], rAT); G.dump("A", A_[:], rA_); G.dump("IT", IT[:], rIT); G.dump("Pt", Pt[:], rPt); G.dump("u", u[:], ru)
            G.dump("wT", wT[:], rwT); G.dump("kdec", kdec[:], rkdec); G.dump("vb", vb[:], rvb); G.dump("kbg", kbg[:], rkbg)
            G.dump("gl", gl[d][0][:], gl[d][1]); G.dump("Gcol", Gcol[d][0][:], Gcol[d][1]); G.dump("ekd", ekd[d][0][:], ekd[d][1]); G.dump("bg", bg[d][0][:], bg[d][1])
        yield

    def dn_seq(d, gi):
        gmem = order[d][gi]
        n0 = gmem * GC
        S2 = DS[d][gi % 2]
        (IT, rIT), (qd, rqd), (kdec, rkdec), (u, ru), (wT, rwT) = S2["IT"], S2["qd"], S2["kdec"], S2["u"], S2["wT"]
        (l_, rl_) = gl[d]
        (oacc, roacc) = acc_dn[d]
        for n in corder[d]:
            def chunk(n=n):
                c = scur["dn"][d]
                (Sc, rSc), (Sn, rSn) = Sdn[d][c], Sdn[d][1 - c]
                scur["dn"][d] = 1 - c
                (vn, rvn) = vnb[d][scur["vn"][d]]
                scur["vn"][d] ^= 1
                b1, rb1 = bank()
                P.op("pe", lambda e: e.matmul(b1[0:64, 0:128], lhsT=wT[:, n, :], rhs=Sc[:], start=True, stop=True), reads=[rwT, rSc], writes=[rb1])
                P.op("dve", lambda e: e.tensor_tensor(out=vn[:], in0=u[:, n, :], in1=b1[0:64, 0:128], op=ALU.subtract), reads=[ru, rb1], writes=[rvn])
                b3, rb3 = bank()
                P.op("pe", lambda e: e.matmul(b3[:, 0:128], lhsT=kdec[:, n, :], rhs=vn[:], start=True, stop=True), reads=[rkdec, rvn], writes=[rb3])
                P.op("pe", lambda e: e.matmul(oacc[:, cs(n)], lhsT=Sc[:], rhs=qd[:, cs(n)], start=True, stop=False), reads=[rSc, rqd], writes=[roacc])
                P.op("pe", lambda e: e.matmul(oacc[:, cs(n)], lhsT=vn[:], rhs=IT[:, n, :], start=False, stop=True), reads=[rvn, rIT], writes=[roacc])
                P.op("dve", lambda e: e.scalar_tensor_tensor(out=Sn[:], in0=Sc[:], scalar=l_[:, n0 + n:n0 + n + 1], in1=b3[:, 0:128], op0=ALU.mult, op1=ALU.add),
                     reads=[rSc, rl_, rb3], writes=[rSn])
                if d == 0 and gi == 0 and hh == 0:
                    G.dump("vn%d" % n, vn[:], rvn); G.dump("S%d" % n, Sn[:], rSn); G.dump("Sin%d" % n, Sc[:], rSc)
            chunk()
            yield
        (o_, ro_) = ost[0][d]
        P.op("act", lambda e: e.activation(out=o_[:], in_=oacc[:, 0:GW], func=AF.Copy), reads=[roacc], writes=[ro_])
        P.dma("sp", odst(0, d, gmem), o_[:], reads=[ro_])
        yield

    def gl_prep(d, gi):
        n0 = order[d][gi] * GC
        p = gi % 2
        U = CMU[d]
        G, Tm, S2 = GL[d][p], GT[d], GS[d][p]
        (qT, rqT), (kT, rkT), (ktm, rktm), (la, rla) = G["q"], G["k"], G["ktm"], G["la"]
        (gv_, rgv_) = GV[d][gi % 3]
        (b_, rb_), (dd, rdd), (kst, rkst), (bT, rbT), (e1, re1), (eq, req), (ek, rek), (ei, rei), (qtl, rqtl), (ktl, rktl) = (
            Tm["b"], Tm["dd"], Tm["kst"], Tm["bT"], Tm["e1"], Tm["eq"], Tm["ek"], Tm["ei"], Tm["qtl"], Tm["ktl"])
        (qi, rqi), (att, ratt), (KV, rKV), (al, ral) = S2["qi"], S2["att"], S2["KV"], S2["al"]
        bb, rbb = bank()
        P.op("pe", lambda e: e.matmul(bb[0:64, 0:GW], lhsT=U, rhs=flat(la), start=True, stop=True), reads=[rcm, rla], writes=[rbb])
        bl, rbl = bank()
        P.op("pe", lambda e: e.matmul(bl[0:64, 0:GW], lhsT=ones_f[0:64, 0:64], rhs=flat(la), start=True, stop=True), reads=[r_of, rla], writes=[rbl])
        bt, rbt = bank()
        for n in range(GC):
            P.op("pe", lambda e, n=n: e.matmul(bt[0:64, cs(n)], lhsT=la[:, n, :], rhs=U, start=True, stop=True), reads=[rcm, rla], writes=[rbt])
        P.op("act", lambda e: e.activation(out=flat(b_), in_=bb[0:64, 0:GW], func=AF.Copy), reads=[rbb], writes=[rb_])
        P.op("dve", lambda e: e.tensor_tensor(out=flat(dd), in0=bl[0:64, 0:GW], in1=flat(b_), op=ALU.subtract), reads=[rbl, rb_], writes=[rdd])
        P.op("act", lambda e: e.activation(out=dd[:], in_=dd[:], func=AF.Exp), reads=[rdd], writes=[rdd])
        P.op("dve", lambda e: e.tensor_tensor(out=kst[:], in0=ktm[:], in1=dd[:], op=ALU.mult), reads=[rktm, rdd], writes=[rkst])
        P.op("act", lambda e: e.activation(out=flat(bT), in_=bt[0:64, 0:GW], func=AF.Copy), reads=[rbt], writes=[rbT])
        yield
        P.op("pool", lambda e: e.tensor_tensor(out=e1[:], in0=bT[:], in1=bT[:, :, mid_idx[d]:mid_idx[d] + 1].to_broadcast([64, GC, 64]), op=ALU.subtract), reads=[rbT], writes=[re1])
        P.op("act", lambda e: e.activation(out=eq[:], in_=e1[:], func=AF.Exp), reads=[re1], writes=[req])
        P.op("act", lambda e: e.activation(out=ek[:], in_=e1[:], func=AF.Exp, scale=-1.0), reads=[re1], writes=[rek])
        P.op("act", lambda e: e.activation(out=ei[:], in_=bT[:], func=AF.Exp), reads=[rbT], writes=[rei])
        P.op("dve", lambda e: e.tensor_tensor(out=qtl[:], in0=qT[:], in1=flat(eq), op=ALU.mult), reads=[rqT, req], writes=[rqtl])
        P.op("dve", lambda e: e.tensor_tensor(out=ktl[:], in0=kT[:], in1=flat(ek), op=ALU.mult), reads=[rkT, rek], writes=[rktl])
        P.op("pool", lambda e: e.tensor_tensor(out=qi[:], in0=qT[:], in1=flat(ei), op=ALU.mult), reads=[rqT, rei], writes=[rqi])
        P.op("act", lambda e: e.activation(out=al[:], in_=bT[:, :, last_idx[d]], func=AF.Exp), reads=[rbT], writes=[ral])
        yield
        ba, rba = bank()
        for n in range(GC):
            P.op("pe", lambda e, n=n: e.matmul(ba[0:64, cs(n)], lhsT=ktl[:, cs(n)], rhs=qtl[:, cs(n)], start=True, stop=True), reads=[rktl, rqtl], writes=[rba])
        P.op("dve", lambda e: e.tensor_tensor(out=att[:], in0=v3(ba[0:64, 0:GW]), in1=bc(U), op=ALU.mult), reads=[rba, rcm], writes=[ratt])
        bkv, rbkv = bank()
        for n in range(GC):
            P.op("pe", lambda e, n=n: e.matmul(bkv[0:64, n * 128:(n + 1) * 128], lhsT=kst[:, n, :], rhs=gv_[:, n, :], start=True, stop=True), reads=[rkst, rgv_], writes=[rbkv])
        P.op("act", lambda e: e.activation(out=flat(KV), in_=bkv[0:64, 0:GC * 128], func=AF.Copy), reads=[rbkv], writes=[rKV])
        yield

    def gl_seq(d, gi):
        gmem = order[d][gi]
        n0 = gmem * GC
        S2 = GS[d][gi % 2]
        (qi, rqi), (att, ratt), (KV, rKV), (al, ral) = S2["qi"], S2["att"], S2["KV"], S2["al"]
        (gv_, rgv_) = GV[d][gi % 3]
        (oacc, roacc) = acc_gl[d]
        for n in corder[d]:
            def chunk(n=n):
                c = scur["gl"][d]
                (Sc, rSc), (Sn, rSn) = Sgl[d][c], Sgl[d][1 - c]
                scur["gl"][d] = 1 - c
                P.op("pe", lambda e: e.matmul(oacc[:, cs(n)], lhsT=Sc[:], rhs=qi[:, cs(n)], start=True, stop=False), reads=[rSc, rqi], writes=[roacc])
                P.op("pe", lambda e: e.matmul(oacc[:, cs(n)], lhsT=gv_[:, n, :], rhs=att[:, n, :], start=False, stop=True), reads=[rgv_, ratt], writes=[roacc])
                P.op("dve", lambda e: e.scalar_tensor_tensor(out=Sn[:], in0=Sc[:], scalar=al[:, n:n + 1], in1=KV[:, n, :], op0=ALU.mult, op1=ALU.add),
                     reads=[rSc, ral, rKV], writes=[rSn])
            chunk()
            yield
        (o_, ro_) = ost[1][d]
        P.op("dve", lambda e: e.tensor_copy(out=o_[:], in_=oacc[:, 0:GW]), reads=[roacc], writes=[ro_])
        P.dma("sp", odst(1, d, gmem), o_[:], reads=[ro_])
        yield

    def rr(gens):
        gens = list(gens)
        while gens:
            nxt = []
            for g in gens:
                try:
                    next(g)
                    nxt.append(g)
                except StopIteration:
                    pass
            gens = nxt

    loads(0)
    for s_ in range(NG + 1):
        if s_ + 1 < NG:
            loads(s_ + 1)
        gens = []
        if s_ >= 1:
            gens += [dn_seq(0, s_ - 1), dn_seq(1, s_ - 1), gl_seq(0, s_ - 1), gl_seq(1, s_ - 1)]
        if s_ < NG:
            gens += [dn_prep(0, s_), dn_prep(1, s_), gl_prep(0, s_), gl_prep(1, s_)]
        rr(gens)
    K.end()


IN_SHAPES = {
    "xT": [D, SEQ], "ctxT": [D, CTX], "condT": [D, 2], "mod_w": [4, D, 6 * D], "mod_bT": [128, 4, 48],
    "rec_w_in": [2, D, REC_IN], "rec_cw": [2, 128, 3, 12], "dt_bias": [2, 1, 8], "a_log": [2, 1, 8], "nrm": [2, 128, 8],
    "w2": [2, 16, 2, 256], "b2": [2, 1, 512], "rec_w_out": [2, D, D], "att_w_qkv": [2, D, 1536], "gains": [2, 128, 2],
    "att_w_out": [2, D, D], "ffn_w_up": [4, D, 2 * DFF], "ffn_cw": [4, 128, 3, NJ], "ffn_w_down": [4, DFF, D], "fnormT": [128, 8],
    "cosT": [4, 128, LQ], "sinT": [4, 128, LQ], "perm": [128, 128], "ident": [128, 128], "cm": [64, 2, 6, 64],
}


class LazyIn(dict):
    def __init__(self, K):
        super().__init__()
        self.K = K

    def __missing__(self, name):
        if name == "yT":
            ap = self.K.dout("yT", [D, SEQ])
        else:
            ap = self.K.din(name, IN_SHAPES[name])
        self[name] = ap
        return ap


def build_fused(nl=4, ext=None, only=None):
    K = KB(ext)
    P = K.P
    G = GS()
    G.I = LazyIn(K)
    (G.ones_b, G.r_ob), (G.ones_f, G.r_of) = consts(K)
    G.mods, G.rmod = K.gsb([128, 4, 2, 6, 8], F32, "mods")
    G.mod1, G.rmod1 = K.gsb([128, 4, 2, 6, 8], F32, "mod1")
    S = {}
    S["X"] = [{0: K.dram("xl%d" % i, [D, SEQ + 2 * PADC]), 1: K.dram("xc%d" % i, [D, CTX + 2 * PADC])} for i in range(2)]
    S["MIX"] = [{0: K.dram("ml%d" % i, [D, SEQ + 2 * PADC]), 1: K.dram("mc%d" % i, [D, CTX + 2 * PADC])} for i in range(3)]
    S["Q"] = {0: K.dram("ql", [D, SEQ], BF16), 1: K.dram("qc", [D, CTX], BF16)}
    S["KT"] = K.dram("kt", [256, NKEY], BF16)
    S["VT"] = K.dram("vt", [NKEY, 256], BF16)
    S["FQK"] = K.dram("fqk", [1536, NKEY])
    S["DKV"] = K.dram("dkv", [NKEY, 1024])
    S["GKV"] = K.dram("gkv", [NKEY, 768])
    S["LA"] = K.dram("la", [NKEY, 512])
    S["GB"] = K.dram("gb", [NKEY, 16])
    G.S = S
    G.dumps = {}
    if only and "dbg" in only:
        def dump(name, tile_ap, res):
            if name in G.dumps:
                return
            shp = list(tile_ap.shape)
            d_ = K.dout("dbg_" + name, shp)
            G.dumps[name] = d_
            P.dma("sp", d_, tile_ap, reads=[res])
        G.dump = dump
    else:
        G.dump = lambda *a: None
    K.begin()
    z, rz = K.sb([128, 8, PADC], F32, "z")
    P.op("pool", lambda e: e.memset(z[:], 0.0), writes=[rz])
    for arr in S["X"] + S["MIX"]:
        for sid in (0, 1):
            n = arr[sid].shape[1]
            for col in (0, n - PADC):
                P.dma("sp", arr[sid][:, col:col + PADC].rearrange("(k p) n -> p k n", p=128), z[:], reads=[rz])
    if not (only and "noinit" in only):
        for k in range(8):
            P.dma("sp" if k % 2 else "act", S["X"][0][0][k * 128:(k + 1) * 128, PADC:SEQ + PADC], G.I["xT"][k * 128:(k + 1) * 128, :])
        P.dma("sp", S["X"][0][1][:, PADC:CTX + PADC], G.I["ctxT"])
    K.end()
    if not (only and "nomod" in only):
        emit_mod(K, G)
    for l in range(nl):
        last = l == 3
        if l % 2 == 0:
            if not only or "ka" in only:
                for q in range(4):
                    emit_ka_rec(K, G, l, q)
            if not only or "kb" in only:
                for hh in range(1 if (only and "hh0" in only) else 4):
                    emit_kb_rec(K, G, hh, (only or {}).get("ng", NG) if isinstance(only, dict) else NG)
            if only and "kb2" in only:
                for hh in range(4):
                    emit_kb_rec(K, G, hh)
            if not only or "kc" in only:
                for q in range(1 if (only and "q0" in only) else 4):
                    emit_kc(K, G, l, q, True, True, False)
        else:
            for q in range(4):
                emit_ka_att(K, G, l, q)
            for q in range(4):
                emit_kb_att(K, G, q, not last)
            for q in range(4):
                emit_kc(K, G, l, q, False, not last, last)
    nc = K.done()
    return nc, list(G.I.keys())


def rope_tables():
    t = np.arange(SEQ)
    inv = (10000.0 ** (-np.arange(32, dtype=np.float32) / 32)).astype(np.float32)
    row = ((t // 64).astype(np.float32)[:, None] * inv).astype(np.float32)
    col = ((t % 64).astype(np.float32)[:, None] * inv).astype(np.float32)
    cos = np.concatenate([np.cos(row), np.cos(row), np.cos(col), np.cos(col)], 1).astype(np.float32)
    sin = np.concatenate([-np.sin(row), np.sin(row), -np.sin(col), np.sin(col)], 1).astype(np.float32)
    perm = np.zeros((128, 128), np.float32)
    for m in range(128):
        partner = m + 32 if (m // 32) % 2 == 0 else m - 32
        perm[partner, m] = 1.0
    cosq = np.ascontiguousarray(np.stack([cos[q * LQ:(q + 1) * LQ].T for q in range(4)]))
    sinq = np.ascontiguousarray(np.stack([sin[q * LQ:(q + 1) * LQ].T for q in range(4)]))
    return cosq, sinq, perm


def rec_consts():
    i = np.arange(64)
    U = (i[:, None] <= i[None, :]).astype(np.float32)
    Us = (i[:, None] < i[None, :]).astype(np.float32)
    L = U.T.copy()
    Ls = Us.T.copy()
    I_ = np.eye(64, dtype=np.float32)
    d0 = [U, Us, -Ls, (U - 1.0) * 1.0e4, (Ls - 1.0) * 1.0e4, I_]
    d1 = [L, Ls, -Us, (L - 1.0) * 1.0e4, (Us - 1.0) * 1.0e4, I_]
    return np.ascontiguousarray(np.stack([np.stack(d0, 1), np.stack(d1, 1)], 1).astype(np.float32))


def host_inputs(x, c, ctx, c_ctx, mod_w, mod_b, rec_w_in, rec_conv, dn_a_log, dn_dt_bias, dn_norm, gla_w2, gla_b2, gla_norm, rec_w_out,
                att_w_qkv, att_q_norm, att_k_norm, att_w_out, ffn_w_up, ffn_conv, ffn_w_down, final_norm):
    cosq, sinq, perm = rope_tables()
    shared = {
        "mod_w": mod_w, "mod_bT": np.ascontiguousarray(mod_b.reshape(4, 48, 128).transpose(2, 0, 1)),
        "rec_w_in": rec_w_in, "rec_cw": np.ascontiguousarray(rec_conv.reshape(2, 3, 12, 128).transpose(0, 3, 1, 2)),
        "dt_bias": np.ascontiguousarray(dn_dt_bias.reshape(2, 1, 8)), "a_log": np.ascontiguousarray(dn_a_log.reshape(2, 1, 8)),
        "nrm": np.ascontiguousarray(np.stack([np.stack([dn_norm[e]] * 4 + [gla_norm[e]] * 4, 1) for e in range(2)])),
        "w2": np.ascontiguousarray(gla_w2.transpose(0, 2, 1, 3)), "b2": np.ascontiguousarray(gla_b2.reshape(2, 1, 512)),
        "rec_w_out": rec_w_out, "att_w_qkv": att_w_qkv,
        "gains": np.ascontiguousarray(np.stack([att_q_norm, att_k_norm], 2)),
        "att_w_out": att_w_out, "ffn_w_up": ffn_w_up,
        "ffn_cw": np.ascontiguousarray(ffn_conv.reshape(4, 3, NJ, 128).transpose(0, 3, 1, 2)),
        "ffn_w_down": ffn_w_down, "fnormT": np.ascontiguousarray(final_norm.reshape(8, 128).T),
        "cosT": cosq, "sinT": sinq, "perm": perm, "ident": np.eye(128, dtype=np.float32), "cm": rec_consts(),
    }
    per_core = []
    for core in range(NCORES):
        b = core % 2
        m = dict(shared)
        m["xT"] = np.ascontiguousarray(x[b].T)
        m["ctxT"] = np.ascontiguousarray(ctx[b].T)
        m["condT"] = np.ascontiguousarray(np.stack([c[b], c_ctx], 1))
        per_core.append(m)
    return per_core


_CACHE = {}


def kernel(x, c, ctx, c_ctx, mod_w, mod_b, rec_w_in, rec_conv, dn_a_log, dn_dt_bias, dn_norm, gla_w2, gla_b2, gla_norm, rec_w_out,
           att_w_qkv, att_q_norm, att_k_norm, att_w_out, ffn_w_up, ffn_conv, ffn_w_down, final_norm):
    f = lambda a: np.ascontiguousarray(np.asarray(a, dtype=np.float32))
    args = list(map(f, (x, c, ctx, c_ctx, mod_w, mod_b, rec_w_in, rec_conv, dn_a_log, dn_dt_bias, dn_norm, gla_w2, gla_b2, gla_norm, rec_w_out,
                        att_w_qkv, att_q_norm, att_k_norm, att_w_out, ffn_w_up, ffn_conv, ffn_w_down, final_norm)))
    if "nc" not in _CACHE:
        _CACHE["nc"] = build_fused(4)
    nc, names = _CACHE["nc"]
    full = host_inputs(*args)
    in_maps = [{k: m[k] for k in names if k != "yT"} for m in full]
    res = run_bass_kernel_spmd(nc, in_maps, core_ids=list(range(NCORES))).results
    return np.ascontiguousarray(np.stack([res[0]["yT"].T, res[1]["yT"].T]))
```

```python
import numpy as np
from contextlib import ExitStack
import concourse.bass as bass
import concourse.mybir as mybir
from concourse.bass_utils import run_bass_kernel_spmd

F32 = mybir.dt.float32
BF16 = mybir.dt.bfloat16
ALU = mybir.AluOpType
AF = mybir.ActivationFunctionType

NCORES = 8
D = 1024
SEQ = 8192
CTX = 256
LQ = SEQ // 4
CQ = CTX // 4
DFF = 2816
NJ = DFF // 128
EPS = 1e-6


class Res:
    __slots__ = ("w", "r", "x")

    def __init__(self, x=False):
        self.w = None
        self.r = {}
        self.x = x


class Prog:
    ENG = ("pe", "act", "dve", "pool", "sp")

    def __init__(self, nc, stack, ndma=24):
        self.nc = nc
        self.ops = {e: [] for e in self.ENG}
        self.cnt = {e: 0 for e in self.ENG}
        self.esem = {e: stack.enter_context(nc.semaphore("s_" + e)) for e in self.ENG}
        self.dsem = [stack.enter_context(nc.semaphore("d_%d" % i)) for i in range(ndma)]
        self.dcnt = [0] * ndma
        self.dnext = 0
        self.seen = {e: {} for e in self.ENG}

    def _sem(self, k):
        if isinstance(k, tuple):
            return self.dsem[k[1]], 16
        return self.esem[k], 1

    def _waits(self, eng, reads, writes, extra=(), is_dma=False):
        deps = {}

        def add(k, v):
            if deps.get(k, 0) < v:
                deps[k] = v

        for r in reads:
            if r.w is not None and not (r.w[0] == eng and eng == "pe"):
                add(*r.w)
            if r.x:
                for k, v in r.r.items():
                    if k != eng:
                        add(k, v)
        for w in writes:
            if w.w is not None and (is_dma or w.w[0] != eng):
                add(*w.w)
            for k, v in w.r.items():
                if is_dma or k != eng:
                    add(k, v)
        for k, v in extra:
            add(k, v)
        waits = []
        seen = self.seen[eng]
        for k, v in deps.items():
            if seen.get(k, 0) >= v:
                continue
            seen[k] = v
            sem, mul = self._sem(k)
            waits.append((sem, v * mul))
        return waits

    def op(self, eng, fn, reads=(), writes=()):
        waits = self._waits(eng, reads, writes)
        self.cnt[eng] += 1
        seq = self.cnt[eng]
        self.ops[eng].append((waits, fn, (self.esem[eng], 1)))
        for r in reads:
            r.r[eng] = seq
        for w in writes:
            w.w = (eng, seq)
            w.r = {}

    def dma(self, q, out, in_, reads=(), writes=(), slow=False):
        i = self.dnext
        self.dnext = (self.dnext + 1) % len(self.dsem)
        key = ("d", i)
        extra = ((key, self.dcnt[i]),) if self.dcnt[i] else ()
        waits = self._waits(q, reads, writes, extra, is_dma=True)
        self.dcnt[i] += 1
        seq = self.dcnt[i]
        if slow:
            self.ops[q].append((waits, lambda e: e.dma_start(out=out, in_=in_, allow_slow_non_contiguous=True), (self.dsem[i], 16)))
        else:
            self.ops[q].append((waits, lambda e: e.dma_start(out=out, in_=in_), (self.dsem[i], 16)))
        for r in reads:
            r.r[key] = seq
        for w in writes:
            w.w = (key, seq)
            w.r = {}

    def barrier(self):
        for e in self.ENG:
            waits = []
            seen = self.seen[e]
            for k in self.ENG:
                if k != e and k != "sp" and self.cnt[k] > seen.get(k, 0):
                    seen[k] = self.cnt[k]
                    waits.append((self.esem[k], self.cnt[k]))
            for i, c in enumerate(self.dcnt):
                k = ("d", i)
                if c > seen.get(k, 0):
                    seen[k] = c
                    waits.append((self.dsem[i], 16 * c))
            self.ops[e].append((waits, None, None))

    def finish(self):
        waits = [(self.dsem[i], 16 * c) for i, c in enumerate(self.dcnt) if c]
        self.ops["sp"].append((waits, None, None))

    def emit(self):
        with self.nc.Block() as block:
            def mk(e):
                def run(eng):
                    for waits, fn, inc in self.ops[e]:
                        for sem, val in waits:
                            eng.wait_ge(sem, val)
                        if fn is not None:
                            r_ = fn(eng)
                            if inc is not None:
                                r_.then_inc(*inc)
                return run
            block.tensor(mk("pe"))
            block.scalar(mk("act"))
            block.vector(mk("dve"))
            block.gpsimd(mk("pool"))
            block.sync(mk("sp"))


class KB:
    def __init__(self, ext=None):
        self.nc = bass.Bass("TRN2", target_bir_lowering=False, num_devices=NCORES)
        self.st = ExitStack()
        self.cur = self.st
        self.P = Prog(self.nc, self.st)
        self.banks = [self.st.enter_context(self.nc.psum_tensor("bank%d" % i, [128, 512], F32)) for i in range(8)]
        self.rbank = [Res(True) for _ in range(8)]
        self.bi = 0
        self.n = 0
        self.ext = ext or {}

    def din(self, name, shape, dt=F32):
        return self.nc.dram_tensor(name, list(shape), dt, kind="ExternalInput").ap()

    def dout(self, name, shape, dt=F32):
        return self.nc.dram_tensor(name, list(shape), dt, kind="ExternalOutput").ap()

    def dram(self, name, shape, dt=F32):
        kind = {"in": "ExternalInput", "out": "ExternalOutput"}.get(self.ext.get(name), "Internal")
        return self.nc.dram_tensor(name, list(shape), dt, kind=kind).ap()

    def sb(self, shape, dt=F32, name=None):
        self.n += 1
        t = self.cur.enter_context(self.nc.sbuf_tensor("%s_%d" % (name or "t", self.n), list(shape), dt))
        return t, Res()

    def gsb(self, shape, dt=F32, name=None):
        self.n += 1
        t = self.st.enter_context(self.nc.sbuf_tensor("%s_%d" % (name or "g", self.n), list(shape), dt))
        return t, Res()

    def begin(self):
        self.cur = ExitStack()
        self.bi = 0

    def end(self):
        self.P.barrier()
        self.cur.close()
        self.cur = self.st

    def bank(self):
        i = self.bi
        self.bi = (self.bi + 1) % 8
        return self.banks[i], self.rbank[i]

    def done(self):
        self.P.finish()
        self.P.emit()
        self.st.close()
        return self.nc


def split(a, b, maxn=512):
    n = b - a
    k = (n + maxn - 1) // maxn
    base, rem = divmod(n, k)
    out = []
    s = a
    for i in range(k):
        e = s + base + (1 if i < rem else 0)
        out.append((s, e))
        s = e
    return out


class GS:
    pass


QN = {0: LQ, 1: CQ}
PADC = 16


def halo_ap(X, sid, q):
    return X[sid][:, PADC - 1 + q * QN[sid]:PADC + (q + 1) * QN[sid] + 1]


def inte_ap(X, sid, q):
    return X[sid][:, PADC + q * QN[sid]:PADC + (q + 1) * QN[sid]]


class WLoader:
    def __init__(self, K, shape, n=2, name="stg"):
        self.K = K
        self.stg = [K.sb(shape, F32, "%s%d" % (name, i)) for i in range(n)]
        self.i = 0

    def load(self, dst_ap, rdst, src_ap, view):
        P = self.K.P
        (t, rt) = self.stg[self.i % len(self.stg)]
        q = "sp" if self.i % 2 == 0 else "act"
        eng = "dve" if self.i % 3 == 0 else "act"
        self.i += 1
        sv = view(t)
        P.dma(q, sv, src_ap, writes=[rt])
        if eng == "act":
            P.op("act", lambda e: e.activation(out=dst_ap, in_=sv, func=AF.Copy), reads=[rt], writes=[rdst])
        else:
            P.op("dve", lambda e: e.tensor_copy(out=dst_ap, in_=sv), reads=[rt], writes=[rdst])


def consts(K):
    P = K.P
    ones_b, r1 = K.gsb([128, 128], BF16, "ones_b")
    ones_f, r2 = K.gsb([128, 128], F32, "ones_f")
    P.op("pool", lambda e: e.memset(ones_b[:], 1.0), writes=[r1])
    P.op("pool", lambda e: e.memset(ones_f[:], 1.0), writes=[r2])
    return (ones_b, r1), (ones_f, r2)


def rms_modulate(K, x, rx, a, b, ones_b, r_ones, s1, sh, rmod, h, rh, hoff, scr):
    P = K.P
    (sq, rsq), (t, rt), (rstd, rrs) = scr
    n = b - a
    P.op("act", lambda e: e.activation(out=sq[:, :, 0:n], in_=x[:, :, a:b], func=AF.Square), reads=[rx], writes=[rsq])
    bk, rb = K.bank()
    for k in range(8):
        P.op("pe", lambda e, k=k: e.matmul(bk[:, 0:n], lhsT=ones_b[:], rhs=sq[:, k, 0:n], start=(k == 0), stop=(k == 7)),
             reads=[r_ones, rsq], writes=[rb])
    P.op("act", lambda e: e.activation(out=rstd[:, 0:n], in_=bk[:, 0:n], func=AF.Sqrt, scale=1.0 / D, bias=EPS),
         reads=[rb], writes=[rrs])
    P.op("dve", lambda e: e.reciprocal(out=rstd[:, 0:n], in_=rstd[:, 0:n]), reads=[rrs], writes=[rrs])
    P.op("dve", lambda e: e.tensor_tensor(out=t[:, :, 0:n], in0=x[:, :, a:b],
                                          in1=rstd[:, 0:n].unsqueeze(1).to_broadcast([128, 8, n]), op=ALU.mult),
         reads=[rx, rrs], writes=[rt])
    for k in range(8):
        if k % 2:
            P.op("act", lambda e, k=k: e.activation(out=h[:, k, hoff:hoff + n], in_=t[:, k, 0:n], func=AF.Identity, scale=s1(k), bias=sh(k)),
                 reads=[rt, rmod], writes=[rh])
        else:
            P.op("dve", lambda e, k=k: e.tensor_scalar(out=h[:, k, hoff:hoff + n], in0=t[:, k, 0:n], scalar1=s1(k), scalar2=sh(k),
                                                       op0=ALU.mult, op1=ALU.add),
                 reads=[rt, rmod], writes=[rh])


def rms_scratch(K):
    return (K.sb([128, 8, 512], BF16, "rs_sq"), K.sb([128, 8, 512], F32, "rs_t"), K.sb([128, 512], F32, "rs_rstd"))


def emit_mod(K, G):
    P = K.P
    I = G.I
    K.begin()
    cs, rcs = K.sb([128, 8, 2], F32)
    sg, rsg = K.sb([128, 8, 2], F32)
    bs, rbs = K.sb([128, 4, 48], F32)
    P.dma("sp", cs[:], I["condT"].rearrange("(k p) r -> p k r", p=128), writes=[rcs])
    P.dma("sp", bs[:], I["mod_bT"], writes=[rbs])
    P.op("act", lambda e: e.activation(out=sg[:], in_=cs[:], func=AF.Silu), reads=[rcs], writes=[rsg])
    wt = [K.sb([128, 8, 512], F32, "wt%d" % i) for i in range(3)]
    it = 0
    for l in range(4):
        for g in range(12):
            w, rw = wt[it % 3]
            it += 1
            P.dma("sp" if it % 2 else "act", w[:], I["mod_w"][l][:, g * 512:(g + 1) * 512].rearrange("(k p) c -> p k c", p=128), writes=[rw])
            for jj in range(4):
                j = g * 4 + jj
                bk, rb = K.bank()
                for k in range(8):
                    P.op("pe", lambda e, k=k, jj=jj, w=w, bk=bk: e.matmul(bk[:, 0:2], lhsT=w[:, k, jj * 128:(jj + 1) * 128], rhs=sg[:, k, :],
                                                                         start=(k == 0), stop=(k == 7)),
                         reads=[rw, rsg], writes=[rb])
                P.op("dve", lambda e, j=j, l=l, bk=bk: e.tensor_scalar(out=G.mods[:, l, :, j // 8, j % 8], in0=bk[:, 0:2], scalar1=bs[:, l, j:j + 1],
                                                                       scalar2=None, op0=ALU.add),
                     reads=[rb, rbs], writes=[G.rmod])
    P.op("pool", lambda e: e.tensor_scalar(out=G.mod1[:], in0=G.mods[:], scalar1=1.0, scalar2=None, op0=ALU.add), reads=[G.rmod], writes=[G.rmod1])
    K.end()


def layout(with_ctx):
    if with_ctx:
        segs = [(1, 0, CQ + 2), (0, CQ + 2, CQ + 2 + LQ + 2)]
    else:
        segs = [(0, 0, LQ + 2)]
    return segs, segs[-1][2]


def emit_kc(K, G, l, q, rec, with_ctx, final):
    P = K.P
    I = G.I
    S = G.S
    K.begin()
    segs, NT = layout(with_ctx)
    Xin, Xout = S["X"][l % 2], S["X"][(l + 1) % 2]
    MIX = S["MIX"]
    w_out = I["rec_w_out"][l // 2] if rec else I["att_w_out"][l // 2]
    w_up = I["ffn_w_up"][l]
    cw = I["ffn_cw"][l]
    w_down = I["ffn_w_down"][l]
    (ones_b, r_ob), (ones_f, r_of) = (G.ones_b, G.r_ob), (G.ones_f, G.r_of)
    mods, rmod, mod1, rmod1 = G.mods[:, l], G.rmod, G.mod1[:, l], G.rmod1
    x, rx = K.sb([128, 8, NT], F32, "x")
    cws, rcw = K.sb([128, 3, NJ], F32, "cws")
    fns, rfn = K.sb([128, 8], F32, "fns")
    for (sid, s0, s1) in segs:
        P.dma("sp", x[:, :, s0:s1], halo_ap(Xin, sid, q).rearrange("(k p) n -> p k n", p=128), writes=[rx])
    P.dma("sp", cws[:], cw, writes=[rcw])
    P.dma("sp", fns[:], I["fnormT"], writes=[rfn])

    with ExitStack() as ph:
        def sbp(shape, dt, name):
            K.n += 1
            return ph.enter_context(K.nc.sbuf_tensor("%s_%d" % (name, K.n), list(shape), dt)), Res()
        MT, rMT = sbp([128, 8, NT], BF16, "MT")
        if rec:
            nrs, rnr = sbp([128, 8], F32, "nrs")
            P.dma("sp", nrs[:], I["nrm"][l // 2], writes=[rnr])
            bufs = [[sbp([128, NT], F32, "mg%d_%d" % (i, q)) for q in range(3)] for i in range(2)]
            sq, rsq = sbp([128, NT], BF16, "mg_sq")
            rs, rrs = sbp([128, NT], F32, "mg_rs")
            for kt in range(8):
                (f, rf), (b_, rb_), (z, rz) = bufs[kt % 2]
                for (sid, s0, s1) in segs:
                    P.dma("sp", f[:, s0:s1], halo_ap(MIX[0], sid, q)[kt * 128:(kt + 1) * 128, :], writes=[rf])
                    P.dma("act", b_[:, s0:s1], halo_ap(MIX[1], sid, q)[kt * 128:(kt + 1) * 128, :], writes=[rb_])
                    P.dma("sp", z[:, s0:s1], halo_ap(MIX[2], sid, q)[kt * 128:(kt + 1) * 128, :], writes=[rz])
                P.op("dve", lambda e, f=f, b_=b_: e.tensor_tensor(out=f[:], in0=f[:], in1=b_[:], op=ALU.add), reads=[rf, rb_], writes=[rf])
                P.op("act", lambda e, f=f: e.activation(out=sq[:], in_=f[:], func=AF.Square), reads=[rf], writes=[rsq])
                for (a, b) in split(0, NT):
                    bk, rbk = K.bank()
                    P.op("pe", lambda e, a=a, b=b, bk=bk: e.matmul(bk[:, 0:b - a], lhsT=ones_b[:], rhs=sq[:, a:b], start=True, stop=True),
                         reads=[r_ob, rsq], writes=[rbk])
                    P.op("act", lambda e, a=a, b=b, bk=bk: e.activation(out=rs[:, a:b], in_=bk[:, 0:b - a], func=AF.Sqrt, scale=1.0 / 128, bias=EPS),
                         reads=[rbk], writes=[rrs])
                P.op("dve", lambda e: e.reciprocal(out=rs[:], in_=rs[:]), reads=[rrs], writes=[rrs])
                P.op("dve", lambda e, f=f: e.tensor_tensor(out=f[:], in0=f[:], in1=rs[:], op=ALU.mult), reads=[rf, rrs], writes=[rf])
                P.op("dve", lambda e, f=f, z=z, kt=kt: e.scalar_tensor_tensor(out=MT[:, kt, :], in0=f[:], scalar=nrs[:, kt:kt + 1], in1=z[:],
                                                                              op0=ALU.mult, op1=ALU.mult),
                     reads=[rf, rz, rnr], writes=[rMT])
        else:
            abuf = [sbp([128, NT], F32, "mga%d" % i) for i in range(3)]
            for kt in range(8):
                (f, rf) = abuf[kt % 3]
                for (sid, s0, s1) in segs:
                    P.dma("sp" if kt % 2 else "act", f[:, s0:s1], halo_ap(MIX[0], sid, q)[kt * 128:(kt + 1) * 128, :], writes=[rf])
                if kt % 2:
                    P.op("act", lambda e, f=f, kt=kt: e.activation(out=MT[:, kt, :], in_=f[:], func=AF.Copy), reads=[rf], writes=[rMT])
                else:
                    P.op("dve", lambda e, f=f, kt=kt: e.tensor_copy(out=MT[:, kt, :], in_=f[:]), reads=[rf], writes=[rMT])
        wo = [sbp([128, 8, 128], BF16, "wo%d" % i) for i in range(2)]
        keep = K.cur
        K.cur = ph
        wl1 = WLoader(K, [128, 8, 128], 2, "wos")
        K.cur = keep
        for m in range(8):
            w, rw = wo[m % 2]
            wl1.load(w[:], rw, w_out[:, m * 128:(m + 1) * 128].rearrange("(k p) c -> p k c", p=128), lambda t: t[:])
            for (sid, s0, s1) in segs:
                for (a, b) in split(s0, s1):
                    bk, rbk = K.bank()
                    for k in range(8):
                        P.op("pe", lambda e, k=k, a=a, b=b, w=w, bk=bk: e.matmul(bk[:, 0:b - a], lhsT=w[:, k, :], rhs=MT[:, k, a:b],
                                                                                start=(k == 0), stop=(k == 7)),
                             reads=[rw, rMT], writes=[rbk])
                    P.op("dve", lambda e, a=a, b=b, m=m, sid=sid, bk=bk: e.scalar_tensor_tensor(
                        out=x[:, m, a:b], in0=bk[:, 0:b - a], scalar=mods[:, sid, 2, m:m + 1], in1=x[:, m, a:b], op0=ALU.mult, op1=ALU.add),
                        reads=[rbk, rmod, rx], writes=[rx])
        P.barrier()

    (lsid, l0, l1) = segs[-1]
    half = LQ // 2
    passes = [[(lsid, l0, l0 + half + 2)], [(lsid, l0 + half, l1)]]
    if with_ctx:
        passes[0].insert(0, segs[0])
    PL = max(sum(p[2] - p[1] for p in ps) for ps in passes)
    h, rh = K.sb([128, 8, PL], BF16, "h")
    aT, raT = K.sb([128, NJ, PL], BF16, "aT")
    ecnt = [0]

    def evac_eng():
        ecnt[0] += 1
        return "act" if ecnt[0] % 2 else "dve"

    hc = l0 + half
    xsave, rxs = K.sb([128, 8, 1], F32, "xsave")
    xupd, rxu = K.sb([128, 8, 1], F32, "xupd")
    P.op("pool", lambda e: e.tensor_copy(out=xsave[:], in_=x[:, :, hc:hc + 1]), reads=[rx], writes=[rxs])
    for pi, ps in enumerate(passes):
        offs = []
        o = 0
        for (sid, a, b) in ps:
            offs.append(o)
            o += b - a
        outer = K.cur
        K.cur = ExitStack()
        scr = rms_scratch(K)
        if pi == 1:
            P.op("pool", lambda e: e.tensor_copy(out=xupd[:], in_=x[:, :, hc:hc + 1]), reads=[rx], writes=[rxu])
            P.op("pool", lambda e: e.tensor_copy(out=x[:, :, hc:hc + 1], in_=xsave[:]), reads=[rxs], writes=[rx])
        for (sid, a, b), o in zip(ps, offs):
            for (ba, bb) in split(a, b):
                rms_modulate(K, x, rx, ba, bb, ones_b, r_ob,
                             lambda k, sid=sid: mod1[:, sid, 4, k:k + 1], lambda k, sid=sid: mods[:, sid, 3, k:k + 1],
                             rmod1, h, rh, o + ba - a, scr)
        if pi == 1:
            P.op("pool", lambda e: e.tensor_copy(out=x[:, :, hc:hc + 1], in_=xupd[:]), reads=[rxu], writes=[rx])
        P.barrier()
        K.cur.close()
        K.cur = ExitStack()
        gb = [K.sb([128, PL], F32, "gb%d" % i) for i in range(2)]
        cb = [K.sb([128, PL], F32, "cb%d" % i) for i in range(2)]
        vb = [K.sb([128, PL], BF16, "vb%d" % i) for i in range(2)]
        wu = [K.sb([128, 8, 256], BF16, "wu%d" % i) for i in range(3)]
        wd = [K.sb([128, NJ, 128], BF16, "wd%d" % i) for i in range(2)]
        HJ = NJ // 2
        wl = WLoader(K, [128, HJ * 128], 3, "wst")

        def load_wu(j):
            w, rw = wu[j % 3]
            wl.load(w[:, :, 0:128], rw, w_up[:, j * 128:(j + 1) * 128].rearrange("(k p) c -> p k c", p=128),
                    lambda t: t[:, 0:1024].rearrange("p (k c) -> p k c", k=8))
            wl.load(w[:, :, 128:256], rw, w_up[:, DFF + j * 128:DFF + (j + 1) * 128].rearrange("(k p) c -> p k c", p=128),
                    lambda t: t[:, 0:1024].rearrange("p (k c) -> p k c", k=8))

        def gate_stage(j):
            w, rw = wu[j % 3]
            g, rg = gb[j % 2]
            for (sid, a, b), o in zip(ps, offs):
                n = b - a
                for (ba, bb) in split(0, n):
                    bk, rbk = K.bank()
                    for k in range(8):
                        P.op("pe", lambda e, k=k, ba=ba, bb=bb, o=o, w=w, bk=bk: e.matmul(bk[:, 0:bb - ba], lhsT=w[:, k, 0:128], rhs=h[:, k, o + ba:o + bb],
                                                                                         start=(k == 0), stop=(k == 7)),
                             reads=[rw, rh], writes=[rbk])
                    P.op("act", lambda e, ba=ba, bb=bb, o=o, g=g, bk=bk: e.activation(out=g[:, o + ba:o + bb], in_=bk[:, 0:bb - ba], func=AF.Copy),
                         reads=[rbk], writes=[rg])

        def rest_stage(j):
            w, rw = wu[j % 3]
            g, rg = gb[j % 2]
            c, rc = cb[j % 2]
            v_, rv_ = vb[j % 2]
            for (sid, a, b), o in zip(ps, offs):
                n = b - a
                (bs0, bs1) = [(q_[1], q_[2]) for q_ in segs if q_[0] == sid][0]
                for (col, cond) in ((o, a == bs0 and q == 0), (o + n - 1, b == bs1 and q == 3)):
                    if cond:
                        P.op("pool", lambda e, col=col: e.memset(g[:, col:col + 1], 0.0), reads=[rg], writes=[rg])
                P.op("dve", lambda e, o=o, n=n: e.tensor_scalar(out=c[:, o + 1:o + n - 1], in0=g[:, o:o + n - 2],
                                                               scalar1=cws[:, 0, j:j + 1], scalar2=None, op0=ALU.mult),
                     reads=[rg, rcw], writes=[rc])
                P.op("dve", lambda e, o=o, n=n: e.scalar_tensor_tensor(out=c[:, o + 1:o + n - 1], in0=g[:, o + 1:o + n - 1],
                                                                      scalar=cws[:, 1, j:j + 1], in1=c[:, o + 1:o + n - 1],
                                                                      op0=ALU.mult, op1=ALU.add),
                     reads=[rg, rc, rcw], writes=[rc])
                P.op("dve", lambda e, o=o, n=n: e.scalar_tensor_tensor(out=c[:, o + 1:o + n - 1], in0=g[:, o + 2:o + n],
                                                                      scalar=cws[:, 2, j:j + 1], in1=c[:, o + 1:o + n - 1],
                                                                      op0=ALU.mult, op1=ALU.add),
                     reads=[rg, rc, rcw], writes=[rc])
                P.op("act", lambda e, o=o, n=n: e.activation(out=c[:, o + 1:o + n - 1], in_=c[:, o + 1:o + n - 1], func=AF.Silu),
                     reads=[rc], writes=[rc])
                for (ba, bb) in split(1, n - 1):
                    bk, rbk = K.bank()
                    for k in range(8):
                        P.op("pe", lambda e, k=k, ba=ba, bb=bb, o=o, bk=bk: e.matmul(bk[:, 0:bb - ba], lhsT=w[:, k, 128:256], rhs=h[:, k, o + ba:o + bb],
                                                                                    start=(k == 0), stop=(k == 7)),
                             reads=[rw, rh], writes=[rbk])
                    P.op("act", lambda e, ba=ba, bb=bb, o=o, bk=bk: e.activation(out=v_[:, o + ba:o + bb], in_=bk[:, 0:bb - ba], func=AF.Copy),
                         reads=[rbk], writes=[rv_])
                P.op("dve", lambda e, o=o, n=n: e.tensor_tensor(out=aT[:, j, o + 1:o + n - 1], in0=c[:, o + 1:o + n - 1], in1=v_[:, o + 1:o + n - 1], op=ALU.mult),
                     reads=[rc, rv_], writes=[raT])

        load_wu(0)
        load_wu(1)
        gate_stage(0)
        for j in range(NJ):
            if j + 2 < NJ:
                load_wu(j + 2)
            if j + 1 < NJ:
                gate_stage(j + 1)
            rest_stage(j)
        for m in range(8):
            w, rw = wd[m % 2]
            for hf in range(2):
                wl.load(w[:, hf * HJ:(hf + 1) * HJ, :], rw, w_down[hf * HJ * 128:(hf + 1) * HJ * 128, m * 128:(m + 1) * 128].rearrange("(j p) c -> p j c", p=128),
                        lambda t: t[:].rearrange("p (j c) -> p j c", j=HJ))
            for (sid, a, b), o in zip(ps, offs):
                n = b - a
                for (ba, bb) in split(1, n - 1):
                    bk, rbk = K.bank()
                    for j in range(NJ):
                        P.op("pe", lambda e, j=j, ba=ba, bb=bb, o=o, w=w, bk=bk: e.matmul(bk[:, 0:bb - ba], lhsT=w[:, j, :], rhs=aT[:, j, o + ba:o + bb],
                                                                                         start=(j == 0), stop=(j == NJ - 1)),
                             reads=[rw, raT], writes=[rbk])
                    P.op("dve", lambda e, ba=ba, bb=bb, a=a, m=m, sid=sid, bk=bk: e.scalar_tensor_tensor(
                        out=x[:, m, a + ba:a + bb], in0=bk[:, 0:bb - ba], scalar=mods[:, sid, 5, m:m + 1], in1=x[:, m, a + ba:a + bb],
                        op0=ALU.mult, op1=ALU.add),
                        reads=[rbk, rmod, rx], writes=[rx])
        P.barrier()
        K.cur.close()
        K.cur = outer

    for (sid, s0, s1) in segs:
        if final:
            yv = I["yT"].rearrange("(k p) n -> p k n", p=128)
            (sq, rsq), (t, rt), (rstd, rrs) = rms_scratch(K)
            for (a, b) in split(s0 + 1, s1 - 1):
                n = b - a

                def fblk(a=a, b=b, n=n):
                    P.op("act", lambda e: e.activation(out=sq[:, :, 0:n], in_=x[:, :, a:b], func=AF.Square), reads=[rx], writes=[rsq])
                    bk, rbk = K.bank()
                    for k in range(8):
                        P.op("pe", lambda e, k=k: e.matmul(bk[:, 0:n], lhsT=ones_b[:], rhs=sq[:, k, 0:n], start=(k == 0), stop=(k == 7)),
                             reads=[r_ob, rsq], writes=[rbk])
                    P.op("act", lambda e: e.activation(out=rstd[:, 0:n], in_=bk[:, 0:n], func=AF.Sqrt, scale=1.0 / D, bias=EPS), reads=[rbk], writes=[rrs])
                    P.op("dve", lambda e: e.reciprocal(out=rstd[:, 0:n], in_=rstd[:, 0:n]), reads=[rrs], writes=[rrs])
                    P.op("dve", lambda e: e.tensor_tensor(out=t[:, :, 0:n], in0=x[:, :, a:b],
                                                          in1=rstd[:, 0:n].unsqueeze(1).to_broadcast([128, 8, n]), op=ALU.mult),
                         reads=[rx, rrs], writes=[rt])
                    P.op("dve", lambda e: e.tensor_tensor(out=t[:, :, 0:n], in0=t[:, :, 0:n],
                                                          in1=fns[:].unsqueeze(2).to_broadcast([128, 8, n]), op=ALU.mult),
                         reads=[rt, rfn], writes=[rt])
                    P.dma("sp", yv[:, :, q * LQ + a - s0 - 1:q * LQ + b - s0 - 1], t[:, :, 0:n], reads=[rt])
                fblk()
        else:
            P.dma("sp", inte_ap(Xout, sid, q).rearrange("(k p) n -> p k n", p=128), x[:, :, s0 + 1:s1 - 1], reads=[rx])
    K.end()


NA = CQ + LQ
HD = 128
NKEY = CTX + SEQ
NKT = NKEY // 128


def emit_ka_att(K, G, l, q):
    P = K.P
    I = G.I
    S = G.S
    K.begin()
    Xin = S["X"][l % 2]
    w_qkv = I["att_w_qkv"][l // 2]
    (ones_b, r_ob), (ones_f, r_of) = (G.ones_b, G.r_ob), (G.ones_f, G.r_of)
    mods, rmod, mod1, rmod1 = G.mods[:, l], G.rmod, G.mod1[:, l], G.rmod1
    x, rx = K.sb([128, 8, NA], F32, "x")
    h, rh = K.sb([128, 8, NA], BF16, "h")
    gs, rgs = K.sb([128, 2], F32, "gs")
    cs, rcs = K.sb([128, LQ], F32, "cs")
    sn, rsn = K.sb([128, LQ], F32, "sn")
    pm, rpm = K.sb([128, 128], F32, "pm")
    P.dma("sp", x[:, :, 0:CQ], inte_ap(Xin, 1, q).rearrange("(k p) n -> p k n", p=128), writes=[rx])
    P.dma("sp", x[:, :, CQ:NA], inte_ap(Xin, 0, q).rearrange("(k p) n -> p k n", p=128), writes=[rx])
    P.dma("sp", gs[:], I["gains"][l // 2], writes=[rgs])
    P.dma("act", cs[:], I["cosT"][q], writes=[rcs])
    P.dma("act", sn[:], I["sinT"][q], writes=[rsn])
    P.dma("sp", pm[:], I["perm"], writes=[rpm])
    base = {1: 0, 0: CQ}
    koff = {1: 0, 0: CTX}

    def dst(m, sid, a, b):
        c0 = q * QN[sid] + a - base[sid]
        if m < 8:
            return S["Q"][sid][m * 128:(m + 1) * 128, c0:c0 + b - a]
        return S["KT"][(m - 8) * 128:(m - 7) * 128, koff[sid] + c0:koff[sid] + c0 + b - a]

    scr = rms_scratch(K)
    blocks = [(1, 0, CQ)] + [(0, a, b) for (a, b) in split(CQ, NA)]
    for (sid, a, b) in blocks:
        rms_modulate(K, x, rx, a, b, ones_b, r_ob, lambda k, sid=sid: mod1[:, sid, 1, k:k + 1], lambda k, sid=sid: mods[:, sid, 0, k:k + 1],
                     rmod1, h, rh, a, scr)
    wq = [K.sb([128, 8, 128], BF16, "wq%d" % i) for i in range(2)]
    sq = [K.sb([128, 512], BF16, "sq%d" % i) for i in range(2)]
    rs = [K.sb([128, 512], F32, "rs%d" % i) for i in range(2)]
    qn = [K.sb([128, 512], F32, "qn%d" % i) for i in range(2)]
    t1 = [K.sb([128, 512], F32, "t1%d" % i) for i in range(2)]
    t2 = [K.sb([128, 512], F32, "t2%d" % i) for i in range(2)]
    o16 = [K.sb([128, 512], BF16, "o16%d" % i) for i in range(2)]
    it = 0
    wl = WLoader(K, [128, 8, 256], 2, "wqs")
    for m in range(10):
        w, rw = wq[m % 2]
        wl.load(w[:], rw, w_qkv[:, m * 128:(m + 1) * 128].rearrange("(k p) c -> p k c", p=128), lambda t: t[:, :, 0:128])
        for (sid, a, b) in blocks:
            n = b - a
            it += 1
            bk, rbk = K.bank()
            for k in range(8):
                P.op("pe", lambda e, k=k, a=a, b=b, w=w, bk=bk: e.matmul(bk[:, 0:b - a], lhsT=w[:, k, :], rhs=h[:, k, a:b], start=(k == 0), stop=(k == 7)),
                     reads=[rw, rh], writes=[rbk])
            (q_, rq_) = qn[it % 2]
            (s_, rs_) = sq[it % 2]
            (r_, rr_) = rs[it % 2]
            gi = 0 if m < 8 else 1
            P.op("act", lambda e, n=n, bk=bk, s_=s_: e.activation(out=s_[:, 0:n], in_=bk[:, 0:n], func=AF.Square), reads=[rbk], writes=[rs_])
            b2, rb2 = K.bank()
            P.op("pe", lambda e, n=n, b2=b2, s_=s_: e.matmul(b2[:, 0:n], lhsT=ones_b[:], rhs=s_[:, 0:n], start=True, stop=True),
                 reads=[r_ob, rs_], writes=[rb2])
            P.op("act", lambda e, n=n, b2=b2, r_=r_: e.activation(out=r_[:, 0:n], in_=b2[:, 0:n], func=AF.Sqrt, scale=1.0 / HD, bias=EPS),
                 reads=[rb2], writes=[rr_])
            P.op("dve", lambda e, n=n, r_=r_: e.reciprocal(out=r_[:, 0:n], in_=r_[:, 0:n]), reads=[rr_], writes=[rr_])
            (ob_, rob_) = o16[it % 2]
            if sid == 1:
                P.op("dve", lambda e, n=n, bk=bk, ob_=ob_, r_=r_, gi=gi: e.scalar_tensor_tensor(out=ob_[:, 0:n], in0=bk[:, 0:n], scalar=gs[:, gi:gi + 1],
                                                                                             in1=r_[:, 0:n], op0=ALU.mult, op1=ALU.mult),
                     reads=[rbk, rr_, rgs], writes=[rob_])
                P.dma("sp", dst(m, sid, a, b), ob_[:, 0:n], reads=[rob_])
                continue
            P.op("dve", lambda e, n=n, bk=bk, q_=q_, r_=r_, gi=gi: e.scalar_tensor_tensor(out=q_[:, 0:n], in0=bk[:, 0:n], scalar=gs[:, gi:gi + 1], in1=r_[:, 0:n],
                                                                                       op0=ALU.mult, op1=ALU.mult),
                 reads=[rbk, rr_, rgs], writes=[rq_])
            b3, rb3 = K.bank()
            P.op("pe", lambda e, n=n, b3=b3, q_=q_: e.matmul(b3[:, 0:n], lhsT=pm[:], rhs=q_[:, 0:n], start=True, stop=True),
                 reads=[rpm, rq_], writes=[rb3])
            (u1, ru1) = t1[it % 2]
            (u2, ru2) = t2[it % 2]
            P.op("dve", lambda e, n=n, a=a, q_=q_, u1=u1: e.tensor_tensor(out=u1[:, 0:n], in0=q_[:, 0:n], in1=cs[:, a - CQ:a - CQ + n], op=ALU.mult),
                 reads=[rq_, rcs], writes=[ru1])
            P.op("dve", lambda e, n=n, a=a, b3=b3, u2=u2: e.tensor_tensor(out=u2[:, 0:n], in0=b3[:, 0:n], in1=sn[:, a - CQ:a - CQ + n], op=ALU.mult),
                 reads=[rb3, rsn], writes=[ru2])
            P.op("dve", lambda e, n=n, u1=u1, u2=u2, ob_=ob_: e.tensor_tensor(out=ob_[:, 0:n], in0=u1[:, 0:n], in1=u2[:, 0:n], op=ALU.add),
                 reads=[ru1, ru2], writes=[rob_])
            P.dma("sp", dst(m, sid, a, b), ob_[:, 0:n], reads=[rob_])
    wv, rwv = K.sb([128, 8, 256], BF16, "wv")
    wl.load(wv[:], rwv, w_qkv[:, 1280:1536].rearrange("(k p) c -> p k c", p=128), lambda t: t[:])
    vst = [K.sb([128, 256], BF16, "vst%d" % i) for i in range(3)]
    tblocks = [(1, 0, CQ)] + [(0, CQ + i * 128, CQ + (i + 1) * 128) for i in range(LQ // 128)]
    for i, (sid, a, b) in enumerate(tblocks):
        n = b - a
        bk, rbk = K.bank()
        for k in range(8):
            P.op("pe", lambda e, k=k, a=a, b=b, n=n, bk=bk: e.matmul(bk[0:n, 0:256], lhsT=h[:, k, a:b], rhs=wv[:, k, :], start=(k == 0), stop=(k == 7)),
                 reads=[rh, rwv], writes=[rbk])
        v_, rv_ = vst[i % 3]
        if i % 2:
            P.op("dve", lambda e, n=n, bk=bk, v_=v_: e.tensor_copy(out=v_[0:n, :], in_=bk[0:n, 0:256]), reads=[rbk], writes=[rv_])
        else:
            P.op("act", lambda e, n=n, bk=bk, v_=v_: e.activation(out=v_[0:n, :], in_=bk[0:n, 0:256], func=AF.Copy), reads=[rbk], writes=[rv_])
        t0 = koff[sid] + q * QN[sid] + a - base[sid]
        P.dma("sp", S["VT"][t0:t0 + n, :], v_[0:n, :], reads=[rv_])
    K.end()


def emit_kb_att(K, G, q, need_ctx):
    P = K.P
    S = G.S
    K.begin()
    (ones_b, r_ob), (ones_f, r_of) = (G.ones_b, G.r_ob), (G.ones_f, G.r_of)
    q_sb, rq = K.sb([128, 8, NA], BF16, "q")
    k, rk = K.sb([128, 2, NKEY], BF16, "k")
    v, rv = K.sb([128, NKT, 256], BF16, "v")
    P.dma("sp", q_sb[:, :, 0:CQ], S["Q"][1][:, q * CQ:(q + 1) * CQ].rearrange("(h p) n -> p h n", p=128), writes=[rq])
    P.dma("sp", q_sb[:, :, CQ:NA], S["Q"][0][:, q * LQ:(q + 1) * LQ].rearrange("(h p) n -> p h n", p=128), writes=[rq])
    P.dma("act", k[:], S["KT"].rearrange("(h p) n -> p h n", p=128), writes=[rk])
    P.dma("sp", v[:], S["VT"].rearrange("(t p) f -> p t f", p=128), writes=[rv])
    oseg = {1: (0, CQ), 0: (CQ, NA)}

    def odst(hq, a, b):
        sid = 1 if a < CQ else 0
        c0 = PADC + q * QN[sid] + a - oseg[sid][0]
        return S["MIX"][0][sid][hq * 128:(hq + 1) * 128, c0:c0 + b - a]

    pt = [K.sb([128, 512], BF16, "pt%d" % i) for i in range(4)]
    ob = [K.sb([128, 512], F32, "ob%d" % i) for i in range(2)]
    rc = [K.sb([128, 512], F32, "rc%d" % i) for i in range(2)]
    rsacc = [[K.sb([128, 512], F32, "rsacc%d%d" % (i, j)) for j in range(2)] for i in range(2)]
    sbank = [(K.banks[i], K.rbank[i]) for i in range(4)]
    accs = [((K.banks[4], K.rbank[4]), (K.banks[5], K.rbank[5])), ((K.banks[6], K.rbank[6]), (K.banks[7], K.rbank[7]))]
    scale = float(HD) ** -0.5
    qblocks = [(a, b, NKT) for (a, b) in split(CQ, NA)]
    if need_ctx:
        qblocks.append((0, CQ, CTX // 128))
    state = {"it": 0, "si": 0}

    def qblock(hq, kvh, a, b, nkt):
        n = b - a
        it = state["it"]
        (acc, racc), (rsum, rrsum) = accs[it % 2]
        o_, ro_ = ob[it % 2]
        c_, rc_ = rc[it % 2]
        ra = rsacc[it % 2]
        state["it"] += 1

        def smm(kt, si):
            bk, rbk = sbank[si % 4]
            P.op("pe", lambda e: e.matmul(bk[:, 0:n], lhsT=k[:, kvh, kt * 128:(kt + 1) * 128], rhs=q_sb[:, hq, a:b], start=True, stop=True),
                 reads=[rk, rq], writes=[rbk])

        def step(kt, si):
            bk, rbk = sbank[si % 4]
            p_, rp_ = pt[si % 4]
            P.op("act", lambda e: e.activation(out=p_[:, 0:n], in_=bk[:, 0:n], func=AF.Exp, scale=scale), reads=[rbk], writes=[rp_])
            P.op("pe", lambda e: e.matmul(acc[:, 0:n], lhsT=v[:, kt, kvh * 128:(kvh + 1) * 128], rhs=p_[:, 0:n],
                                          start=(kt == 0), stop=(kt == nkt - 1)),
                 reads=[rv, rp_], writes=[racc])
            (a_, ra_) = ra[kt % 2]
            eng = "pool" if kt % 2 == 0 else "dve"
            if kt < 2:
                P.op(eng, lambda e: e.tensor_copy(out=a_[:, 0:n], in_=p_[:, 0:n]), reads=[rp_], writes=[ra_])
            else:
                P.op(eng, lambda e: e.tensor_tensor(out=a_[:, 0:n], in0=a_[:, 0:n], in1=p_[:, 0:n], op=ALU.add), reads=[ra_, rp_], writes=[ra_])

        smm(0, state["si"])
        for kt in range(nkt):
            if kt + 1 < nkt:
                smm(kt + 1, state["si"] + 1)
            step(kt, state["si"])
            state["si"] += 1
        nh = min(2, nkt)
        for hf in range(nh):
            (a_, ra_) = ra[hf]
            P.op("pe", lambda e, a_=a_, hf=hf: e.matmul(rsum[:, 0:n], lhsT=ones_f[:], rhs=a_[:, 0:n], start=(hf == 0), stop=(hf == nh - 1)),
                 reads=[r_of, ra_], writes=[rrsum])
        P.op("dve", lambda e: e.reciprocal(out=c_[:, 0:n], in_=rsum[:, 0:n]), reads=[rrsum], writes=[rc_])
        P.op("dve", lambda e: e.tensor_tensor(out=o_[:, 0:n], in0=acc[:, 0:n], in1=c_[:, 0:n], op=ALU.mult), reads=[racc, rc_], writes=[ro_])
        P.dma("sp", odst(hq, a, b), o_[:, 0:n], reads=[ro_])

    for hq in range(8):
        for (a, b, nkt) in qblocks:
            qblock(hq, hq // 4, a, b, nkt)
    K.end()


REC_IN = 3632
NF = 3584
CH = 64
NCHUNK = NKEY // CH


def emit_ka_rec(K, G, l, q):
    P = K.P
    I = G.I
    S = G.S
    e_ = l // 2
    K.begin()
    segs, NT = layout(True)
    Xin = S["X"][l % 2]
    w_in = I["rec_w_in"][e_]
    (ones_b, r_ob), (ones_f, r_of) = (G.ones_b, G.r_ob), (G.ones_f, G.r_of)
    mods, rmod, mod1, rmod1 = G.mods[:, l], G.rmod, G.mod1[:, l], G.rmod1
    h, rh = K.sb([128, 8, NT], BF16, "h")
    cws, rcw = K.sb([128, 3, 12], F32, "cws")
    dtb, rdtb = K.sb([128, 8], F32, "dtb")
    nA, rnA = K.sb([128, 8], F32, "nA")
    w2s, rw2 = K.sb([16, 2, 256], F32, "w2s")
    b2bc, rb2 = K.sb([128, 2, 256], F32, "b2bc")
    idt, ridt = K.sb([128, 128], F32, "idt")
    outer = K.cur
    K.cur = ExitStack()
    x, rx = K.sb([128, 8, NT], F32, "x")
    for (sid, s0, s1) in segs:
        P.dma("sp", x[:, :, s0:s1], halo_ap(Xin, sid, q).rearrange("(k p) n -> p k n", p=128), writes=[rx])
    P.dma("sp", cws[:], I["rec_cw"][e_], writes=[rcw])
    P.dma("sp", dtb[:], I["dt_bias"][e_].partition_broadcast(128), writes=[rdtb])
    P.dma("sp", nA[:], I["a_log"][e_].partition_broadcast(128), writes=[rnA])
    P.dma("sp", w2s[:], I["w2"][e_], writes=[rw2])
    P.dma("sp", b2bc[:].rearrange("p d f -> p (d f)"), I["b2"][e_].partition_broadcast(128), writes=[rb2])
    P.dma("sp", idt[:], I["ident"], writes=[ridt])
    P.op("act", lambda e: e.activation(out=nA[:], in_=nA[:], func=AF.Exp), reads=[rnA], writes=[rnA])
    P.op("dve", lambda e: e.tensor_scalar(out=nA[:], in0=nA[:], scalar1=-1.0, scalar2=None, op0=ALU.mult), reads=[rnA], writes=[rnA])
    scr = rms_scratch(K)
    for (sid, s0, s1) in segs:
        for (a, b) in split(s0, s1):
            rms_modulate(K, x, rx, a, b, ones_b, r_ob, lambda k, sid=sid: mod1[:, sid, 1, k:k + 1], lambda k, sid=sid: mods[:, sid, 0, k:k + 1],
                         rmod1, h, rh, a, scr)
    P.barrier()
    K.cur.close()
    K.cur = outer
    wq = [K.sb([128, 8, 128], BF16, "wq%d" % i) for i in range(2)]
    pre = [K.sb([128, NT], F32, "pre%d" % i) for i in range(2)]
    cb = [K.sb([128, NT], F32, "cb%d" % i) for i in range(2)]
    sqb = [K.sb([128, 512], F32, "sqb%d" % i) for i in range(2)]
    rsb = [K.sb([128, 512], F32, "rsb%d" % i) for i in range(2)]
    tmst = [K.sb([128, 17, 128], F32, "tmst%d" % i) for i in range(2)]
    koff = {1: 0, 0: CTX}
    NTB = LQ // 128
    (c_s0, c_s1) = [(s0, s1) for (sid, s0, s1) in segs if sid == 1][0]
    (l_s0, l_s1) = [(s0, s1) for (sid, s0, s1) in segs if sid == 0][0]
    tblocks = [(c_s0 + 1, c_s0 + 1 + CQ)] + [(l_s0 + 1 + i * 128, l_s0 + 1 + (i + 1) * 128) for i in range(NTB)]

    def seqcols(sid):
        c0 = koff[sid] + q * QN[sid]
        return c0, c0 + QN[sid]

    def tm_out(D_, col, width, st):
        (t_, rt_) = st
        c0, c1 = seqcols(1)
        P.dma("sp", D_[c0:c1, col:col + width], t_[0:CQ, 0, 0:width], reads=[rt_])
        c0, c1 = seqcols(0)
        P.dma("sp", D_[c0:c1, col:col + width].rearrange("(t p) f -> p t f", p=128), t_[:, 1:17, 0:width], reads=[rt_])

    wl = WLoader(K, [128, 8, 512], 2, "wis")

    def load_w(i, col, width=128):
        w, rw = wq[i % 2]
        wl.load(w[:, :, 0:width], rw, w_in[:, col:col + width].rearrange("(k p) c -> p k c", p=128), lambda t: t[:, :, 0:width])
        return w, rw

    def proj(w, rw, width, a, b):
        bk, rbk = K.bank()
        for k in range(8):
            P.op("pe", lambda e, k=k: e.matmul(bk[0:width, 0:b - a], lhsT=w[:, k, 0:width], rhs=h[:, k, a:b], start=(k == 0), stop=(k == 7)),
                 reads=[rw, rh], writes=[rbk])
        return bk, rbk

    def conv_tile(m):
        w, rw = load_w(m, m * 128)
        g, rg = pre[m % 2]
        c, rc = cb[m % 2]
        for (sid, s0, s1) in segs:
            for i, (a, b) in enumerate(split(s0, s1)):
                bk, rbk = proj(w, rw, 128, a, b)
                if i % 2:
                    P.op("dve", lambda e, a=a, b=b, bk=bk: e.tensor_copy(out=g[:, a:b], in_=bk[:, 0:b - a]), reads=[rbk], writes=[rg])
                else:
                    P.op("act", lambda e, a=a, b=b, bk=bk: e.activation(out=g[:, a:b], in_=bk[:, 0:b - a], func=AF.Copy), reads=[rbk], writes=[rg])
        for (sid, s0, s1) in segs:
            for (col, cond) in ((s0, q == 0), (s1 - 1, q == 3)):
                if cond:
                    P.op("pool", lambda e, col=col: e.memset(g[:, col:col + 1], 0.0), reads=[rg], writes=[rg])
        for (sid, s0, s1) in segs:
            P.op("dve", lambda e, s0=s0, s1=s1: e.tensor_scalar(out=c[:, s0 + 1:s1 - 1], in0=g[:, s0:s1 - 2], scalar1=cws[:, 0, m:m + 1],
                                                                scalar2=None, op0=ALU.mult), reads=[rg, rcw], writes=[rc])
            P.op("dve", lambda e, s0=s0, s1=s1: e.scalar_tensor_tensor(out=c[:, s0 + 1:s1 - 1], in0=g[:, s0 + 1:s1 - 1], scalar=cws[:, 1, m:m + 1],
                                                                       in1=c[:, s0 + 1:s1 - 1], op0=ALU.mult, op1=ALU.add),
                 reads=[rg, rc, rcw], writes=[rc])
            P.op("dve", lambda e, s0=s0, s1=s1: e.scalar_tensor_tensor(out=c[:, s0 + 1:s1 - 1], in0=g[:, s0 + 2:s1], scalar=cws[:, 2, m:m + 1],
                                                                       in1=c[:, s0 + 1:s1 - 1], op0=ALU.mult, op1=ALU.add),
                 reads=[rg, rc, rcw], writes=[rc])
            P.op("act", lambda e, s0=s0, s1=s1: e.activation(out=c[:, s0 + 1:s1 - 1], in_=c[:, s0 + 1:s1 - 1], func=AF.Silu), reads=[rc], writes=[rc])
        if m < 8:
            sc, bi = (128.0, 128.0 * EPS) if m < 4 else (1.0, EPS)
            it = 0
            for (sid, s0, s1) in segs:
                for (a, b) in split(s0 + 1, s1 - 1):
                    n = b - a
                    sq_, rsq_ = sqb[it % 2]
                    rs_, rrs_ = rsb[it % 2]
                    it += 1

                    def blk(a=a, b=b, n=n, sq_=sq_, rsq_=rsq_, rs_=rs_, rrs_=rrs_):
                        P.op("act", lambda e: e.activation(out=sq_[:, 0:n], in_=c[:, a:b], func=AF.Square), reads=[rc], writes=[rsq_])
                        bk, rbk = K.bank()
                        P.op("pe", lambda e: e.matmul(bk[:, 0:n], lhsT=ones_f[:], rhs=sq_[:, 0:n], start=True, stop=True), reads=[r_of, rsq_], writes=[rbk])
                        P.op("act", lambda e: e.activation(out=rs_[:, 0:n], in_=bk[:, 0:n], func=AF.Sqrt, scale=sc, bias=bi), reads=[rbk], writes=[rrs_])
                        P.op("dve", lambda e: e.reciprocal(out=rs_[:, 0:n], in_=rs_[:, 0:n]), reads=[rrs_], writes=[rrs_])
                        P.op("dve", lambda e: e.tensor_tensor(out=c[:, a:b], in0=c[:, a:b], in1=rs_[:, 0:n], op=ALU.mult), reads=[rc, rrs_], writes=[rc])
                    blk()
            for (sid, s0, s1) in segs:
                c0, c1 = seqcols(sid)
                P.dma("sp", S["FQK"][m * 128:(m + 1) * 128, c0:c1], c[:, s0 + 1:s1 - 1], reads=[rc])
        if m >= 4:
            st = tmst[m % 2]
            (t_, rt_) = st
            for i, (a, b) in enumerate(tblocks):
                n = b - a
                bk, rbk = K.bank()
                P.op("pe", lambda e, a=a, b=b, n=n, bk=bk: e.transpose(out=bk[0:n, 0:128], in_=c[:, a:b], identity=idt[:]), reads=[rc, ridt], writes=[rbk])
                if i % 2:
                    P.op("dve", lambda e, i=i, n=n, bk=bk: e.tensor_copy(out=t_[0:n, i, :], in_=bk[0:n, 0:128]), reads=[rbk], writes=[rt_])
                else:
                    P.op("act", lambda e, i=i, n=n, bk=bk: e.activation(out=t_[0:n, i, :], in_=bk[0:n, 0:128], func=AF.Copy), reads=[rbk], writes=[rt_])
            tm_out(S["DKV"], (m - 4) * 128, 128, st)

    for m in range(12):
        conv_tile(m)

    plain = [(1536 + i * 128, ("zr", i * 128), "silu") for i in range(4)]
    plain += [(3088 + i * 128, ("zr", 512 + i * 128), "silu") for i in range(4)]
    plain += [(2064 + i * 128, ("fqk", 1024 + i * 128), "scale") for i in range(2)]
    plain += [(2320 + i * 128, ("fqk", 1280 + i * 128), "copy") for i in range(2)]

    def plain_tile(i, wcol, orow, kind):
        w, rw = load_w(i, wcol)
        c, rc = cb[i % 2]
        for (sid, s0, s1) in segs:
            for (a, b) in split(s0 + 1, s1 - 1):
                bk, rbk = proj(w, rw, 128, a, b)
                if kind == "silu":
                    P.op("act", lambda e, a=a, b=b, bk=bk: e.activation(out=c[:, a:b], in_=bk[:, 0:b - a], func=AF.Silu), reads=[rbk], writes=[rc])
                elif kind == "scale":
                    P.op("dve", lambda e, a=a, b=b, bk=bk: e.tensor_scalar(out=c[:, a:b], in0=bk[:, 0:b - a], scalar1=0.125, scalar2=None, op0=ALU.mult),
                         reads=[rbk], writes=[rc])
                else:
                    P.op("dve", lambda e, a=a, b=b, bk=bk: e.tensor_copy(out=c[:, a:b], in_=bk[:, 0:b - a]), reads=[rbk], writes=[rc])
        for (sid, s0, s1) in segs:
            if orow[0] == "zr":
                P.dma("sp", inte_ap(S["MIX"][2], sid, q)[orow[1]:orow[1] + 128, :], c[:, s0 + 1:s1 - 1], reads=[rc])
            else:
                c0, c1 = seqcols(sid)
                P.dma("sp", S["FQK"][orow[1]:orow[1] + 128, c0:c1], c[:, s0 + 1:s1 - 1], reads=[rc])

    for i, (wcol, orow, kind) in enumerate(plain):
        plain_tile(i, wcol, orow, kind)

    wg, rwg = K.sb([128, 8, 768], BF16, "wg")
    wl.load(wg[:, :, 0:256], rwg, w_in[:, 2320:2576].rearrange("(k p) c -> p k c", p=128), lambda t: t[:, :, 0:256])
    wl.load(wg[:, :, 256:768], rwg, w_in[:, 2576:3088].rearrange("(k p) c -> p k c", p=128), lambda t: t[:, :, 0:512])
    gst = [K.sb([128, 768], F32, "gst%d" % i) for i in range(3)]
    for i, (a, b) in enumerate(tblocks):
        n = b - a
        (t_, rt_) = gst[i % 3]
        for part, (c0_, wd) in enumerate(((0, 256), (256, 512))):
            bk, rbk = K.bank()
            for k in range(8):
                P.op("pe", lambda e, k=k, a=a, b=b, n=n, bk=bk, c0_=c0_, wd=wd: e.matmul(bk[0:n, 0:wd], lhsT=h[:, k, a:b], rhs=wg[:, k, c0_:c0_ + wd],
                                                                                        start=(k == 0), stop=(k == 7)),
                     reads=[rh, rwg], writes=[rbk])
            if part:
                P.op("dve", lambda e, n=n, bk=bk, wd=wd, c0_=c0_, t_=t_: e.tensor_copy(out=t_[0:n, c0_:c0_ + wd], in_=bk[0:n, 0:wd]), reads=[rbk], writes=[rt_])
            else:
                P.op("act", lambda e, n=n, bk=bk, wd=wd, c0_=c0_, t_=t_: e.activation(out=t_[0:n, c0_:c0_ + wd], in_=bk[0:n, 0:wd], func=AF.Copy),
                     reads=[rbk], writes=[rt_])
        r0 = (seqcols(1)[0]) if i == 0 else (seqcols(0)[0] + (i - 1) * 128)
        P.dma("sp", S["GKV"][r0:r0 + n, :], t_[0:n, :], reads=[rt_])

    wab, rwab = load_w(0, 2048, 16)
    gbs, rgbs = K.sb([128, 17, 16], F32, "gbs")
    bk, rbk = K.bank()
    for i, (a, b) in enumerate(tblocks):
        n = b - a
        for k in range(8):
            P.op("pe", lambda e, k=k, a=a, b=b, n=n, i=i, bk=bk, wab=wab: e.matmul(bk[0:n, i * 16:(i + 1) * 16], lhsT=h[:, k, a:b], rhs=wab[:, k, 0:16], start=(k == 0), stop=(k == 7)),
                 reads=[rh, rwab], writes=[rbk])
    for (p0, p1, i0, i1) in ((0, CQ, 0, 1), (0, 128, 1, 17)):
        nb = i1 - i0
        src = bk[p0:p1, i0 * 16:i1 * 16].rearrange("p (t c) -> p t c", c=16)

        def gates(p0=p0, p1=p1, i0=i0, i1=i1, nb=nb, src=src):
            gv_ = gbs[p0:p1, i0:i1, 0:8]
            bv_ = gbs[p0:p1, i0:i1, 8:16]
            P.op("dve", lambda e: e.tensor_tensor(out=gv_, in0=src[:, :, 0:8], in1=dtb[p0:p1, :].unsqueeze(1).to_broadcast([p1 - p0, nb, 8]), op=ALU.add),
                 reads=[rbk, rdtb], writes=[rgbs])
            P.op("act", lambda e: e.activation(out=gv_, in_=gv_, func=AF.Exp), reads=[rgbs], writes=[rgbs])
            P.op("act", lambda e: e.activation(out=gv_, in_=gv_, func=AF.Ln, bias=1.0), reads=[rgbs], writes=[rgbs])
            P.op("dve", lambda e: e.tensor_tensor(out=gv_, in0=gv_, in1=nA[p0:p1, :].unsqueeze(1).to_broadcast([p1 - p0, nb, 8]), op=ALU.mult),
                 reads=[rgbs, rnA], writes=[rgbs])
            P.op("act", lambda e: e.activation(out=bv_, in_=src[:, :, 8:16], func=AF.Exp, scale=-1.0), reads=[rbk], writes=[rgbs])
            P.op("dve", lambda e: e.tensor_scalar(out=bv_, in0=bv_, scalar1=1.0, scalar2=None, op0=ALU.add), reads=[rgbs], writes=[rgbs])
            P.op("dve", lambda e: e.reciprocal(out=bv_, in_=bv_), reads=[rgbs], writes=[rgbs])
        gates()
    tm_out(S["GB"], 0, 16, (gbs, rgbs))

    ggs1 = K.sb([16, NT], F32, "ggs")
    lst = [K.sb([128, 256], F32, "lst%d" % i) for i in range(3)]
    for d in range(2):
        wgd, rwgd = load_w(d + 1, 3600 + 16 * d, 16)
        gg_, rgg_ = ggs1
        for (sid, s0, s1) in segs:
            for (a, b) in split(s0 + 1, s1 - 1):
                bk, rbk = proj(wgd, rwgd, 16, a, b)
                P.op("dve", lambda e, a=a, b=b, bk=bk, gg_=gg_: e.tensor_copy(out=gg_[0:16, a:b], in_=bk[0:16, 0:b - a]), reads=[rbk], writes=[rgg_])
        for i, (a, b) in enumerate(tblocks):
            n = b - a
            (v_, rv_) = lst[i % 3]
            bk, rbk = K.bank()
            P.op("pe", lambda e, a=a, b=b, n=n, bk=bk, gg_=gg_, d=d: e.matmul(bk[0:n, 0:256], lhsT=gg_[0:16, a:b], rhs=w2s[:, d, :], start=True, stop=True),
                 reads=[rgg_, rw2], writes=[rbk])
            P.op("dve", lambda e, n=n, bk=bk, d=d, v_=v_: e.tensor_tensor(out=v_[0:n, :], in0=bk[0:n, 0:256], in1=b2bc[0:n, d, :], op=ALU.add),
                 reads=[rbk, rb2], writes=[rv_])
            P.op("act", lambda e, n=n, v_=v_: e.activation(out=v_[0:n, :], in_=v_[0:n, :], func=AF.Exp, scale=-1.0), reads=[rv_], writes=[rv_])
            P.op("act", lambda e, n=n, v_=v_: e.activation(out=v_[0:n, :], in_=v_[0:n, :], func=AF.Ln, bias=1.0), reads=[rv_], writes=[rv_])
            P.op("dve", lambda e, n=n, v_=v_: e.tensor_scalar(out=v_[0:n, :], in0=v_[0:n, :], scalar1=-1.0 / 16.0, scalar2=None, op0=ALU.mult),
                 reads=[rv_], writes=[rv_])
            r0 = (seqcols(1)[0]) if i == 0 else (seqcols(0)[0] + (i - 1) * 128)
            P.dma("sp", S["LA"][r0:r0 + n, d * 256:(d + 1) * 256], v_[0:n, :], reads=[rv_])
    K.end()


GC = 4
GW = GC * CH
NG = NCHUNK // GC


def emit_kb_rec(K, G, hh, NG=NG):
    P = K.P
    I = G.I
    S = G.S
    K.begin()
    T = NKEY
    FQK = S["FQK"]

    def tmv(ap2):
        return ap2.rearrange("(n j) f -> j n f", j=CH)
    dq = [FQK[hh * 128:(hh + 1) * 128, :]] * 2
    dk = [FQK[512 + hh * 128:512 + (hh + 1) * 128, :]] * 2
    dktm = [tmv(S["DKV"][:, hh * 128:(hh + 1) * 128])] * 2
    dvtm = [tmv(S["DKV"][:, 512 + hh * 128:512 + (hh + 1) * 128])] * 2
    gq = [FQK[1024 + hh * 64:1024 + (hh + 1) * 64, :]] * 2
    gk = [FQK[1280 + hh * 64:1280 + (hh + 1) * 64, :]] * 2
    gktm = [tmv(S["GKV"][:, hh * 64:(hh + 1) * 64])] * 2
    gvtm = [tmv(S["GKV"][:, 256 + hh * 128:256 + (hh + 1) * 128])] * 2
    gla = [tmv(S["LA"][:, d * 256 + hh * 64:d * 256 + (hh + 1) * 64]) for d in range(2)]
    order = [list(range(NG)), [0] + list(range(NG - 1, 0, -1))]
    corder = [list(range(GC)), list(range(GC - 1, -1, -1))]
    mid_idx = [32, 31]
    last_idx = [63, 0]

    def odst(mixer, d, gmem):
        r0 = mixer * 512 + hh * 128
        if gmem == 0:
            return S["MIX"][d][1][r0:r0 + 128, PADC:PADC + CTX]
        c0 = PADC + (gmem - 1) * GW
        return S["MIX"][d][0][r0:r0 + 128, c0:c0 + GW]

    (ones_b, r_ob), (ones_f, r_of) = (G.ones_b, G.r_ob), (G.ones_f, G.r_of)
    acc_dn = [(K.banks[4 + d], K.rbank[4 + d]) for d in range(2)]
    acc_gl = [(K.banks[6 + d], K.rbank[6 + d]) for d in range(2)]
    rot = {"i": 0}

    def bank():
        i = rot["i"]
        rot["i"] = (i + 1) % 4
        return K.banks[i], K.rbank[i]

    cms, rcm = K.sb([64, 2, 6, 64], F32, "cms")
    P.dma("sp", cms[:], I["cm"], writes=[rcm])
    CMU = [cms[:, d, 0, :] for d in range(2)]
    CMUs = [cms[:, d, 1, :] for d in range(2)]
    CMLs = [cms[:, d, 2, :] for d in range(2)]
    CMNegU = [cms[:, d, 3, :] for d in range(2)]
    CMNegLs = [cms[:, d, 4, :] for d in range(2)]
    I64 = cms[:, 0, 5, :]

    def bc(ap2, n=GC):
        return ap2.unsqueeze(1).to_broadcast([64, n, 64])

    def flat(t):
        return t[:].rearrange("p a b -> p (a b)")

    gcol, bcol, Gcol, ekd, gl, bg = [], [], [], [], [], []
    gb_all, rgb_all = K.sb([64, NCHUNK, 16], F32, "gb_all")
    P.dma("sp", gb_all[:], tmv(S["GB"]), writes=[rgb_all])
    for d in range(2):
        g_, rg_ = gb_all[:, :, d * 4 + hh], rgb_all
        b_, rb_ = gb_all[:, :, 8 + d * 4 + hh], rgb_all
        G_, rG_ = K.sb([64, NCHUNK], F32, "Gcol%d" % d)
        e_, re_ = K.sb([64, NCHUNK], F32, "ekd%d" % d)
        l_, rl_ = K.sb([128, NCHUNK], F32, "gl%d" % d)
        x_, rx_ = K.sb([64, NCHUNK], F32, "bg%d" % d)
        U = CMU[d]

        def pre(g_=g_, rg_=rg_, b_=b_, rb_=rb_, G_=G_, rG_=rG_, e_=e_, re_=re_, l_=l_, rl_=rl_, x_=x_, rx_=rx_, U=U):
            b1, rb1 = bank()
            P.op("pe", lambda e: e.matmul(b1[0:64, 0:NCHUNK], lhsT=U, rhs=g_[:], start=True, stop=True), reads=[rcm, rg_], writes=[rb1])
            P.op("act", lambda e: e.activation(out=G_[:], in_=b1[0:64, 0:NCHUNK], func=AF.Copy), reads=[rb1], writes=[rG_])
            b2, rb2 = bank()
            P.op("pe", lambda e: e.matmul(b2[:, 0:NCHUNK], lhsT=ones_f[0:64, :], rhs=g_[:], start=True, stop=True), reads=[r_of, rg_], writes=[rb2])
            P.op("act", lambda e: e.activation(out=l_[:], in_=b2[:, 0:NCHUNK], func=AF.Exp), reads=[rb2], writes=[rl_])
            P.op("dve", lambda e: e.tensor_tensor(out=e_[:], in0=b2[0:64, 0:NCHUNK], in1=G_[:], op=ALU.subtract), reads=[rb2, rG_], writes=[re_])
            P.op("act", lambda e: e.activation(out=e_[:], in_=e_[:], func=AF.Exp), reads=[re_], writes=[re_])
            P.op("act", lambda e: e.activation(out=x_[:], in_=G_[:], func=AF.Exp), reads=[rG_], writes=[rx_])
            P.op("dve", lambda e: e.tensor_tensor(out=x_[:], in0=x_[:], in1=b_[:], op=ALU.mult), reads=[rx_, rb_], writes=[rx_])
        pre()
        gcol.append((g_, rg_)); bcol.append((b_, rb_)); Gcol.append((G_, rG_)); ekd.append((e_, re_)); gl.append((l_, rl_)); bg.append((x_, rx_))

    def tl(shape, name):
        return K.sb(shape, F32, name)
    DL = [[{"q": tl([128, GW], "dLq%d%d" % (d, p)), "k": tl([128, GW], "dLk%d%d" % (d, p)), "ktm": tl([64, GC, 128], "dLkt%d%d" % (d, p)),
            "vtm": tl([64, GC, 128], "dLvt%d%d" % (d, p))} for p in range(2)] for d in range(2)]
    DT = [{"R": tl([64, GC, 64], "R%d" % d), "R2": tl([64, GC, 64], "R2%d" % d), "eG": tl([128, GW], "eG%d" % d), "kbT": tl([128, GW], "kbT%d" % d),
           "D1": tl([64, GC, 64], "D1%d" % d), "D2": tl([64, GC, 64], "D2%d" % d), "E1s": tl([64, GC, 64], "E1s%d" % d),
           "AT": tl([64, GC, 64], "AT%d" % d), "A": tl([64, GC, 64], "A%d" % d), "Pt": tl([64, GC, 64], "Pt%d" % d),
           "M": [tl([64, GC, 64], "M%d%d" % (d, i)) for i in range(2)], "MT": [tl([64, GC, 64], "MT%d%d" % (d, i)) for i in range(2)],
           "vb": tl([64, GC, 128], "vb%d" % d), "kbg": tl([64, GC, 128], "kbg%d" % d)} for d in range(2)]
    DS = [[{"IT": tl([64, GC, 64], "IT%d%d" % (d, p)), "qd": tl([128, GW], "qd%d%d" % (d, p)), "kdec": tl([64, GC, 128], "kdec%d%d" % (d, p)),
            "u": tl([64, GC, 128], "u%d%d" % (d, p)), "wT": tl([128, GC, 64], "wT%d%d" % (d, p))} for p in range(2)] for d in range(2)]
    Sdn = [[tl([128, 128], "Sdn%d%d" % (d, i)) for i in range(2)] for d in range(2)]
    vnb = [[tl([64, 128], "vn%d%d" % (d, i)) for i in range(2)] for d in range(2)]
    ost = [[tl([128, GW], "ost%d%d" % (m, d)) for d in range(2)] for m in range(2)]
    GL = [[{"q": tl([64, GW], "gLq%d%d" % (d, p)), "k": tl([64, GW], "gLk%d%d" % (d, p)), "ktm": tl([64, GC, 64], "gLkt%d%d" % (d, p)),
            "la": tl([64, GC, 64], "gLla%d%d" % (d, p))} for p in range(2)] for d in range(2)]
    GV = [[tl([64, GC, 128], "gLv%d%d" % (d, p)) for p in range(3)] for d in range(2)]
    GT = [{"b": tl([64, GC, 64], "gb%d" % d), "dd": tl([64, GC, 64], "gdd%d" % d), "kst": tl([64, GC, 64], "gkst%d" % d),
           "bT": tl([64, GC, 64], "gbT%d" % d), "e1": tl([64, GC, 64], "ge1%d" % d), "eq": tl([64, GC, 64], "geq%d" % d),
           "ek": tl([64, GC, 64], "gek%d" % d), "ei": tl([64, GC, 64], "gei%d" % d), "qtl": tl([64, GW], "gqtl%d" % d),
           "ktl": tl([64, GW], "gktl%d" % d)} for d in range(2)]
    GS = [[{"qi": tl([64, GW], "gqi%d%d" % (d, p)), "att": tl([64, GC, 64], "gatt%d%d" % (d, p)), "KV": tl([64, GC, 128], "gKV%d%d" % (d, p)),
            "al": tl([64, GC], "gal%d%d" % (d, p))} for p in range(2)] for d in range(2)]
    Sgl = [[tl([64, 128], "Sgl%d%d" % (d, i)) for i in range(2)] for d in range(2)]
    for d in range(2):
        for (t_, r_) in (Sdn[d][0], Sgl[d][0]):
            P.op("pool", lambda e, t_=t_: e.memset(t_[:], 0.0), writes=[r_])
    scur = {"dn": [0, 0], "gl": [0, 0], "vn": [0, 0]}

    def loads(gi):
        p = gi % 2
        for d in range(2):
            n0 = order[d][gi] * GC
            t0 = n0 * CH
            L = DL[d][p]
            P.dma("sp", L["q"][0][:], dq[d][:, t0:t0 + GW], writes=[L["q"][1]])
            P.dma("sp", L["k"][0][:], dk[d][:, t0:t0 + GW], writes=[L["k"][1]])
            P.dma("sp", L["ktm"][0][:], dktm[d][:, n0:n0 + GC, :], writes=[L["ktm"][1]])
            P.dma("sp", L["vtm"][0][:], dvtm[d][:, n0:n0 + GC, :], writes=[L["vtm"][1]])
            G = GL[d][p]
            P.dma("sp", G["q"][0][:], gq[d][:, t0:t0 + GW], writes=[G["q"][1]])
            P.dma("sp", G["k"][0][:], gk[d][:, t0:t0 + GW], writes=[G["k"][1]])
            P.dma("sp", G["ktm"][0][:], gktm[d][:, n0:n0 + GC, :], writes=[G["ktm"][1]])
            P.dma("sp", G["la"][0][:], gla[d][:, n0:n0 + GC, :], writes=[G["la"][1]])
            gv_, rgv_ = GV[d][gi % 3]
            P.dma("sp", gv_[:], gvtm[d][:, n0:n0 + GC, :], writes=[rgv_])

    def cs(n):
        return slice(n * CH, (n + 1) * CH)

    def v3(ap2):
        return ap2.rearrange("p (a b) -> p a b", a=GC)

    def dn_prep(d, gi):
        n0 = order[d][gi] * GC
        p = gi % 2
        U, Us, Ls, NegU, NegLs = CMU[d], CMUs[d], CMLs[d], CMNegU[d], CMNegLs[d]
        L, Tm, S2 = DL[d][p], DT[d], DS[d][p]
        (qT, rqT), (kT, rkT), (ktm, rktm), (vtm, rvtm) = L["q"], L["k"], L["ktm"], L["vtm"]
        (R, rR), (R2, rR2), (eG, reG), (kbT, rkbT) = Tm["R"], Tm["R2"], Tm["eG"], Tm["kbT"]
        (D1, rD1), (D2, rD2), (E1s, rE1s), (AT, rAT), (A_, rA_), (Pt, rPt) = Tm["D1"], Tm["D2"], Tm["E1s"], Tm["AT"], Tm["A"], Tm["Pt"]
        (vb, rvb), (kbg, rkbg) = Tm["vb"], Tm["kbg"]
        (IT, rIT), (qd, rqd), (kdec, rkdec), (u, ru), (wT, rwT) = S2["IT"], S2["qd"], S2["kdec"], S2["u"], S2["wT"]
        (g_, rg_), (b_, rb_), (G_, rG_), (e_, re_), (x_, rx_) = gcol[d], bcol[d], Gcol[d], ekd[d], bg[d]
        gs = slice(n0, n0 + GC)
        P.op("dve", lambda e: e.tensor_tensor(out=R[:], in0=g_[:, gs].unsqueeze(2).to_broadcast([64, GC, 64]), in1=bc(U), op=ALU.mult),
             reads=[rg_, rcm], writes=[rR])
        P.op("dve", lambda e: e.tensor_tensor(out=R2[:], in0=b_[:, gs].unsqueeze(2).to_broadcast([64, GC, 64]), in1=bc(I64), op=ALU.mult),
             reads=[rb_, rcm], writes=[rR2])
        bA, rbA = bank()
        P.op("pe", lambda e: e.matmul(bA[:, 0:GW], lhsT=ones_f[0:64, :], rhs=flat(R), start=True, stop=True), reads=[r_of, rR], writes=[rbA])
        bB, rbB = bank()
        P.op("pe", lambda e: e.matmul(bB[:, 0:GW], lhsT=ones_f[0:64, :], rhs=flat(R2), start=True, stop=True), reads=[r_of, rR2], writes=[rbB])
        P.op("act", lambda e: e.activation(out=eG[:], in_=bA[:, 0:GW], func=AF.Exp), reads=[rbA], writes=[reG])
        P.op("dve", lambda e: e.tensor_tensor(out=kbT[:], in0=kT[:], in1=bB[:, 0:GW], op=ALU.mult), reads=[rkT, rbB], writes=[rkbT])
        P.op("dve", lambda e: e.tensor_tensor(out=D1[:], in0=v3(bA[0:64, 0:GW]), in1=G_[:, gs].unsqueeze(2).to_broadcast([64, GC, 64]), op=ALU.subtract),
             reads=[rbA, rG_], writes=[rD1])
        yield
        P.op("dve", lambda e: e.tensor_tensor(out=D2[:], in0=D1[:], in1=bc(Ls), op=ALU.mult), reads=[rD1, rcm], writes=[rD2])
        P.op("dve", lambda e: e.tensor_tensor(out=D2[:], in0=D2[:], in1=bc(NegLs), op=ALU.add), reads=[rD2, rcm], writes=[rD2])
        P.op("dve", lambda e: e.tensor_tensor(out=D1[:], in0=D1[:], in1=bc(U), op=ALU.mult), reads=[rD1, rcm], writes=[rD1])
        P.op("dve", lambda e: e.tensor_tensor(out=D1[:], in0=D1[:], in1=bc(NegU), op=ALU.add), reads=[rD1, rcm], writes=[rD1])
        P.op("act", lambda e: e.activation(out=D1[:], in_=D1[:], func=AF.Exp), reads=[rD1], writes=[rD1])
        P.op("act", lambda e: e.activation(out=D2[:], in_=D2[:], func=AF.Exp), reads=[rD2], writes=[rD2])
        P.op("dve", lambda e: e.tensor_tensor(out=E1s[:], in0=D1[:], in1=bc(Us), op=ALU.mult), reads=[rD1, rcm], writes=[rE1s])
        P.op("dve", lambda e: e.tensor_tensor(out=qd[:], in0=qT[:], in1=eG[:], op=ALU.mult), reads=[rqT, reG], writes=[rqd])
        if d == 0 and gi == 0 and hh == 0:
            G.dump("E1", D1[:], rD1); G.dump("E2s", D2[:], rD2); G.dump("E1s", E1s[:], rE1s); G.dump("kbT", kbT[:], rkbT); G.dump("qd", qd[:], rqd)
        yield
        bC, rbC = bank()
        for n in range(GC):
            P.op("pe", lambda e, n=n: e.matmul(bC[0:64, cs(n)], lhsT=kT[:, cs(n)], rhs=kbT[:, cs(n)], start=True, stop=True), reads=[rkT, rkbT], writes=[rbC])
        P.op("dve", lambda e: e.tensor_tensor(out=AT[:], in0=v3(bC[0:64, 0:GW]), in1=E1s[:], op=ALU.mult), reads=[rbC, rE1s], writes=[rAT])
        bD, rbD = bank()
        for n in range(GC):
            P.op("pe", lambda e, n=n: e.matmul(bD[0:64, cs(n)], lhsT=kbT[:, cs(n)], rhs=kT[:, cs(n)], start=True, stop=True), reads=[rkT, rkbT], writes=[rbD])
        P.op("dve", lambda e: e.tensor_tensor(out=A_[:], in0=v3(bD[0:64, 0:GW]), in1=D2[:], op=ALU.mult), reads=[rbD, rD2], writes=[rA_])
        yield
        bE, rbE = bank()
        for n in range(GC):
            P.op("pe", lambda e, n=n: e.matmul(bE[0:64, cs(n)], lhsT=kT[:, cs(n)], rhs=qT[:, cs(n)], start=True, stop=True), reads=[rkT, rqT], writes=[rbE])
        P.op("dve", lambda e: e.tensor_tensor(out=IT[:], in0=v3(bE[0:64, 0:GW]), in1=D1[:], op=ALU.mult), reads=[rbE, rD1], writes=[rIT])
        P.op("dve", lambda e: e.tensor_tensor(out=Pt[:], in0=bc(I64), in1=AT[:], op=ALU.subtract), reads=[rcm, rAT], writes=[rPt])
        yield
        (M0, rM0), (MT0, rMT0) = Tm["M"][0], Tm["MT"][0]
        b1, rb1 = bank()
        for n in range(GC):
            P.op("pe", lambda e, n=n: e.matmul(b1[0:64, cs(n)], lhsT=AT[:, n, :], rhs=A_[:, n, :], start=True, stop=True), reads=[rAT, rA_], writes=[rb1])
        P.op("act", lambda e: e.activation(out=M0[:], in_=v3(b1[0:64, 0:GW]), func=AF.Copy), reads=[rb1], writes=[rM0])
        b2, rb2 = bank()
        for n in range(GC):
            P.op("pe", lambda e, n=n: e.matmul(b2[0:64, cs(n)], lhsT=A_[:, n, :], rhs=AT[:, n, :], start=True, stop=True), reads=[rAT, rA_], writes=[rb2])
        P.op("dve", lambda e: e.tensor_copy(out=MT0[:], in_=v3(b2[0:64, 0:GW])), reads=[rb2], writes=[rMT0])
        yield
        for k in range(1, 6):
            (Mc, rMc), (MTc, rMTc) = Tm["M"][(k - 1) % 2], Tm["MT"][(k - 1) % 2]
            (Mn, rMn), (MTn, rMTn) = Tm["M"][k % 2], Tm["MT"][k % 2]

            def step(k=k, Mc=Mc, rMc=rMc, MTc=MTc, rMTc=rMTc, Mn=Mn, rMn=rMn, MTn=MTn, rMTn=rMTn):
                b1, rb1 = bank()
                for n in range(GC):
                    P.op("pe", lambda e, n=n: e.matmul(b1[0:64, cs(n)], lhsT=Mc[:, n, :], rhs=Pt[:, n, :], start=True, stop=True), reads=[rMc, rPt], writes=[rb1])
                P.op("dve", lambda e: e.tensor_tensor(out=Pt[:], in0=Pt[:], in1=v3(b1[0:64, 0:GW]), op=ALU.add), reads=[rPt, rb1], writes=[rPt])
                if k < 5:
                    b2, rb2 = bank()
                    for n in range(GC):
                        P.op("pe", lambda e, n=n: e.matmul(b2[0:64, cs(n)], lhsT=MTc[:, n, :], rhs=Mc[:, n, :], start=True, stop=True), reads=[rMc, rMTc], writes=[rb2])
                    P.op("act", lambda e: e.activation(out=Mn[:], in_=v3(b2[0:64, 0:GW]), func=AF.Copy), reads=[rb2], writes=[rMn])
                if k < 4:
                    b3, rb3 = bank()
                    for n in range(GC):
                        P.op("pe", lambda e, n=n: e.matmul(b3[0:64, cs(n)], lhsT=Mc[:, n, :], rhs=MTc[:, n, :], start=True, stop=True), reads=[rMc, rMTc], writes=[rb3])
                    P.op("act", lambda e: e.activation(out=MTn[:], in_=v3(b3[0:64, 0:GW]), func=AF.Copy), reads=[rb3], writes=[rMTn])
            step()
            yield
        P.op("dve", lambda e: e.tensor_tensor(out=vb[:], in0=vtm[:], in1=b_[:, gs].unsqueeze(2).to_broadcast([64, GC, 128]), op=ALU.mult),
             reads=[rvtm, rb_], writes=[rvb])
        P.op("dve", lambda e: e.tensor_tensor(out=kbg[:], in0=ktm[:], in1=x_[:, gs].unsqueeze(2).to_broadcast([64, GC, 128]), op=ALU.mult),
             reads=[rktm, rx_], writes=[rkbg])
        P.op("dve", lambda e: e.tensor_tensor(out=kdec[:], in0=ktm[:], in1=e_[:, gs].unsqueeze(2).to_broadcast([64, GC, 128]), op=ALU.mult),
             reads=[rktm, re_], writes=[rkdec])
        bu, rbu = bank()
        for n in range(GC):
            P.op("pe", lambda e, n=n: e.matmul(bu[0:64, n * 128:(n + 1) * 128], lhsT=Pt[:, n, :], rhs=vb[:, n, :], start=True, stop=True), reads=[rPt, rvb], writes=[rbu])
        P.op("act", lambda e: e.activation(out=flat(u), in_=bu[0:64, 0:GC * 128], func=AF.Copy), reads=[rbu], writes=[ru])
        bw, rbw = bank()
        for n in range(GC):
            P.op("pe", lambda e, n=n: e.matmul(bw[:, cs(n)], lhsT=kbg[:, n, :], rhs=Pt[:, n, :], start=True, stop=True), reads=[rPt, rkbg], writes=[rbw])
        P.op("dve", lambda e: e.tensor_copy(out=flat(wT), in_=bw[:, 0:GW]), reads=[rbw], writes=[rwT])
        if d == 0 and gi == 0 and hh == 0:
            G.dump("AT", AT[:], rAT); G.dump("A", A_[:], rA_); G.dump("IT", IT[:], rIT); G.dump("Pt", Pt[:], rPt); G.dump("u", u[:], ru)
            G.dump("wT", wT[:], rwT); G.dump("kdec", kdec[:], rkdec); G.dump("vb", vb[:], rvb); G.dump("kbg", kbg[:], rkbg)
            G.dump("gl", gl[d][0][:], gl[d][1]); G.dump("Gcol", Gcol[d][0][:], Gcol[d][1]); G.dump("ekd", ekd[d][0][:], ekd[d][1]); G.dump("bg", bg[d][0][:], bg[d][1])
        yield

    def dn_seq(d, gi):
        gmem = order[d][gi]
        n0 = gmem * GC
        S2 = DS[d][gi % 2]
        (IT, rIT), (qd, rqd), (kdec, rkdec), (u, ru), (wT, rwT) = S2["IT"], S2["qd"], S2["kdec"], S2["u"], S2["wT"]
        (l_, rl_) = gl[d]
        (oacc, roacc) = acc_dn[d]
        for n in corder[d]:
            def chunk(n=n):
                c = scur["dn"][d]
                (Sc, rSc), (Sn, rSn) = Sdn[d][c], Sdn[d][1 - c]
                scur["dn"][d] = 1 - c
                (vn, rvn) = vnb[d][scur["vn"][d]]
                scur["vn"][d] ^= 1
                b1, rb1 = bank()
                P.op("pe", lambda e: e.matmul(b1[0:64, 0:128], lhsT=wT[:, n, :], rhs=Sc[:], start=True, stop=True), reads=[rwT, rSc], writes=[rb1])
                P.op("dve", lambda e: e.tensor_tensor(out=vn[:], in0=u[:, n, :], in1=b1[0:64, 0:128], op=ALU.subtract), reads=[ru, rb1], writes=[rvn])
                b3, rb3 = bank()
                P.op("pe", lambda e: e.matmul(b3[:, 0:128], lhsT=kdec[:, n, :], rhs=vn[:], start=True, stop=True), reads=[rkdec, rvn], writes=[rb3])
                P.op("pe", lambda e: e.matmul(oacc[:, cs(n)], lhsT=Sc[:], rhs=qd[:, cs(n)], start=True, stop=False), reads=[rSc, rqd], writes=[roacc])
                P.op("pe", lambda e: e.matmul(oacc[:, cs(n)], lhsT=vn[:], rhs=IT[:, n, :], start=False, stop=True), reads=[rvn, rIT], writes=[roacc])
                P.op("dve", lambda e: e.scalar_tensor_tensor(out=Sn[:], in0=Sc[:], scalar=l_[:, n0 + n:n0 + n + 1], in1=b3[:, 0:128], op0=ALU.mult, op1=ALU.add),
                     reads=[rSc, rl_, rb3], writes=[rSn])
                if d == 0 and gi == 0 and hh == 0:
                    G.dump("vn%d" % n, vn[:], rvn); G.dump("S%d" % n, Sn[:], rSn); G.dump("Sin%d" % n, Sc[:], rSc)
            chunk()
            yield
        (o_, ro_) = ost[0][d]
        P.op("act", lambda e: e.activation(out=o_[:], in_=oacc[:, 0:GW], func=AF.Copy), reads=[roacc], writes=[ro_])
        P.dma("sp", odst(0, d, gmem), o_[:], reads=[ro_])
        yield

    def gl_prep(d, gi):
        n0 = order[d][gi] * GC
        p = gi % 2
        U = CMU[d]
        G, Tm, S2 = GL[d][p], GT[d], GS[d][p]
        (qT, rqT), (kT, rkT), (ktm, rktm), (la, rla) = G["q"], G["k"], G["ktm"], G["la"]
        (gv_, rgv_) = GV[d][gi % 3]
        (b_, rb_), (dd, rdd), (kst, rkst), (bT, rbT), (e1, re1), (eq, req), (ek, rek), (ei, rei), (qtl, rqtl), (ktl, rktl) = (
            Tm["b"], Tm["dd"], Tm["kst"], Tm["bT"], Tm["e1"], Tm["eq"], Tm["ek"], Tm["ei"], Tm["qtl"], Tm["ktl"])
        (qi, rqi), (att, ratt), (KV, rKV), (al, ral) = S2["qi"], S2["att"], S2["KV"], S2["al"]
        bb, rbb = bank()
        P.op("pe", lambda e: e.matmul(bb[0:64, 0:GW], lhsT=U, rhs=flat(la), start=True, stop=True), reads=[rcm, rla], writes=[rbb])
        bl, rbl = bank()
        P.op("pe", lambda e: e.matmul(bl[0:64, 0:GW], lhsT=ones_f[0:64, 0:64], rhs=flat(la), start=True, stop=True), reads=[r_of, rla], writes=[rbl])
        bt, rbt = bank()
        for n in range(GC):
            P.op("pe", lambda e, n=n: e.matmul(bt[0:64, cs(n)], lhsT=la[:, n, :], rhs=U, start=True, stop=True), reads=[rcm, rla], writes=[rbt])
        P.op("act", lambda e: e.activation(out=flat(b_), in_=bb[0:64, 0:GW], func=AF.Copy), reads=[rbb], writes=[rb_])
        P.op("dve", lambda e: e.tensor_tensor(out=flat(dd), in0=bl[0:64, 0:GW], in1=flat(b_), op=ALU.subtract), reads=[rbl, rb_], writes=[rdd])
        P.op("act", lambda e: e.activation(out=dd[:], in_=dd[:], func=AF.Exp), reads=[rdd], writes=[rdd])
        P.op("dve", lambda e: e.tensor_tensor(out=kst[:], in0=ktm[:], in1=dd[:], op=ALU.mult), reads=[rktm, rdd], writes=[rkst])
        P.op("act", lambda e: e.activation(out=flat(bT), in_=bt[0:64, 0:GW], func=AF.Copy), reads=[rbt], writes=[rbT])
        yield
        P.op("pool", lambda e: e.tensor_tensor(out=e1[:], in0=bT[:], in1=bT[:, :, mid_idx[d]:mid_idx[d] + 1].to_broadcast([64, GC, 64]), op=ALU.subtract), reads=[rbT], writes=[re1])
        P.op("act", lambda e: e.activation(out=eq[:], in_=e1[:], func=AF.Exp), reads=[re1], writes=[req])
        P.op("act", lambda e: e.activation(out=ek[:], in_=e1[:], func=AF.Exp, scale=-1.0), reads=[re1], writes=[rek])
        P.op("act", lambda e: e.activation(out=ei[:], in_=bT[:], func=AF.Exp), reads=[rbT], writes=[rei])
        P.op("dve", lambda e: e.tensor_tensor(out=qtl[:], in0=qT[:], in1=flat(eq), op=ALU.mult), reads=[rqT, req], writes=[rqtl])
        P.op("dve", lambda e: e.tensor_tensor(out=ktl[:], in0=kT[:], in1=flat(ek), op=ALU.mult), reads=[rkT, rek], writes=[rktl])
        P.op("pool", lambda e: e.tensor_tensor(out=qi[:], in0=qT[:], in1=flat(ei), op=ALU.mult), reads=[rqT, rei], writes=[rqi])
        P.op("act", lambda e: e.activation(out=al[:], in_=bT[:, :, last_idx[d]], func=AF.Exp), reads=[rbT], writes=[ral])
        yield
        ba, rba = bank()
        for n in range(GC):
            P.op("pe", lambda e, n=n: e.matmul(ba[0:64, cs(n)], lhsT=ktl[:, cs(n)], rhs=qtl[:, cs(n)], start=True, stop=True), reads=[rktl, rqtl], writes=[rba])
        P.op("dve", lambda e: e.tensor_tensor(out=att[:], in0=v3(ba[0:64, 0:GW]), in1=bc(U), op=ALU.mult), reads=[rba, rcm], writes=[ratt])
        bkv, rbkv = bank()
        for n in range(GC):
            P.op("pe", lambda e, n=n: e.matmul(bkv[0:64, n * 128:(n + 1) * 128], lhsT=kst[:, n, :], rhs=gv_[:, n, :], start=True, stop=True), reads=[rkst, rgv_], writes=[rbkv])
        P.op("act", lambda e: e.activation(out=flat(KV), in_=bkv[0:64, 0:GC * 128], func=AF.Copy), reads=[rbkv], writes=[rKV])
        yield

    def gl_seq(d, gi):
        gmem = order[d][gi]
        n0 = gmem * GC
        S2 = GS[d][gi % 2]
        (qi, rqi), (att, ratt), (KV, rKV), (al, ral) = S2["qi"], S2["att"], S2["KV"], S2["al"]
        (gv_, rgv_) = GV[d][gi % 3]
        (oacc, roacc) = acc_gl[d]
        for n in corder[d]:
            def chunk(n=n):
                c = scur["gl"][d]
                (Sc, rSc), (Sn, rSn) = Sgl[d][c], Sgl[d][1 - c]
                scur["gl"][d] = 1 - c
                P.op("pe", lambda e: e.matmul(oacc[:, cs(n)], lhsT=Sc[:], rhs=qi[:, cs(n)], start=True, stop=False), reads=[rSc, rqi], writes=[roacc])
                P.op("pe", lambda e: e.matmul(oacc[:, cs(n)], lhsT=gv_[:, n, :], rhs=att[:, n, :], start=False, stop=True), reads=[rgv_, ratt], writes=[roacc])
                P.op("dve", lambda e: e.scalar_tensor_tensor(out=Sn[:], in0=Sc[:], scalar=al[:, n:n + 1], in1=KV[:, n, :], op0=ALU.mult, op1=ALU.add),
                     reads=[rSc, ral, rKV], writes=[rSn])
            chunk()
            yield
        (o_, ro_) = ost[1][d]
        P.op("dve", lambda e: e.tensor_copy(out=o_[:], in_=oacc[:, 0:GW]), reads=[roacc], writes=[ro_])
        P.dma("sp", odst(1, d, gmem), o_[:], reads=[ro_])
        yield

    def rr(gens):
        gens = list(gens)
        while gens:
            nxt = []
            for g in gens:
                try:
                    next(g)
                    nxt.append(g)
                except StopIteration:
                    pass
            gens = nxt

    loads(0)
    for s_ in range(NG + 1):
        if s_ + 1 < NG:
            loads(s_ + 1)
        gens = []
        if s_ >= 1:
            gens += [dn_seq(0, s_ - 1), dn_seq(1, s_ - 1), gl_seq(0, s_ - 1), gl_seq(1, s_ - 1)]
        if s_ < NG:
            gens += [dn_prep(0, s_), dn_prep(1, s_), gl_prep(0, s_), gl_prep(1, s_)]
        rr(gens)
    K.end()


IN_SHAPES = {
    "xT": [D, SEQ], "ctxT": [D, CTX], "condT": [D, 2], "mod_w": [4, D, 6 * D], "mod_bT": [128, 4, 48],
    "rec_w_in": [2, D, REC_IN], "rec_cw": [2, 128, 3, 12], "dt_bias": [2, 1, 8], "a_log": [2, 1, 8], "nrm": [2, 128, 8],
    "w2": [2, 16, 2, 256], "b2": [2, 1, 512], "rec_w_out": [2, D, D], "att_w_qkv": [2, D, 1536], "gains": [2, 128, 2],
    "att_w_out": [2, D, D], "ffn_w_up": [4, D, 2 * DFF], "ffn_cw": [4, 128, 3, NJ], "ffn_w_down": [4, DFF, D], "fnormT": [128, 8],
    "cosT": [4, 128, LQ], "sinT": [4, 128, LQ], "perm": [128, 128], "ident": [128, 128], "cm": [64, 2, 6, 64],
}


class LazyIn(dict):
    def __init__(self, K):
        super().__init__()
        self.K = K

    def __missing__(self, name):
        if name == "yT":
            ap = self.K.dout("yT", [D, SEQ])
        else:
            ap = self.K.din(name, IN_SHAPES[name])
        self[name] = ap
        return ap


def build_fused(nl=4, ext=None, only=None):
    K = KB(ext)
    P = K.P
    G = GS()
    G.I = LazyIn(K)
    (G.ones_b, G.r_ob), (G.ones_f, G.r_of) = consts(K)
    G.mods, G.rmod = K.gsb([128, 4, 2, 6, 8], F32, "mods")
    G.mod1, G.rmod1 = K.gsb([128, 4, 2, 6, 8], F32, "mod1")
    S = {}
    S["X"] = [{0: K.dram("xl%d" % i, [D, SEQ + 2 * PADC]), 1: K.dram("xc%d" % i, [D, CTX + 2 * PADC])} for i in range(2)]
    S["MIX"] = [{0: K.dram("ml%d" % i, [D, SEQ + 2 * PADC]), 1: K.dram("mc%d" % i, [D, CTX + 2 * PADC])} for i in range(3)]
    S["Q"] = {0: K.dram("ql", [D, SEQ], BF16), 1: K.dram("qc", [D, CTX], BF16)}
    S["KT"] = K.dram("kt", [256, NKEY], BF16)
    S["VT"] = K.dram("vt", [NKEY, 256], BF16)
    S["FQK"] = K.dram("fqk", [1536, NKEY])
    S["DKV"] = K.dram("dkv", [NKEY, 1024])
    S["GKV"] = K.dram("gkv", [NKEY, 768])
    S["LA"] = K.dram("la", [NKEY, 512])
    S["GB"] = K.dram("gb", [NKEY, 16])
    G.S = S
    G.dumps = {}
    if only and "dbg" in only:
        def dump(name, tile_ap, res):
            if name in G.dumps:
                return
            shp = list(tile_ap.shape)
            d_ = K.dout("dbg_" + name, shp)
            G.dumps[name] = d_
            P.dma("sp", d_, tile_ap, reads=[res])
        G.dump = dump
    else:
        G.dump = lambda *a: None
    K.begin()
    z, rz = K.sb([128, 8, PADC], F32, "z")
    P.op("pool", lambda e: e.memset(z[:], 0.0), writes=[rz])
    for arr in S["X"] + S["MIX"]:
        for sid in (0, 1):
            n = arr[sid].shape[1]
            for col in (0, n - PADC):
                P.dma("sp", arr[sid][:, col:col + PADC].rearrange("(k p) n -> p k n", p=128), z[:], reads=[rz])
    if not (only and "noinit" in only):
        for k in range(8):
            P.dma("sp" if k % 2 else "act", S["X"][0][0][k * 128:(k + 1) * 128, PADC:SEQ + PADC], G.I["xT"][k * 128:(k + 1) * 128, :])
        P.dma("sp", S["X"][0][1][:, PADC:CTX + PADC], G.I["ctxT"])
    K.end()
    if not (only and "nomod" in only):
        emit_mod(K, G)
    for l in range(nl):
        last = l == 3
        if l % 2 == 0:
            if not only or "ka" in only:
                for q in range(4):
                    emit_ka_rec(K, G, l, q)
            if not only or "kb" in only:
                for hh in range(1 if (only and "hh0" in only) else 4):
                    emit_kb_rec(K, G, hh, (only or {}).get("ng", NG) if isinstance(only, dict) else NG)
            if only and "kb2" in only:
                for hh in range(4):
                    emit_kb_rec(K, G, hh)
            if not only or "kc" in only:
                for q in range(1 if (only and "q0" in only) else 4):
                    emit_kc(K, G, l, q, True, True, False)
        else:
            for q in range(4):
                emit_ka_att(K, G, l, q)
            for q in range(4):
                emit_kb_att(K, G, q, not last)
            for q in range(4):
                emit_kc(K, G, l, q, False, not last, last)
    nc = K.done()
    return nc, list(G.I.keys())


def rope_tables():
    t = np.arange(SEQ)
    inv = (10000.0 ** (-np.arange(32, dtype=np.float32) / 32)).astype(np.float32)
    row = ((t // 64).astype(np.float32)[:, None] * inv).astype(np.float32)
    col = ((t % 64).astype(np.float32)[:, None] * inv).astype(np.float32)
    cos = np.concatenate([np.cos(row), np.cos(row), np.cos(col), np.cos(col)], 1).astype(np.float32)
    sin = np.concatenate([-np.sin(row), np.sin(row), -np.sin(col), np.sin(col)], 1).astype(np.float32)
    perm = np.zeros((128, 128), np.float32)
    for m in range(128):
        partner = m + 32 if (m // 32) % 2 == 0 else m - 32
        perm[partner, m] = 1.0
    cosq = np.ascontiguousarray(np.stack([cos[q * LQ:(q + 1) * LQ].T for q in range(4)]))
    sinq = np.ascontiguousarray(np.stack([sin[q * LQ:(q + 1) * LQ].T for q in range(4)]))
    return cosq, sinq, perm


def rec_consts():
    i = np.arange(64)
    U = (i[:, None] <= i[None, :]).astype(np.float32)
    Us = (i[:, None] < i[None, :]).astype(np.float32)
    L = U.T.copy()
    Ls = Us.T.copy()
    I_ = np.eye(64, dtype=np.float32)
    d0 = [U, Us, -Ls, (U - 1.0) * 1.0e4, (Ls - 1.0) * 1.0e4, I_]
    d1 = [L, Ls, -Us, (L - 1.0) * 1.0e4, (Us - 1.0) * 1.0e4, I_]
    return np.ascontiguousarray(np.stack([np.stack(d0, 1), np.stack(d1, 1)], 1).astype(np.float32))


def host_inputs(x, c, ctx, c_ctx, mod_w, mod_b, rec_w_in, rec_conv, dn_a_log, dn_dt_bias, dn_norm, gla_w2, gla_b2, gla_norm, rec_w_out,
                att_w_qkv, att_q_norm, att_k_norm, att_w_out, ffn_w_up, ffn_conv, ffn_w_down, final_norm):
    cosq, sinq, perm = rope_tables()
    shared = {
        "mod_w": mod_w, "mod_bT": np.ascontiguousarray(mod_b.reshape(4, 48, 128).transpose(2, 0, 1)),
        "rec_w_in": rec_w_in, "rec_cw": np.ascontiguousarray(rec_conv.reshape(2, 3, 12, 128).transpose(0, 3, 1, 2)),
        "dt_bias": np.ascontiguousarray(dn_dt_bias.reshape(2, 1, 8)), "a_log": np.ascontiguousarray(dn_a_log.reshape(2, 1, 8)),
        "nrm": np.ascontiguousarray(np.stack([np.stack([dn_norm[e]] * 4 + [gla_norm[e]] * 4, 1) for e in range(2)])),
        "w2": np.ascontiguousarray(gla_w2.transpose(0, 2, 1, 3)), "b2": np.ascontiguousarray(gla_b2.reshape(2, 1, 512)),
        "rec_w_out": rec_w_out, "att_w_qkv": att_w_qkv,
        "gains": np.ascontiguousarray(np.stack([att_q_norm, att_k_norm], 2)),
        "att_w_out": att_w_out, "ffn_w_up": ffn_w_up,
        "ffn_cw": np.ascontiguousarray(ffn_conv.reshape(4, 3, NJ, 128).transpose(0, 3, 1, 2)),
        "ffn_w_down": ffn_w_down, "fnormT": np.ascontiguousarray(final_norm.reshape(8, 128).T),
        "cosT": cosq, "sinT": sinq, "perm": perm, "ident": np.eye(128, dtype=np.float32), "cm": rec_consts(),
    }
    per_core = []
    for core in range(NCORES):
        b = core % 2
        m = dict(shared)
        m["xT"] = np.ascontiguousarray(x[b].T)
        m["ctxT"] = np.ascontiguousarray(ctx[b].T)
        m["condT"] = np.ascontiguousarray(np.stack([c[b], c_ctx], 1))
        per_core.append(m)
    return per_core


_CACHE = {}


def kernel(x, c, ctx, c_ctx, mod_w, mod_b, rec_w_in, rec_conv, dn_a_log, dn_dt_bias, dn_norm, gla_w2, gla_b2, gla_norm, rec_w_out,
           att_w_qkv, att_q_norm, att_k_norm, att_w_out, ffn_w_up, ffn_conv, ffn_w_down, final_norm):
    f = lambda a: np.ascontiguousarray(np.asarray(a, dtype=np.float32))
    args = list(map(f, (x, c, ctx, c_ctx, mod_w, mod_b, rec_w_in, rec_conv, dn_a_log, dn_dt_bias, dn_norm, gla_w2, gla_b2, gla_norm, rec_w_out,
                        att_w_qkv, att_q_norm, att_k_norm, att_w_out, ffn_w_up, ffn_conv, ffn_w_down, final_norm)))
    if "nc" not in _CACHE:
        _CACHE["nc"] = build_fused(4)
    nc, names = _CACHE["nc"]
    full = host_inputs(*args)
    in_maps = [{k: m[k] for k in names if k != "yT"} for m in full]
    res = run_bass_kernel_spmd(nc, in_maps, core_ids=list(range(NCORES))).results
    return np.ascontiguousarray(np.stack([res[0]["yT"].T, res[1]["yT"].T]))
```
